# Optimizing a Trainium2 kernel written in Bass

```python
import math
import jax, jax.numpy as jnp
from jax import lax
import numpy as np

D_MODEL = 1024
BATCH = 4
SEQ = 4096
DEPTH = 4

N_A_LAYERS = max(1, DEPTH // 2)
N_B_LAYERS = DEPTH - N_A_LAYERS

DIFF_HEADS = 8
DIFF_QK_DIM = 64
DIFF_V_DIM = 2 * DIFF_QK_DIM
T5_BUCKETS = 32
T5_MAX_DISTANCE = 128
MLA_HEADS = 16
MLA_QK_NOPE = 128
MLA_QK_ROPE = 64
MLA_V_DIM = 128
MLA_Q_LORA = 512
MLA_KV_LORA = 256
ROPE_BASE = 10000.0
D_FF = 2816
CONV_WIDTH = 3
PLE_DIM = 256
Q_BLOCK = 128
RMS_EPS = 1e-6
NEG_INF = -1e30
POS_OFFSET_MAX = 1024

kernel_name = 'yoco_diffattn_mla_convffn_hybrid'


def rms_norm(x, gain):
    xf = x.astype(jnp.float32)
    y = xf * lax.rsqrt(jnp.mean(xf * xf, axis=-1, keepdims=True) + RMS_EPS)
    return (y * gain.astype(jnp.float32)).astype(x.dtype)


def rotary(x, pos):
    d = x.shape[-1]
    half = d // 2
    inv_freq = jnp.exp(-math.log(ROPE_BASE) * jnp.arange(half, dtype=jnp.float32) * (2.0 / d))
    ang = pos.astype(jnp.float32)[..., None] * inv_freq
    if x.ndim == 4:
        ang = ang[:, :, None, :]
    cos, sin = jnp.cos(ang), jnp.sin(ang)
    xf = x.astype(jnp.float32)
    x1, x2 = xf[..., :half], xf[..., half:]
    return jnp.concatenate([x1 * cos - x2 * sin, x1 * sin + x2 * cos], axis=-1).astype(x.dtype)


def t5_bucket(dist):
    n = jnp.maximum(dist, 0)
    max_exact = T5_BUCKETS // 2
    log_ratio = jnp.log(jnp.maximum(n, 1).astype(jnp.float32) / max_exact) / math.log(T5_MAX_DISTANCE / max_exact)
    large = max_exact + (log_ratio * (T5_BUCKETS - max_exact)).astype(jnp.int32)
    large = jnp.minimum(large, T5_BUCKETS - 1)
    return jnp.where(n < max_exact, n, large)


def to_blocks(t):
    b, s = t.shape[:2]
    return t.reshape((b, s // Q_BLOCK, Q_BLOCK) + t.shape[2:]).swapaxes(0, 1)


def from_blocks(t):
    nb, b, q = t.shape[:3]
    return t.swapaxes(0, 1).reshape((b, nb * q) + t.shape[3:])


def causal_mask(blk_idx, seq):
    q_idx = blk_idx * Q_BLOCK + jnp.arange(Q_BLOCK)
    return jnp.arange(seq)[None, :] <= q_idx[:, None]


def diff_attention(h, pos, w_qkv, q_gain, k_gain, lam_q1, lam_k1, lam_q2, lam_k2, sub_gain, w_o, rel_table, lam_init):
    b, s, _ = h.shape
    qk_w = DIFF_HEADS * 2 * DIFF_QK_DIM
    qkv = h @ w_qkv
    q = rms_norm(qkv[..., :qk_w].reshape(b, s, DIFF_HEADS, 2, DIFF_QK_DIM), q_gain)
    k = rms_norm(qkv[..., qk_w:2 * qk_w].reshape(b, s, DIFF_HEADS, 2, DIFF_QK_DIM), k_gain)
    v = qkv[..., 2 * qk_w:].reshape(b, s, DIFF_HEADS, DIFF_V_DIM)
    f32 = jnp.float32
    lam = (jnp.exp(jnp.sum(lam_q1.astype(f32) * lam_k1.astype(f32)))
           - jnp.exp(jnp.sum(lam_q2.astype(f32) * lam_k2.astype(f32))) + lam_init)
    scale = DIFF_QK_DIM ** -0.5
    nb = s // Q_BLOCK

    def block(args):
        i, q_blk, p_blk = args
        sc = jnp.einsum('bqhcd,bkhcd->bhcqk', q_blk, k).astype(f32) * scale
        bucket = t5_bucket(p_blk[:, :, None] - pos[:, None, :])
        bias = jnp.take(rel_table, bucket, axis=0).astype(f32)
        bias = bias.transpose(0, 3, 1, 2)[:, :, None]
        sc = jnp.where(causal_mask(i, s), sc + bias, NEG_INF)
        a = jax.nn.softmax(sc, axis=-1)
        wts = a[:, :, 0] - lam * a[:, :, 1]
        return jnp.einsum('bhqk,bkhv->bqhv', wts.astype(v.dtype), v)

    o = lax.map(block, (jnp.arange(nb), to_blocks(q), to_blocks(pos)))
    o = from_blocks(o)
    o = rms_norm(o, sub_gain) * (1.0 - lam_init)
    return o.reshape(b, s, DIFF_HEADS * DIFF_V_DIM) @ w_o


def shared_latent_kv(h, pos, kv_norm, w_dkv, ckv_norm, w_ukv, k_nope_norm, k_pe_norm):
    b, s, _ = h.shape
    hn = rms_norm(h, kv_norm)
    ckv_full = hn @ w_dkv
    c_kv = rms_norm(ckv_full[..., :MLA_KV_LORA], ckv_norm)
    k_pe = rotary(rms_norm(ckv_full[..., MLA_KV_LORA:], k_pe_norm), pos)
    kv = (c_kv @ w_ukv).reshape(b, s, MLA_HEADS, MLA_QK_NOPE + MLA_V_DIM)
    k_nope = rms_norm(kv[..., :MLA_QK_NOPE], k_nope_norm)
    v = kv[..., MLA_QK_NOPE:]
    return k_nope, k_pe, v


def mla_attention(h, pos, w_dq, cq_norm, w_uq, q_nope_norm, q_pe_norm, w_o, k_nope, k_pe, v):
    b, s, _ = h.shape
    c_q = rms_norm(h @ w_dq, cq_norm)
    q = (c_q @ w_uq).reshape(b, s, MLA_HEADS, MLA_QK_NOPE + MLA_QK_ROPE)
    q_nope = rms_norm(q[..., :MLA_QK_NOPE], q_nope_norm)
    q_pe = rotary(rms_norm(q[..., MLA_QK_NOPE:], q_pe_norm), pos)
    scale = (MLA_QK_NOPE + MLA_QK_ROPE) ** -0.5
    nb = s // Q_BLOCK

    def block(args):
        i, qn, qp = args
        sc = (jnp.einsum('bqhd,bkhd->bhqk', qn, k_nope)
              + jnp.einsum('bqhr,bkr->bhqk', qp, k_pe)).astype(jnp.float32) * scale
        sc = jnp.where(causal_mask(i, s), sc, NEG_INF)
        a = jax.nn.softmax(sc, axis=-1)
        return jnp.einsum('bhqk,bkhv->bqhv', a.astype(v.dtype), v)

    o = from_blocks(lax.map(block, (jnp.arange(nb), to_blocks(q_nope), to_blocks(q_pe))))
    return o.reshape(b, s, MLA_HEADS * MLA_V_DIM) @ w_o


def conv_gated_ffn(h, w_in, conv_w, conv_b, w_out):
    s = h.shape[1]
    u = h @ w_in
    u_pad = jnp.pad(u, ((0, 0), (CONV_WIDTH - 1, 0), (0, 0)))
    c = conv_b
    for j in range(CONV_WIDTH):
        c = c + u_pad[:, j:j + s, :] * conv_w[j]
    a, g = jnp.split(c, 2, axis=-1)
    return (jax.nn.silu(g) * a) @ w_out


def per_layer_embedding(h, p_i, norm_g, w_proj, w_gate):
    gate = jax.nn.sigmoid(rms_norm(h, norm_g) @ w_gate)
    return (p_i @ w_proj) * gate


def setup_inputs(seed: int = 0) -> dict:
    key = jax.random.key(seed)
    keys = iter(jax.random.split(key, 48))
    f32 = jnp.float32

    def dense(shape, fan_in):
        return jax.random.normal(next(keys), shape, f32) * fan_in ** -0.5

    def gain(shape):
        return 1.0 + 0.05 * jax.random.normal(next(keys), shape, f32)

    def small(shape, scale):
        return scale * jax.random.normal(next(keys), shape, f32)

    na, nbl = N_A_LAYERS, N_B_LAYERS
    qkv_w = 2 * DIFF_HEADS * 2 * DIFF_QK_DIM + DIFF_HEADS * DIFF_V_DIM
    x = jax.random.normal(next(keys), (BATCH, SEQ, D_MODEL), f32)
    p = jax.random.normal(next(keys), (DEPTH, BATCH, SEQ, PLE_DIM), f32)
    offsets = jax.random.randint(next(keys), (BATCH, 1), 0, POS_OFFSET_MAX, dtype=jnp.int32)
    positions = offsets + jnp.arange(SEQ, dtype=jnp.int32)[None, :]
    return {
        'x': x,
        'p': p,
        'positions': positions,
        'rel_bias_table': small((T5_BUCKETS, DIFF_HEADS), 0.5),
        'attn_norm': gain((DEPTH, D_MODEL)),
        'a_w_qkv': dense((na, D_MODEL, qkv_w), D_MODEL),
        'a_q_norm': gain((na, DIFF_QK_DIM)),
        'a_k_norm': gain((na, DIFF_QK_DIM)),
        'a_lam_q1': small((na, DIFF_QK_DIM), 0.1),
        'a_lam_k1': small((na, DIFF_QK_DIM), 0.1),
        'a_lam_q2': small((na, DIFF_QK_DIM), 0.1),
        'a_lam_k2': small((na, DIFF_QK_DIM), 0.1),
        'a_sub_norm': gain((na, DIFF_V_DIM)),
        'a_w_o': dense((na, DIFF_HEADS * DIFF_V_DIM, D_MODEL), DIFF_HEADS * DIFF_V_DIM),
        'kv_norm': gain((D_MODEL,)),
        'w_dkv': dense((D_MODEL, MLA_KV_LORA + MLA_QK_ROPE), D_MODEL),
        'ckv_norm': gain((MLA_KV_LORA,)),
        'w_ukv': dense((MLA_KV_LORA, MLA_HEADS * (MLA_QK_NOPE + MLA_V_DIM)), MLA_KV_LORA),
        'k_nope_norm': gain((MLA_QK_NOPE,)),
        'k_pe_norm': gain((MLA_QK_ROPE,)),
        'b_w_dq': dense((nbl, D_MODEL, MLA_Q_LORA), D_MODEL),
        'b_cq_norm': gain((nbl, MLA_Q_LORA)),
        'b_w_uq': dense((nbl, MLA_Q_LORA, MLA_HEADS * (MLA_QK_NOPE + MLA_QK_ROPE)), MLA_Q_LORA),
        'b_q_nope_norm': gain((nbl, MLA_QK_NOPE)),
        'b_q_pe_norm': gain((nbl, MLA_QK_ROPE)),
        'b_w_o': dense((nbl, MLA_HEADS * MLA_V_DIM, D_MODEL), MLA_HEADS * MLA_V_DIM),
        'ffn_norm': gain((DEPTH, D_MODEL)),
        'ffn_w_in': dense((DEPTH, D_MODEL, 2 * D_FF), D_MODEL),
        'ffn_conv_w': dense((DEPTH, CONV_WIDTH, 2 * D_FF), CONV_WIDTH),
        'ffn_conv_b': small((DEPTH, 2 * D_FF), 0.02),
        'ffn_w_out': dense((DEPTH, D_FF, D_MODEL), D_FF),
        'ple_norm': gain((DEPTH, D_MODEL)),
        'ple_w_proj': dense((DEPTH, PLE_DIM, D_MODEL), PLE_DIM),
        'ple_w_gate': dense((DEPTH, D_MODEL, D_MODEL), D_MODEL),
    }


def reference(x, p, positions, rel_bias_table, attn_norm,
              a_w_qkv, a_q_norm, a_k_norm, a_lam_q1, a_lam_k1, a_lam_q2, a_lam_k2, a_sub_norm, a_w_o,
              kv_norm, w_dkv, ckv_norm, w_ukv, k_nope_norm, k_pe_norm,
              b_w_dq, b_cq_norm, b_w_uq, b_q_nope_norm, b_q_pe_norm, b_w_o,
              ffn_norm, ffn_w_in, ffn_conv_w, ffn_conv_b, ffn_w_out,
              ple_norm, ple_w_proj, ple_w_gate):
    h = x
    shared = None
    for i in range(DEPTH):
        hn = rms_norm(h, attn_norm[i])
        if i < N_A_LAYERS:
            lam_init = 0.8 - 0.6 * math.exp(-0.3 * i)
            mix = diff_attention(hn, positions, a_w_qkv[i], a_q_norm[i], a_k_norm[i],
                                 a_lam_q1[i], a_lam_k1[i], a_lam_q2[i], a_lam_k2[i],
                                 a_sub_norm[i], a_w_o[i], rel_bias_table, lam_init)
        else:
            if shared is None:
                shared = shared_latent_kv(h, positions, kv_norm, w_dkv, ckv_norm, w_ukv,
                                          k_nope_norm, k_pe_norm)
            j = i - N_A_LAYERS
            k_nope, k_pe, v = shared
            mix = mla_attention(hn, positions, b_w_dq[j], b_cq_norm[j], b_w_uq[j],
                                b_q_nope_norm[j], b_q_pe_norm[j], b_w_o[j], k_nope, k_pe, v)
        h = h + mix
        h = h + conv_gated_ffn(rms_norm(h, ffn_norm[i]), ffn_w_in[i], ffn_conv_w[i],
                               ffn_conv_b[i], ffn_w_out[i])
        h = h + per_layer_embedding(h, p[i], ple_norm[i], ple_w_proj[i], ple_w_gate[i])
    return h
```

```python
import math
from contextlib import ExitStack

import numpy as np
import concourse.bass as bass
import concourse.mybir as mybir
from concourse.bass_utils import run_bass_kernel_spmd

F32 = mybir.dt.float32
BF16 = mybir.dt.bfloat16
I32 = mybir.dt.int32
AF = mybir.ActivationFunctionType
ALU = mybir.AluOpType

D = 1024
DFF = 2816
NFC = 22
TW = 512
EPS = 1e-6
NCORES = 8


class Buf:
    __slots__ = ("name", "t", "writers", "readers", "dkey")

    def __init__(self, name, t=None):
        self.name = name
        self.t = t
        self.writers = {}
        self.readers = {}
        self.dkey = None

    def __getitem__(self, idx):
        return self.t[idx]


class Op:
    __slots__ = ("q", "fn", "deps", "key", "signaled", "count", "ninc")


class Sched:
    def __init__(self, nc, es):
        self.nc = nc
        self.es = es
        self.ops = []
        self.eng = {"pe": nc.tensor, "act": nc.scalar, "dve": nc.vector,
                    "pool": nc.gpsimd, "sp": nc.sync}
        self.last_dma = {}
        self.ndkeys = 0

    def sbuf(self, name, shape, dt):
        return Buf(name, self.es.enter_context(self.nc.sbuf_tensor("sb_" + name, list(shape), dt)))

    def psum(self, name, shape, dt):
        return Buf(name, self.es.enter_context(self.nc.psum_tensor("ps_" + name, list(shape), dt)))

    def op(self, q, fn, reads=(), writes=(), dma=None, ninc=1):
        o = Op()
        o.q = q
        o.fn = fn
        o.signaled = False
        o.count = 0
        o.ninc = ninc
        isdma = dma is not None
        if isdma:
            if dma.dkey is None:
                dma.dkey = self.ndkeys
                self.ndkeys += 1
            o.key = ("dma", dma.dkey)
        else:
            o.key = q
        key = o.key
        deps = {}
        for b in reads:
            for w in b.writers.values():
                deps[id(w)] = w
        for b in writes:
            for k, r in b.readers.items():
                if k != key or isdma:
                    deps[id(r)] = r
            for k, w in b.writers.items():
                if k != key or isdma:
                    deps[id(w)] = w
        if isdma:
            prev = self.last_dma.get(key)
            if prev is not None:
                deps[id(prev)] = prev
            self.last_dma[key] = o
            o.signaled = True
        o.deps = list(deps.values())
        for d in o.deps:
            d.signaled = True
        for b in reads:
            b.readers[key] = o
        for b in writes:
            b.readers = {}
            b.writers = {key: o}
        self.ops.append(o)
        return o

    def emit(self):
        nc = self.nc
        sems = {}
        counts = {}
        waited = {}
        for o in self.ops:
            if o.signaled:
                if o.key not in sems:
                    nm = "s_" + (o.key if isinstance(o.key, str) else "d%d" % o.key[1])
                    sems[o.key] = self.es.enter_context(nc.semaphore(nm))
                    counts[o.key] = 0
                counts[o.key] += (16 * o.ninc) if not isinstance(o.key, str) else 1
                o.count = counts[o.key]
        nwaits = 0
        for o in self.ops:
            e = self.eng[o.q]
            wq = waited.setdefault(o.q, {})
            for d in o.deps:
                if wq.get(d.key, 0) < d.count:
                    e.wait_ge(sems[d.key], d.count)
                    wq[d.key] = d.count
                    nwaits += 1
            ins = o.fn(e)
            if o.signaled:
                if isinstance(o.key, str):
                    last = ins[-1] if isinstance(ins, (list, tuple)) else ins
                    last.then_inc(sems[o.key], 1)
                else:
                    lst = list(ins) if isinstance(ins, (list, tuple)) else [ins]
                    assert len(lst) == o.ninc, (len(lst), o.ninc)
                    for i in lst:
                        i.then_inc(sems[o.key], 16)
        e = self.eng["sp"]
        wq = waited.setdefault("sp", {})
        for key, s in sems.items():
            if counts[key] > wq.get(key, 0):
                e.wait_ge(s, counts[key])
        return dict(nops=len(self.ops), nwaits=nwaits, nsems=len(sems),
                    maxcount=max(counts.values()) if counts else 0)


class Ring:
    def __init__(self, bufs):
        self.bufs = bufs
        self.i = 0

    def next(self):
        b = self.bufs[self.i % len(self.bufs)]
        self.i += 1
        return b


def t5_thresholds():
    n = np.arange(0, 400)
    f = np.maximum(n, 1).astype(np.float32) / np.float32(16)
    lr = np.log(f).astype(np.float32) / np.float32(math.log(128 / 16))
    large = 16 + (lr * np.float32(16)).astype(np.int32)
    large = np.minimum(large, 31)
    bucket = np.where(n < 16, n, large)
    lo = [int(np.min(n[bucket >= b])) for b in range(32)]
    return lo


class ColMap:
    def __init__(self):
        self.n = 0
        self.m = {}

    def add(self, name, w):
        self.m[name] = self.n
        self.n += w
        return self.m[name]


def make_colmap():
    cm = ColMap()
    for l in range(4):
        cm.add("attn_norm%d" % l, 8)
        cm.add("ffn_norm%d" % l, 8)
        cm.add("ple_norm%d" % l, 8)
        for j in range(3):
            cm.add("conv_w%d_%d" % (l, j), 44)
        cm.add("conv_b%d" % l, 44)
    cm.add("kv_norm", 8)
    for l in range(2):
        cm.add("a_q_norm%d" % l, 1)
        cm.add("a_k_norm%d" % l, 1)
        cm.add("b_cq_norm%d" % l, 4)
        cm.add("b_q_nope_norm%d" % l, 1)
        cm.add("b_q_pe_norm%d" % l, 1)
    cm.add("ckv_norm", 2)
    cm.add("k_nope_norm", 1)
    cm.add("k_pe_norm", 1)
    cm.add("b31", 8)
    cm.add("table", 256)
    cm.add("invfreq", 1)
    return cm


def make_rowmap():
    rm = ColMap()
    for l in range(2):
        for nm in ("q1", "k1", "q2", "k2"):
            rm.add("lam_%s%d" % (nm, l), 64)
        rm.add("sub_gain%d" % l, 128)
    return rm


def build_program(SL, L, wt_plan=None):
    NT = SL // TW
    NB = SL // 128
    nc = bass.Bass("TRN2", target_bir_lowering=False)
    cm = make_colmap()
    rm = make_rowmap()
    lo_thr = t5_thresholds()

    def din(name, shape, dt=F32):
        return nc.dram_tensor(name, list(shape), dt, kind="ExternalInput").ap()

    def dscr(name, shape, dt):
        return nc.dram_tensor(name, list(shape), dt, kind="Internal").ap()

    xT = din("xT", [D, SL])
    pT = din("pT", [4, 256, SL])
    posb = din("posb", [128, SL], I32)
    poscol = din("poscol", [128, 2], I32)
    colsd = din("cols", [128, cm.n])
    rowsd = din("rows", [128, rm.n])
    protd = din("prot", [128, 128])
    a_w_qkv = din("a_w_qkv", [2, D, 3072])
    a_w_o = din("a_w_o", [2, D, D])
    w_dkv = din("w_dkv", [D, 320])
    w_ukvK = din("w_ukvK", [256, 2048])
    w_ukvV = din("w_ukvV", [256, 2048])
    b_w_dq = din("b_w_dq", [2, D, 512])
    b_w_uqN = din("b_w_uqN", [2, 512, 2048])
    b_w_uqP = din("b_w_uqP", [2, 512, 1024])
    b_w_o = din("b_w_o", [2, 2048, D])
    ffn_w_in = din("ffn_w_in", [4, D, 2 * DFF])
    ffn_w_out = din("ffn_w_out", [4, DFF, D])
    ple_w_proj = din("ple_w_proj", [4, 256, D])
    ple_w_gate = din("ple_w_gate", [4, D, D])
    outT = nc.dram_tensor("outT", [D, SL], F32, kind="ExternalOutput").ap()

    hT = dscr("hT", [D, SL], F32)
    QA = dscr("QA", [16, 128, SL], BF16)
    KA = dscr("KA", [16, 128, SL], BF16)
    QP = dscr("QP", [8, 128, SL], BF16)
    KPE = dscr("KPE", [64, SL], BF16)
    VV = dscr("VV", [16, 128, NB, 129], BF16)
    OT = dscr("OT", [16, 128, SL], BF16)
    NWT = max(1, sum(1 for v in wt_plan.values() if v[0] > 0)) if wt_plan is not None else 1
    WT = dscr("WT", [NWT, 128, 4096], BF16)
    COS = dscr("COS", [128, SL], F32)
    SIN = dscr("SIN", [128, SL], F32)

    es = ExitStack()
    S = Sched(nc, es)

    def regions(name, n1):
        return [[Buf("%s_%d_%d" % (name, i, t)) for t in range(NT)] for i in range(n1)]

    hT_r = [[Buf("hT_%d_%d" % (t, k)) for k in range(8)] for t in range(NT)]
    QA_r = regions("QA", 16)
    KA_r = regions("KA", 16)
    QP_r = regions("QP", 8)
    KPE_r = [Buf("KPE_%d" % t) for t in range(NT)]
    VV_r = regions("VV", 16)
    OT_r = regions("OT", 16)
    CS_r = [Buf("CS_%d" % t) for t in range(NT)]

    cols = S.sbuf("cols", [128, cm.n], F32)
    rows = S.sbuf("rows", [128, rm.n], F32)
    ones = S.sbuf("ones", [128, 128], BF16)
    bones = S.sbuf("bones", [128, 128], BF16)
    ident = S.sbuf("ident", [128, 128], F32)
    prot = S.sbuf("prot", [128, 128], BF16)
    tri = S.sbuf("tri", [128, 128], F32)
    epsc = S.sbuf("epsc", [128, 1], F32)
    Rt = S.sbuf("Rt", [128, 2, 8, 128], F32)
    lamt = S.sbuf("lamt", [128, 8], F32)
    subg = S.sbuf("subg", [128, 2, 128], F32)
    wslots = [S.sbuf("w%d" % i, [128, 4096], BF16) for i in range(4)]
    wring = Ring(wslots)
    h_sb = S.sbuf("h_sb", [128, 8, TW], F32)
    h_c = [Buf("h_c%d" % i, h_sb.t) for i in range(8)]
    hn = S.sbuf("hn", [128, 8, TW], BF16)
    sqr = Ring([S.sbuf("sq%d" % i, [128, TW], BF16) for i in range(3)])
    lnr = Ring([S.sbuf("ln%d" % i, [128, TW], F32) for i in range(2)])
    rsr = Ring([S.sbuf("rs%d" % i, [128, TW], F32) for i in range(2)])
    stg = Ring([S.sbuf("stg%d" % i, [128, TW], BF16) for i in range(4)])
    vst = Ring([S.sbuf("vst%d" % i, [128, 4, 4, 129], BF16) for i in range(2)])
    qring = Ring([S.sbuf("attq%d" % i, [128, TW], BF16) for i in range(2)])
    qpring = Ring([S.sbuf("attqp%d" % i, [128, TW], BF16) for i in range(2)])
    attk = S.sbuf("attk", [128, SL], BF16)
    attv = S.sbuf("attv", [128, NB, 129], BF16)
    attkp = S.sbuf("attkp", [128, SL], BF16)
    ppool = Ring([S.sbuf("pt%d" % i, [128, TW], BF16) for i in range(6)])
    otst = Ring([S.sbuf("otst%d" % i, [128, TW], BF16) for i in range(2)])
    o_sb = Ring([S.sbuf("o_sb%d" % i, [128, 128], F32) for i in range(2)])
    on_sb = Ring([S.sbuf("on_sb%d" % i, [128, 128], F32) for i in range(2)])
    smr = Ring([S.sbuf("sm%d" % i, [128, 8], F32) for i in range(4)])
    junk = S.sbuf("junk", [128, 128], F32)
    ot_sb = S.sbuf("ot_sb", [128, 8, TW], BF16)
    z_sb = S.sbuf("z_sb", [128, NFC, TW], BF16)
    uext = Ring([S.sbuf("uext%d" % i, [128, TW + 2], F32) for i in range(4)])
    cpool = Ring([S.sbuf("c%d" % i, [128, TW], F32) for i in range(4)])
    sgp = Ring([S.sbuf("sg%d" % i, [128, TW], F32) for i in range(2)])
    utail_t = es.enter_context(nc.sbuf_tensor("sb_utail", [128, 44, 2], F32))
    utail = [Buf("utail%d" % i, utail_t) for i in range(44)]
    pt_sb = S.sbuf("pt_sb", [128, 2, TW], BF16)
    cs_sb = S.sbuf("cs_sb", [128, 2, TW], F32)
    cqn = S.sbuf("cqn", [128, 4, TW], BF16)
    ckvn = S.sbuf("ckvn", [128, 2, TW], BF16)
    itmp = S.sbuf("itmp", [128, TW], I32)
    itmp2 = S.sbuf("itmp2", [128, TW], I32)
    pb = [S.psum("pb%d" % i, [128, TW], F32) for i in range(8)]

    def C(name, off=0, w=1):
        c0 = cm.m[name] + off
        return cols[:, c0:c0 + w]

    def dve_tt(out_b, out_ap, a_b, a_ap, b_b, b_ap, op, q="dve"):
        S.op(q, lambda e: e.tensor_tensor(out=out_ap, in0=a_ap, in1=b_ap, op=op),
             reads=[a_b, b_b], writes=[out_b])

    wt_first = {}
    wt_tid = {}
    wt_bufs = {}
    cur_phase = [0]

    def _cast_load(slot, srcs, kc, ns):
        n = sum(ns)
        view = slot[:, 0:kc * n].rearrange("p (k n) -> p k n", k=kc)
        svs = [a.rearrange("(k p) n -> p k n", p=128) for a in srcs]
        offs = [sum(ns[:i]) for i in range(len(ns))]
        S.op("pool", lambda e: [e.dma_start(out=view[:, :, offs[i]:offs[i] + ns[i]], in_=svs[i]) for i in range(len(ns))],
             writes=[slot], dma=slot, ninc=len(ns))
        return view

    def src_for_key(key):
        k0 = key[0]
        if k0 == "wo":
            _, l, half, mg = key
            w = a_w_o[l] if l < 2 else b_w_o[l - 2]
            return [w[half * 1024:(half + 1) * 1024, mg * 512:(mg + 1) * 512]], 8, [512]
        if k0 == "win":
            _, l, ig = key
            w = ffn_w_in[l]
            return [w[:, ig * 128:(ig + 2) * 128], w[:, DFF + ig * 128:DFF + (ig + 2) * 128]], 8, [256, 256]
        if k0 == "wout":
            _, l, m = key
            return [ffn_w_out[l][:, m * 128:(m + 1) * 128]], NFC, [128]
        if k0 == "pproj":
            return [ple_w_proj[key[1]]], 2, [1024]
        if k0 == "pgate":
            _, l, mg = key
            return [ple_w_gate[l][:, mg * 512:(mg + 1) * 512]], 8, [512]
        if k0 == "qkv":
            _, l, grp = key
            return [a_w_qkv[l][:, grp * 512:(grp + 1) * 512]], 8, [512]
        if k0 == "qkvv":
            _, l, half = key
            return [a_w_qkv[l][:, 2048 + half * 512:2048 + (half + 1) * 512]], 8, [512]
        if k0 == "dkv":
            return [w_dkv], 8, [320]
        if k0 == "ukvK":
            return [w_ukvK[:, key[1] * 512:(key[1] + 1) * 512]], 2, [512]
        if k0 == "ukvV":
            return [w_ukvV[:, key[1] * 512:(key[1] + 1) * 512]], 2, [512]
        if k0 == "dq":
            return [b_w_dq[key[1]]], 8, [512]
        if k0 == "uqN":
            return [b_w_uqN[key[1]][:, key[2] * 512:(key[2] + 1) * 512]], 4, [512]
        if k0 == "uqP":
            return [b_w_uqP[key[1]][:, key[2] * 512:(key[2] + 1) * 512]], 4, [512]
        raise KeyError(key)

    if wt_plan is not None:
        for key_, (ph_, _) in wt_plan.items():
            if ph_ > 0:
                wt_tid[key_] = len(wt_tid)
                wt_bufs[key_] = Buf("wt_%d" % wt_tid[key_])

    def convert_keys(keys):
        for key in keys:
            srcs, kc, ns = src_for_key(key)
            tid = wt_tid[key]
            slot = wring.next()
            _cast_load(slot, srcs, kc, ns)
            n = kc * sum(ns)
            S.op("sp", lambda e, slot=slot, tid=tid, n=n: e.dma_start(out=WT[tid, :, 0:n], in_=slot[:, 0:n]),
                 reads=[slot], writes=[wt_bufs[key]], dma=slot)

    srcs_of = {}

    def _load(key, srcs, kc, ns):
        n = sum(ns)
        if key not in wt_first:
            wt_first[key] = (cur_phase[0], (None, kc, ns))
        srcs_of.setdefault(key, srcs)
        slot = wring.next()
        if wt_plan is None or key not in wt_tid:
            view = _cast_load(slot, srcs, kc, ns)
        else:
            tid = wt_tid[key]
            view = slot[:, 0:kc * n].rearrange("p (k n) -> p k n", k=kc)
            S.op("pool", lambda e: e.dma_start(out=slot[:, 0:kc * n], in_=WT[tid, :, 0:kc * n]),
                 reads=[wt_bufs[key]], writes=[slot], dma=slot)
        return slot, view

    def load_w(key, src_ap, kc, n):
        return _load(key, [src_ap], kc, [n])

    def load_w2(key, src_a, src_b, kc, na, nb_):
        return _load(key, [src_a, src_b], kc, [na, nb_])

    mmr = Ring(pb[0:5])
    str_ = Ring(pb[5:7])
    STAT = pb[7]

    def rstd_from(stat_b, stat_ap, inv_n, width=TW, parts=128):
        ln = lnr.next()
        rs = rsr.next()
        S.op("act", lambda e: e.activation(out=ln[0:parts, 0:width], in_=stat_ap, func=AF.Ln,
                                           bias=epsc[0:parts, :], scale=inv_n),
             reads=[stat_b, epsc], writes=[ln])
        S.op("act", lambda e: e.activation(out=rs[0:parts, 0:width], in_=ln[0:parts, 0:width],
                                           func=AF.Exp, scale=-0.5),
             reads=[ln], writes=[rs])
        return rs

    def rmsnorm(gain_name):
        st = STAT
        for kc in range(8):
            sq = sqr.next()
            S.op("act", lambda e, kc=kc, sq=sq: e.activation(out=sq[:, :], in_=h_sb[:, kc, :], func=AF.Square),
                 reads=[h_c[kc]], writes=[sq])
            S.op("pe", lambda e, kc=kc, sq=sq: e.matmul(st[:, :], lhsT=ones[:, :], rhs=sq[:, :],
                                                        start=(kc == 0), stop=(kc == 7)),
                 reads=[ones, sq], writes=[st])
        rs = rstd_from(st, st[:, :], 1.0 / D)
        for kc in range(8):
            S.op("dve", lambda e, kc=kc: e.scalar_tensor_tensor(
                out=hn[:, kc, :], in0=h_sb[:, kc, :], scalar=C(gain_name, kc), in1=rs[:, :],
                op0=ALU.mult, op1=ALU.mult), reads=[h_c[kc], cols, rs], writes=[hn])

    def load_h(src_ap, src_buf, t):
        for kc in range(8):
            S.op("sp", lambda e, kc=kc: e.dma_start(out=h_sb[:, kc, :],
                                                    in_=src_ap[kc * 128:(kc + 1) * 128, t * TW:(t + 1) * TW]),
                 reads=[src_buf[kc]] if src_buf is not None else [], writes=[h_c[kc]], dma=h_c[kc])

    def proj_fm(wview, j0, kcn, rhs_b, rhs_of_kc, ps):
        for kc in range(kcn):
            S.op("pe", lambda e, kc=kc: e.matmul(ps[:, :], lhsT=wview[:, kc, j0:j0 + 128], rhs=rhs_of_kc(kc),
                                                 start=(kc == 0), stop=(kc == kcn - 1)),
                 reads=[rhs_b], writes=[ps])

    def group_norm_to(ps, gain_ap, out_b, out_ap, group, parts=128):
        sq = sqr.next()
        S.op("act", lambda e: e.activation(out=sq[0:parts, :], in_=ps[0:parts, :], func=AF.Square),
             reads=[ps], writes=[sq])
        st = str_.next()
        lhs = bones if group == 64 else ones
        S.op("pe", lambda e: e.matmul(st[0:parts, :], lhsT=lhs[0:parts, 0:parts], rhs=sq[0:parts, :], start=True, stop=True),
             reads=[lhs, sq], writes=[st])
        rs = rstd_from(st, st[0:parts, :], 1.0 / group, parts=parts)
        S.op("dve", lambda e: e.scalar_tensor_tensor(out=out_ap, in0=ps[0:parts, :], scalar=gain_ap, in1=rs[0:parts, :],
                                                     op0=ALU.mult, op1=ALU.mult),
             reads=[ps, cols, rs], writes=[out_b])

    S.op("sp", lambda e: e.dma_start(out=cols[:, :], in_=colsd), writes=[cols], dma=cols)
    S.op("sp", lambda e: e.dma_start(out=rows[:, :], in_=rowsd), writes=[rows], dma=rows)
    S.op("pool", lambda e: e.dma_start(out=prot[:, :], in_=protd), writes=[prot], dma=prot)
    S.op("dve", lambda e: e.memset(ones[:, :], 1.0), writes=[ones])
    S.op("dve", lambda e: e.memset(bones[:, :], 0.0), writes=[bones])
    S.op("dve", lambda e: e.memset(bones[0:64, 0:64], 1.0), writes=[bones])
    S.op("dve", lambda e: e.memset(bones[64:128, 64:128], 1.0), writes=[bones])
    S.op("dve", lambda e: e.memset(epsc[:, :], EPS), writes=[epsc])
    S.op("pool", lambda e: e.memset(ident[:, :], 0.0), writes=[ident])
    S.op("pool", lambda e: e.affine_select(out=ident[:, :], in_=ident[:, :], pattern=[[-1, 128]],
                                           compare_op=ALU.not_equal, fill=1.0, base=0, channel_multiplier=1),
         reads=[ident], writes=[ident])
    S.op("pool", lambda e: e.memset(tri[:, :], 1.0), writes=[tri])
    S.op("pool", lambda e: e.affine_select(out=tri[:, :], in_=tri[:, :], pattern=[[1, 128]],
                                           compare_op=ALU.is_ge, fill=0.0, base=0, channel_multiplier=-1),
         reads=[tri], writes=[tri])
    for r_ in vst.bufs:
        S.op("dve", lambda e, r_=r_: e.memset(r_[:, :, :, :], 1.0), writes=[r_])

    nA = min(L, 2)
    for r_ in qring.bufs:
        S.op("dve", lambda e, r_=r_: e.memset(r_[64:128, :], 0.0), writes=[r_])
    for r_ in qpring.bufs:
        S.op("dve", lambda e, r_=r_: e.memset(r_[0:64, :], 0.0), writes=[r_])

    if nA > 0:
        class V3:
            def __init__(self, b):
                self.b = b

            def __getitem__(self, idx):
                return self.b[:, 0:256].rearrange("p (d q) -> p d q", d=2)[idx]

        posi = itmp
        pci = S.sbuf("pci", [128, 2], I32)
        posf = sgp.bufs[0]
        pcf = S.sbuf("pcf", [128, 2], F32)
        dt_b = cpool.bufs[0]
        dt_ = V3(dt_b)
        ge0_b = cpool.bufs[1]
        ge0 = V3(ge0_b)
        ge_bufs = [cpool.bufs[2], cpool.bufs[3]]
        ge = Ring([V3(b) for b in ge_bufs])
        dtab = S.sbuf("dtab", [128, 256], F32)
        nb31 = S.sbuf("nb31", [128, 8], F32)
        S.op("sp", lambda e: e.dma_start(out=posi[:, 0:256], in_=posb[:, 0:256]), writes=[posi], dma=posi)
        S.op("sp", lambda e: e.dma_start(out=pci[:, :], in_=poscol), writes=[pci], dma=pci)
        S.op("dve", lambda e: e.tensor_copy(out=posf[:, 0:256], in_=posi[:, 0:256]), reads=[posi], writes=[posf])
        S.op("dve", lambda e: e.tensor_copy(out=pcf[:, :], in_=pci[:, :]), reads=[pci], writes=[pcf])
        S.op("dve", lambda e: e.tensor_scalar(out=dt_[:, :, :], in0=posf[:, 0:256].rearrange("p (d q) -> p d q", d=2),
                                              scalar1=pcf[:, 0:1], scalar2=None, op0=ALU.subtract),
             reads=[posf, pcf], writes=[dt_b])
        tb = cm.m["table"]
        S.op("dve", lambda e: e.tensor_copy(out=dtab[:, 0:8], in_=cols[:, tb:tb + 8]), reads=[cols], writes=[dtab])
        S.op("dve", lambda e: e.tensor_tensor(out=dtab[:, 8:256], in0=cols[:, tb + 8:tb + 256], in1=cols[:, tb:tb + 248],
                                              op=ALU.subtract), reads=[cols], writes=[dtab])
        S.op("dve", lambda e: e.tensor_scalar(out=nb31[:, :], in0=C("b31", 0, 8), scalar1=-1.0, scalar2=None, op0=ALU.mult),
             reads=[cols], writes=[nb31])
        S.op("dve", lambda e: e.tensor_scalar(out=ge0[:, :, :], in0=dt_[:, :, :], scalar1=0.0, scalar2=None, op0=ALU.is_ge),
             reads=[dt_b], writes=[ge0_b])
        for h in range(8):
            S.op("dve", lambda e, h=h: e.tensor_scalar(out=Rt[:, :, h, :], in0=ge0[:, :, :], scalar1=dtab[:, h:h + 1],
                                                       scalar2=None, op0=ALU.mult), reads=[ge0_b, dtab], writes=[Rt])
        for b in range(1, 32):
            g = ge.next()
            S.op("dve", lambda e, g=g, b=b: e.tensor_scalar(out=g[:, :, :], in0=dt_[:, :, :], scalar1=float(lo_thr[b]),
                                                            scalar2=None, op0=ALU.is_ge), reads=[dt_b], writes=[g.b])
            for h in range(8):
                S.op("dve", lambda e, g=g, b=b, h=h: e.scalar_tensor_tensor(
                    out=Rt[:, :, h, :], in0=g[:, :, :], scalar=dtab[:, b * 8 + h:b * 8 + h + 1], in1=Rt[:, :, h, :],
                    op0=ALU.mult, op1=ALU.add), reads=[g.b, dtab, Rt], writes=[Rt])
        for h in range(8):
            S.op("act", lambda e, h=h: e.activation(out=Rt[:, :, h, :], in_=Rt[:, :, h, :], func=AF.Exp,
                                                    bias=nb31[:, h:h + 1], scale=1.0), reads=[Rt, nb31], writes=[Rt])
            S.op("dve", lambda e, h=h: e.tensor_tensor(out=Rt[:, :, h, :], in0=Rt[:, :, h, :], in1=ge0[:, :, :], op=ALU.mult),
                 reads=[Rt, ge0_b], writes=[Rt])
        for l in range(nA):
            lam_init = 0.8 - 0.6 * math.exp(-0.3 * l)
            sm = smr.next()
            for i, (a, b) in enumerate((("q1", "k1"), ("q2", "k2"))):
                ra = rm.m["lam_%s%d" % (a, l)]
                rb = rm.m["lam_%s%d" % (b, l)]
                S.op("dve", lambda e, ra=ra, rb=rb: e.tensor_tensor(out=junk[:, 0:64], in0=rows[:, ra:ra + 64],
                                                                    in1=rows[:, rb:rb + 64], op=ALU.mult),
                     reads=[rows], writes=[junk])
                S.op("act", lambda e, i=i, sm=sm: e.activation(out=junk[:, 64:128], in_=junk[:, 0:64], func=AF.Identity,
                                                               accum_out=sm[:, i:i + 1]), reads=[junk], writes=[junk, sm])
            S.op("act", lambda e, sm=sm: e.activation(out=sm[:, 2:4], in_=sm[:, 0:2], func=AF.Exp), reads=[sm], writes=[sm])
            S.op("dve", lambda e, sm=sm, l=l: e.tensor_tensor(out=lamt[:, 2 * l:2 * l + 1], in0=sm[:, 3:4], in1=sm[:, 2:3],
                                                              op=ALU.subtract), reads=[sm], writes=[lamt])
            S.op("dve", lambda e, l=l, lam_init=lam_init: e.tensor_scalar(
                out=lamt[:, 2 * l:2 * l + 1], in0=lamt[:, 2 * l:2 * l + 1], scalar1=-lam_init, scalar2=None, op0=ALU.add),
                reads=[lamt], writes=[lamt])
            sg0 = rm.m["sub_gain%d" % l]
            S.op("dve", lambda e, l=l, sg0=sg0, lam_init=lam_init: e.tensor_scalar(
                out=subg[:, l, :], in0=rows[:, sg0:sg0 + 128], scalar1=1.0 - lam_init, scalar2=None, op0=ALU.mult),
                reads=[rows], writes=[subg])

    ST = [[pb[0], pb[1]], [pb[2], pb[3]]]
    ACC = [pb[4], pb[5], pb[6]]
    TP = pb[7]

    def attention_head(kind, l, h):
        nsub = 2 if kind == "A" else 1
        sc = 0.125 if kind == "A" else (192.0 ** -0.5)
        S.op("sp", lambda e: e.dma_start(out=attk[:, :], in_=KA[h]), reads=[KA_r[h][t] for t in range(NT)],
             writes=[attk], dma=attk)
        S.op("sp", lambda e: e.dma_start(out=attv[:, :, :], in_=VV[h]), reads=[VV_r[h][t] for t in range(NT)],
             writes=[attv], dma=attv)
        steps = [(t, j) for t in range(NT) for j in range(4 * t + 4)]
        qtiles = {}
        state = {"touched": set()}

        def load_q(t):
            attq = qring.next()
            attqp = qpring.next()
            if kind == "A":
                S.op("sp", lambda e, attq=attq, t=t: e.dma_start(out=attq[0:64, :], in_=QA[h, 0:64, t * TW:(t + 1) * TW]),
                     reads=[QA_r[h][t]], writes=[attq], dma=attq)
                S.op("sp", lambda e, attqp=attqp, t=t: e.dma_start(out=attqp[64:128, :], in_=QA[h, 64:128, t * TW:(t + 1) * TW]),
                     reads=[QA_r[h][t]], writes=[attqp], dma=attqp)
            else:
                S.op("sp", lambda e, attq=attq, t=t: e.dma_start(out=attq[:, :], in_=QA[h, :, t * TW:(t + 1) * TW]),
                     reads=[QA_r[h][t]], writes=[attq], dma=attq)
                r0 = 64 * (h % 2)
                S.op("sp", lambda e, attqp=attqp, t=t, r0=r0: e.dma_start(
                    out=attqp[0:64, :], in_=QP[h // 2, r0:r0 + 64, t * TW:(t + 1) * TW]),
                    reads=[QP_r[h // 2][t]], writes=[attqp], dma=attqp)
            qtiles[t] = (attq, attqp)

        def scores(i):
            t, j = steps[i]
            if j == 0 and t + 1 < NT:
                load_q(t + 1)
            attq, attqp = qtiles[t]
            b0 = max(0, j - 4 * t)
            n = TW - 128 * b0
            q0 = 128 * b0
            par = i % 2
            for c in range(nsub):
                st = ST[c][par]
                if kind == "A":
                    qq = attq if c == 0 else attqp
                    S.op("pe", lambda e, st=st, j=j, n=n, q0=q0, qq=qq: e.matmul(
                        st[:, 0:n], lhsT=attk[:, j * 128:(j + 1) * 128], rhs=qq[:, q0:q0 + n], start=True, stop=True),
                        reads=[attk, qq], writes=[st])
                else:
                    S.op("pe", lambda e, st=st, j=j, n=n, q0=q0, attq=attq: e.matmul(
                        st[:, 0:n], lhsT=attk[:, j * 128:(j + 1) * 128], rhs=attq[:, q0:q0 + n],
                        start=True, stop=False), reads=[attk, attq], writes=[st])
                    S.op("pe", lambda e, st=st, j=j, n=n, q0=q0, attqp=attqp: e.matmul(
                        st[:, 0:n], lhsT=attkp[:, j * 128:(j + 1) * 128], rhs=attqp[:, q0:q0 + n],
                        start=False, stop=True), reads=[attkp, attqp], writes=[st])

        def probs_pv(i):
            t, j = steps[i]
            touched = state["touched"]
            if j == 0:
                touched.clear()
            b0 = max(0, j - 4 * t)
            n = TW - 128 * b0
            par = i % 2
            pts = []
            for c in range(nsub):
                st = ST[c][par]
                pt = ppool.next()
                pts.append(pt)
                if kind == "A":
                    S.op("act", lambda e, pt=pt, st=st, n=n: e.activation(
                        out=pt[:, 0:n], in_=st[:, 0:n], func=AF.Exp, bias=C("b31", h), scale=sc),
                        reads=[st, cols], writes=[pt])
                else:
                    S.op("act", lambda e, pt=pt, st=st, n=n: e.activation(
                        out=pt[:, 0:n], in_=st[:, 0:n], func=AF.Exp, scale=sc), reads=[st], writes=[pt])
                for blk in range(b0, 4):
                    dd = 4 * t + blk - j
                    cs = slice((blk - b0) * 128, (blk - b0 + 1) * 128)
                    if kind == "A" and dd in (0, 1):
                        S.op("dve", lambda e, pt=pt, cs=cs, dd=dd: e.tensor_tensor(
                            out=pt[:, cs], in0=pt[:, cs], in1=Rt[:, dd, h, :], op=ALU.mult),
                            reads=[pt, Rt], writes=[pt])
                    elif kind == "B" and dd == 0:
                        S.op("dve", lambda e, pt=pt, cs=cs: e.tensor_tensor(
                            out=pt[:, cs], in0=pt[:, cs], in1=tri[:, :], op=ALU.mult),
                            reads=[pt, tri], writes=[pt])
            for c in range(nsub):
                pt = pts[c]
                for blk in range(b0, 4):
                    idx = c * 4 + blk
                    bank = ACC[idx // 3]
                    off = (idx % 3) * 129
                    first = (idx // 3) not in touched
                    touched.add(idx // 3)
                    cs = slice((blk - b0) * 128, (blk - b0 + 1) * 128)
                    S.op("pe", lambda e, pt=pt, cs=cs, bank=bank, off=off, first=first, j=j, t=t, blk=blk: e.matmul(
                        bank[:, off:off + 129], lhsT=pt[:, cs], rhs=attv[:, j, :], start=first,
                        stop=(j == 4 * t + blk), skip_group_check=True), reads=[pt, attv], writes=[bank])

        def finalize(t):
            ost = otst.next()
            for blk in range(4):
                sm = smr.next()
                osb = o_sb.next()
                a0b, a0o = ACC[blk // 3], (blk % 3) * 129
                S.op("dve", lambda e, sm=sm, a0b=a0b, a0o=a0o: e.reciprocal(out=sm[:, 0:1], in_=a0b[:, a0o + 128:a0o + 129]),
                     reads=[a0b], writes=[sm])
                S.op("dve", lambda e, sm=sm, a0b=a0b, a0o=a0o, osb=osb: e.tensor_scalar(
                    out=osb[:, :], in0=a0b[:, a0o:a0o + 128], scalar1=sm[:, 0:1], scalar2=None, op0=ALU.mult),
                    reads=[a0b, sm], writes=[osb])
                if kind == "A":
                    i1 = 4 + blk
                    a1b, a1o = ACC[i1 // 3], (i1 % 3) * 129
                    onb = on_sb.next()
                    S.op("dve", lambda e, sm=sm, a1b=a1b, a1o=a1o: e.reciprocal(out=sm[:, 1:2], in_=a1b[:, a1o + 128:a1o + 129]),
                         reads=[a1b], writes=[sm])
                    S.op("dve", lambda e, sm=sm: e.tensor_tensor(out=sm[:, 1:2], in0=sm[:, 1:2], in1=lamt[:, 2 * l:2 * l + 1],
                                                                 op=ALU.mult), reads=[sm, lamt], writes=[sm])
                    S.op("dve", lambda e, sm=sm, a1b=a1b, a1o=a1o, osb=osb: e.scalar_tensor_tensor(
                        out=osb[:, :], in0=a1b[:, a1o:a1o + 128], scalar=sm[:, 1:2], in1=osb[:, :],
                        op0=ALU.mult, op1=ALU.add), reads=[a1b, sm, osb], writes=[osb])
                    S.op("act", lambda e, sm=sm, osb=osb: e.activation(out=junk[:, :], in_=osb[:, :], func=AF.Square,
                                                                       accum_out=sm[:, 2:3]), reads=[osb], writes=[junk, sm])
                    S.op("act", lambda e, sm=sm: e.activation(out=sm[:, 3:4], in_=sm[:, 2:3], func=AF.Ln, bias=epsc[:, :],
                                                              scale=1.0 / 128), reads=[sm, epsc], writes=[sm])
                    S.op("act", lambda e, sm=sm: e.activation(out=sm[:, 4:5], in_=sm[:, 3:4], func=AF.Exp, scale=-0.5),
                         reads=[sm], writes=[sm])
                    S.op("dve", lambda e, sm=sm, osb=osb, onb=onb: e.scalar_tensor_tensor(
                        out=onb[:, :], in0=osb[:, :], scalar=sm[:, 4:5], in1=subg[:, l, :], op0=ALU.mult, op1=ALU.mult),
                        reads=[osb, sm, subg], writes=[onb])
                    src = onb
                else:
                    src = osb
                S.op("pe", lambda e, src=src, blk=blk: e.transpose(TP[:, blk * 128:(blk + 1) * 128], src[:, :], ident[:, :]),
                     reads=[src, ident], writes=[TP])
            S.op("act", lambda e, ost=ost: e.activation(out=ost[:, :], in_=TP[:, :], func=AF.Identity), reads=[TP], writes=[ost])
            S.op("sp", lambda e, ost=ost, t=t: e.dma_start(out=OT[h, :, t * TW:(t + 1) * TW], in_=ost[:, :]),
                 reads=[ost], writes=[OT_r[h][t]], dma=ost)


        load_q(0)
        scores(0)
        for i in range(len(steps)):
            if i + 1 < len(steps):
                scores(i + 1)
            probs_pv(i)
            t, j = steps[i]
            if j == 4 * t + 3:
                finalize(t)

    ot_pref = set()
    def dense_tile(l, t, H, w_o_ap, hsrc_ap, hsrc_b, hdst_ap, hdst_b):
        def load_ot(tt, half):
            S.op("sp", lambda e: e.dma_start(
                out=ot_sb[:, :, :], in_=OT[8 * half:8 * half + 8, :, tt * TW:(tt + 1) * TW].rearrange("h p s -> p h s")),
                reads=[OT_r[8 * half + i][tt] for i in range(8)], writes=[ot_sb], dma=ot_sb)

        hdst_cb = hdst_b[t] if hdst_b is not None else [Buf("odst_%d_%d_%d" % (l, t, k)) for k in range(8)]
        if (l, t) not in ot_pref:
            load_ot(t, 0)
        load_h(hsrc_ap, hsrc_b, t)
        for half in range(H // 8):
            if half > 0:
                load_ot(t, half)
            for mg in range(2):
                slot, wv = load_w(("wo", l, half, mg), w_o_ap[half * 1024:(half + 1) * 1024, mg * 512:(mg + 1) * 512], 8, 512)
                for mj in range(4):
                    m = mg * 4 + mj
                    ps = mmr.next()
                    for hh in range(8):
                        S.op("pe", lambda e, hh=hh, mj=mj, ps=ps, wv=wv: e.matmul(
                            ps[:, :], lhsT=wv[:, hh, mj * 128:(mj + 1) * 128], rhs=ot_sb[:, hh, :],
                            start=(hh == 0), stop=(hh == 7)), reads=[slot, ot_sb], writes=[ps])
                    S.op("dve", lambda e, m=m, ps=ps: e.tensor_tensor(out=h_sb[:, m, :], in0=ps[:, :], in1=h_sb[:, m, :],
                                                                      op=ALU.add), reads=[ps, h_c[m]], writes=[h_c[m]])
        rmsnorm("ffn_norm%d" % l)
        w_in = ffn_w_in[l]
        for ig in range(0, NFC, 2):
            slot, wv = load_w2(("win", l, ig), w_in[:, ig * 128:(ig + 2) * 128], w_in[:, DFF + ig * 128:DFF + (ig + 2) * 128], 8, 256, 256)
            for ii in range(2):
                i = ig + ii
                pss, uxs, cbs, ccs = [], [], [], []
                for part in range(2):
                    cc = part * NFC + i
                    ps = mmr.next()
                    proj_w(slot, wv, part * 256 + ii * 128, 8, hn, lambda kc: hn[:, kc, :], ps)
                    pss.append(ps)
                    ccs.append(cc)
                for part in range(2):
                    ux = uext.next()
                    uxs.append(ux)
                    S.op("act", lambda e, ux=ux, ps=pss[part]: e.activation(out=ux[:, 2:TW + 2], in_=ps[:, :], func=AF.Identity),
                         reads=[pss[part]], writes=[ux])
                for part in range(2):
                    ux, cc = uxs[part], ccs[part]
                    S.op("dve", lambda e, ux=ux, cc=cc: e.tensor_copy(out=ux[:, 0:2], in_=utail_t[:, cc, :]),
                         reads=[utail[cc]], writes=[ux])
                for part in range(2):
                    ux, cc = uxs[part], ccs[part]
                    S.op("dve", lambda e, ux=ux, cc=cc: e.tensor_copy(out=utail_t[:, cc, :], in_=ux[:, TW:TW + 2]),
                         reads=[ux], writes=[utail[cc]])
                for part in range(2):
                    cc = ccs[part]
                    cb = cpool.next()
                    cbs.append(cb)
                    S.op("act", lambda e, cb=cb, cc=cc, ps=pss[part]: e.activation(
                        out=cb[:, :], in_=ps[:, :], func=AF.Identity, scale=C("conv_w%d_2" % l, cc), bias=C("conv_b%d" % l, cc)),
                        reads=[pss[part], cols], writes=[cb])
                for jj in (1, 0):
                    for part in range(2):
                        ux, cc, cb = uxs[part], ccs[part], cbs[part]
                        S.op("dve", lambda e, ux=ux, cb=cb, cc=cc, jj=jj: e.scalar_tensor_tensor(
                            out=cb[:, :], in0=ux[:, jj:TW + jj], scalar=C("conv_w%d_%d" % (l, jj), cc), in1=cb[:, :],
                            op0=ALU.mult, op1=ALU.add), reads=[ux, cols, cb], writes=[cb])
                sg = sgp.next()
                S.op("act", lambda e, sg=sg, cg=cbs[1]: e.activation(out=sg[:, :], in_=cg[:, :], func=AF.Silu),
                     reads=[cbs[1]], writes=[sg])
                S.op("pool", lambda e, sg=sg, ca=cbs[0], i=i: e.tensor_tensor(out=z_sb[:, i, :], in0=sg[:, :], in1=ca[:, :],
                                                                              op=ALU.mult), reads=[sg, cbs[0]], writes=[z_sb])
        w_out = ffn_w_out[l]
        for m in range(8):
            slot, wv = load_w(("wout", l, m), w_out[:, m * 128:(m + 1) * 128], NFC, 128)
            ps = mmr.next()
            for i in range(NFC):
                S.op("pe", lambda e, i=i, ps=ps, wv=wv: e.matmul(ps[:, :], lhsT=wv[:, i, :], rhs=z_sb[:, i, :],
                                                                 start=(i == 0), stop=(i == NFC - 1)),
                     reads=[slot, z_sb], writes=[ps])
            S.op("dve", lambda e, m=m, ps=ps: e.tensor_tensor(out=h_sb[:, m, :], in0=ps[:, :], in1=h_sb[:, m, :], op=ALU.add),
                 reads=[ps, h_c[m]], writes=[h_c[m]])
        rmsnorm("ple_norm%d" % l)
        S.op("pool", lambda e: e.dma_start(out=pt_sb[:, :, :],
                                           in_=pT[l, :, t * TW:(t + 1) * TW].rearrange("(k p) s -> p k s", p=128)),
             writes=[pt_sb], dma=pt_sb)
        pslot, pwv = load_w(("pproj", l), ple_w_proj[l], 2, 1024)
        for mg in range(2):
            slot, wv = load_w(("pgate", l, mg), ple_w_gate[l][:, mg * 512:(mg + 1) * 512], 8, 512)
            for mj in range(4):
                m = mg * 4 + mj
                psg = mmr.next()
                for kc in range(8):
                    S.op("pe", lambda e, kc=kc, mj=mj, psg=psg, wv=wv: e.matmul(
                        psg[:, :], lhsT=wv[:, kc, mj * 128:(mj + 1) * 128], rhs=hn[:, kc, :],
                        start=(kc == 0), stop=(kc == 7)), reads=[slot, hn], writes=[psg])
                sg = sgp.next()
                S.op("act", lambda e, sg=sg, psg=psg: e.activation(out=sg[:, :], in_=psg[:, :], func=AF.Sigmoid),
                     reads=[psg], writes=[sg])
                psp = mmr.next()
                for kc in range(2):
                    S.op("pe", lambda e, kc=kc, m=m, psp=psp, pwv=pwv: e.matmul(
                        psp[:, :], lhsT=pwv[:, kc, m * 128:(m + 1) * 128], rhs=pt_sb[:, kc, :],
                        start=(kc == 0), stop=(kc == 1)), reads=[pslot, pt_sb], writes=[psp])
                cb = cpool.next()
                S.op("dve", lambda e, cb=cb, psp=psp, sg=sg: e.tensor_tensor(out=cb[:, :], in0=psp[:, :], in1=sg[:, :],
                                                                             op=ALU.mult), reads=[psp, sg], writes=[cb])
                S.op("dve", lambda e, cb=cb, m=m: e.tensor_tensor(out=h_sb[:, m, :], in0=cb[:, :], in1=h_sb[:, m, :],
                                                                  op=ALU.add), reads=[cb, h_c[m]], writes=[h_c[m]])
                if m == 6 and t + 1 < NT:
                    load_ot(t + 1, 0)
                    ot_pref.add((l, t + 1))
                S.op("sp", lambda e, m=m: e.dma_start(out=hdst_ap[m * 128:(m + 1) * 128, t * TW:(t + 1) * TW], in_=h_sb[:, m, :]),
                     reads=[h_c[m]], writes=[hdst_cb[m]], dma=h_c[m])

    def proj_w(slot, wview, j0, kcn, rhs_b, rhs_of_kc, ps, parts=128, mcols=128):
        for kc in range(kcn):
            S.op("pe", lambda e, kc=kc: e.matmul(ps[0:mcols, :], lhsT=wview[:, kc, j0:j0 + mcols], rhs=rhs_of_kc(kc),
                                                 start=(kc == 0), stop=(kc == kcn - 1)),
                 reads=[slot, rhs_b], writes=[ps])

    _dense_slot = {}

    def a_phase1(l, t, hsrc_ap, hsrc_b):
        load_h(hsrc_ap, hsrc_b, t)
        rmsnorm("attn_norm%d" % l)
        wq = a_w_qkv[l]
        for grp in range(4):
            slot, wv = load_w(("qkv", l, grp), wq[:, grp * 512:(grp + 1) * 512], 8, 512)
            for j in range(4):
                oc = grp * 4 + j
                ps = mmr.next()
                proj_w(slot, wv, j * 128, 8, hn, lambda kc: hn[:, kc, :], ps)
                sb = stg.next()
                gname = ("a_q_norm%d" if oc < 8 else "a_k_norm%d") % l
                group_norm_to(ps, C(gname), sb, sb[:, :], 64)
                dst, dreg = (QA, QA_r) if oc < 8 else (KA, KA_r)
                hh = oc % 8
                S.op("sp", lambda e, sb=sb, dst=dst, hh=hh: e.dma_start(out=dst[hh, :, t * TW:(t + 1) * TW], in_=sb[:, :]),
                     reads=[sb], writes=[dreg[hh][t]], dma=sb)
        for half in range(2):
            slot, wv = load_w(("qkvv", l, half), wq[:, 2048 + half * 512:2048 + (half + 1) * 512], 8, 512)
            vs = vst.next()
            for blk in range(4):
                ps = mmr.next()
                for kc in range(8):
                    S.op("pe", lambda e, kc=kc, blk=blk, ps=ps, wv=wv: e.matmul(
                        ps[:, :], lhsT=hn[:, kc, blk * 128:(blk + 1) * 128], rhs=wv[:, kc, :],
                        start=(kc == 0), stop=(kc == 7)), reads=[slot, hn], writes=[ps])
                S.op("dve", lambda e, blk=blk, ps=ps, vs=vs: e.tensor_copy(
                    out=vs[:, :, blk, 0:128], in_=ps[:, :].rearrange("p (h d) -> p h d", h=4)), reads=[ps], writes=[vs])
            S.op("sp", lambda e, half=half, vs=vs: e.dma_start(
                out=VV[4 * half:4 * half + 4, :, 4 * t:4 * t + 4, :].rearrange("h p b e -> p h b e"), in_=vs[:, :, :, :]),
                reads=[vs], writes=[VV_r[4 * half + i][t] for i in range(4)], dma=vs)

    def rotary_tables():
        TWO_PI = 2.0 * math.pi
        C1 = 6.28125
        C2 = TWO_PI - C1
        pi_t = itmp
        ang = cpool.bufs[0]
        kf = cpool.bufs[1]
        ki = itmp2
        mk = cpool.bufs[2]
        for t in range(NT):
            S.op("sp", lambda e, t=t: e.dma_start(out=pi_t[:, :], in_=posb[:, t * TW:(t + 1) * TW]), writes=[pi_t], dma=pi_t)
            S.op("dve", lambda e: e.tensor_copy(out=ang[:, :], in_=pi_t[:, :]), reads=[pi_t], writes=[ang])
            S.op("dve", lambda e: e.tensor_scalar(out=ang[:, :], in0=ang[:, :], scalar1=C("invfreq"), scalar2=None,
                                                  op0=ALU.mult), reads=[ang, cols], writes=[ang])
            S.op("dve", lambda e: e.tensor_scalar(out=ki[:, :], in0=ang[:, :], scalar1=1.0 / TWO_PI, scalar2=None,
                                                  op0=ALU.mult), reads=[ang], writes=[ki])
            S.op("dve", lambda e: e.tensor_copy(out=kf[:, :], in_=ki[:, :]), reads=[ki], writes=[kf])
            S.op("dve", lambda e: e.scalar_tensor_tensor(out=ang[:, :], in0=kf[:, :], scalar=-C1, in1=ang[:, :],
                                                         op0=ALU.mult, op1=ALU.add), reads=[kf, ang], writes=[ang])
            S.op("dve", lambda e: e.scalar_tensor_tensor(out=ang[:, :], in0=kf[:, :], scalar=-C2, in1=ang[:, :],
                                                         op0=ALU.mult, op1=ALU.add), reads=[kf, ang], writes=[ang])
            S.op("dve", lambda e: e.tensor_scalar(out=mk[:, :], in0=ang[:, :], scalar1=math.pi, scalar2=None,
                                                  op0=ALU.is_gt), reads=[ang], writes=[mk])
            S.op("dve", lambda e: e.scalar_tensor_tensor(out=ang[:, :], in0=mk[:, :], scalar=-TWO_PI, in1=ang[:, :],
                                                         op0=ALU.mult, op1=ALU.add), reads=[mk, ang], writes=[ang])
            S.op("dve", lambda e: e.tensor_scalar(out=mk[:, :], in0=ang[:, :], scalar1=-math.pi, scalar2=None,
                                                  op0=ALU.is_lt), reads=[ang], writes=[mk])
            S.op("dve", lambda e: e.scalar_tensor_tensor(out=ang[:, :], in0=mk[:, :], scalar=TWO_PI, in1=ang[:, :],
                                                         op0=ALU.mult, op1=ALU.add), reads=[mk, ang], writes=[ang])
            S.op("dve", lambda e: e.tensor_scalar(out=ang[:, :], in0=ang[:, :], scalar1=-3.1415925, scalar2=3.1415925,
                                                  op0=ALU.max, op1=ALU.min), reads=[ang], writes=[ang])
            S.op("act", lambda e: e.activation(out=cs_sb[:, 1, :], in_=ang[:, :], func=AF.Sin), reads=[ang], writes=[cs_sb])
            S.op("dve", lambda e: e.scalar_tensor_tensor(out=kf[:, :], in0=ang[:, :], scalar=-1.0, in1=ang[:, :],
                                                         op0=ALU.mult, op1=ALU.max), reads=[ang], writes=[kf])
            S.op("dve", lambda e: e.tensor_scalar(out=kf[:, :], in0=kf[:, :], scalar1=-1.0, scalar2=math.pi / 2,
                                                  op0=ALU.mult, op1=ALU.add), reads=[kf], writes=[kf])
            S.op("act", lambda e: e.activation(out=cs_sb[:, 0, :], in_=kf[:, :], func=AF.Sin), reads=[kf], writes=[cs_sb])
            S.op("sp", lambda e, t=t: [e.dma_start(out=COS[:, t * TW:(t + 1) * TW], in_=cs_sb[:, 0, :]),
                                       e.dma_start(out=SIN[:, t * TW:(t + 1) * TW], in_=cs_sb[:, 1, :])],
                 reads=[cs_sb], writes=[CS_r[t]], dma=cs_sb, ninc=2)

    def load_cs(t):
        S.op("sp", lambda e: [e.dma_start(out=cs_sb[:, 0, :], in_=COS[:, t * TW:(t + 1) * TW]),
                              e.dma_start(out=cs_sb[:, 1, :], in_=SIN[:, t * TW:(t + 1) * TW])],
             reads=[CS_r[t]], writes=[cs_sb], dma=cs_sb, ninc=2)

    def apply_rotary(xb, parts, out_b, out_ap):
        rp = mmr.next()
        S.op("pe", lambda e: e.matmul(rp[0:parts, :], lhsT=prot[0:parts, 0:parts], rhs=xb[0:parts, :], start=True, stop=True),
             reads=[prot, xb], writes=[rp])
        cb = cpool.next()
        S.op("dve", lambda e: e.tensor_tensor(out=cb[0:parts, :], in0=rp[0:parts, :], in1=cs_sb[0:parts, 1, :], op=ALU.mult),
             reads=[rp, cs_sb], writes=[cb])
        cb2 = cpool.next()
        S.op("dve", lambda e: e.tensor_tensor(out=cb2[0:parts, :], in0=xb[0:parts, :], in1=cs_sb[0:parts, 0, :], op=ALU.mult),
             reads=[xb, cs_sb], writes=[cb2])
        S.op("dve", lambda e: e.tensor_tensor(out=out_ap, in0=cb[0:parts, :], in1=cb2[0:parts, :], op=ALU.add),
             reads=[cb, cb2], writes=[out_b])

    def shared_kv_tile(t, hsrc_ap, hsrc_b):
        load_h(hsrc_ap, hsrc_b, t)
        load_cs(t)
        rmsnorm("kv_norm")
        slot, wv = load_w(("dkv",), w_dkv, 8, 320)
        pss = []
        for c in range(2):
            ps = mmr.next()
            proj_w(slot, wv, c * 128, 8, hn, lambda kc: hn[:, kc, :], ps)
            pss.append(ps)
        st = STAT
        for c in range(2):
            sq = sqr.next()
            S.op("act", lambda e, sq=sq, c=c: e.activation(out=sq[:, :], in_=pss[c][:, :], func=AF.Square),
                 reads=[pss[c]], writes=[sq])
            S.op("pe", lambda e, sq=sq, c=c: e.matmul(st[:, :], lhsT=ones[:, :], rhs=sq[:, :], start=(c == 0), stop=(c == 1)),
                 reads=[ones, sq], writes=[st])
        rs = rstd_from(st, st[:, :], 1.0 / 256)
        for c in range(2):
            S.op("dve", lambda e, c=c: e.scalar_tensor_tensor(out=ckvn[:, c, :], in0=pss[c][:, :], scalar=C("ckv_norm", c),
                                                              in1=rs[:, :], op0=ALU.mult, op1=ALU.mult),
                 reads=[pss[c], cols, rs], writes=[ckvn])
        ps = mmr.next()
        proj_w(slot, wv, 256, 8, hn, lambda kc: hn[:, kc, :], ps, mcols=64)
        sb = stg.next()
        group_norm_to(ps, cols[0:64, cm.m["k_pe_norm"]:cm.m["k_pe_norm"] + 1], sb, sb[0:64, :], 64, parts=64)
        sb2 = stg.next()
        apply_rotary(sb, 64, sb2, sb2[0:64, :])
        S.op("sp", lambda e: e.dma_start(out=KPE[:, t * TW:(t + 1) * TW], in_=sb2[0:64, :]), reads=[sb2],
             writes=[KPE_r[t]], dma=sb2)
        for g4 in range(4):
            slot, wv = load_w(("ukvK", g4), w_ukvK[:, g4 * 512:(g4 + 1) * 512], 2, 512)
            for j in range(4):
                hh = g4 * 4 + j
                ps = mmr.next()
                proj_w(slot, wv, j * 128, 2, ckvn, lambda kc: ckvn[:, kc, :], ps)
                sb = stg.next()
                group_norm_to(ps, C("k_nope_norm"), sb, sb[:, :], 128)
                S.op("sp", lambda e, sb=sb, hh=hh: e.dma_start(out=KA[hh, :, t * TW:(t + 1) * TW], in_=sb[:, :]),
                     reads=[sb], writes=[KA_r[hh][t]], dma=sb)
        for g4 in range(4):
            slot, wv = load_w(("ukvV", g4), w_ukvV[:, g4 * 512:(g4 + 1) * 512], 2, 512)
            vs = vst.next()
            for blk in range(4):
                ps = mmr.next()
                for kc in range(2):
                    S.op("pe", lambda e, kc=kc, blk=blk, ps=ps, wv=wv: e.matmul(
                        ps[:, :], lhsT=ckvn[:, kc, blk * 128:(blk + 1) * 128], rhs=wv[:, kc, :],
                        start=(kc == 0), stop=(kc == 1)), reads=[slot, ckvn], writes=[ps])
                S.op("dve", lambda e, blk=blk, ps=ps, vs=vs: e.tensor_copy(
                    out=vs[:, :, blk, 0:128], in_=ps[:, :].rearrange("p (h d) -> p h d", h=4)), reads=[ps], writes=[vs])
            S.op("sp", lambda e, g4=g4, vs=vs: e.dma_start(
                out=VV[4 * g4:4 * g4 + 4, :, 4 * t:4 * t + 4, :].rearrange("h p b e -> p h b e"), in_=vs[:, :, :, :]),
                reads=[vs], writes=[VV_r[4 * g4 + i][t] for i in range(4)], dma=vs)

    def b_phase1(j_, l, t, hsrc_ap, hsrc_b):
        load_h(hsrc_ap, hsrc_b, t)
        load_cs(t)
        rmsnorm("attn_norm%d" % l)
        slot, wv = load_w(("dq", j_), b_w_dq[j_], 8, 512)
        pss = []
        for c in range(4):
            ps = mmr.next()
            proj_w(slot, wv, c * 128, 8, hn, lambda kc: hn[:, kc, :], ps)
            pss.append(ps)
        st = STAT
        for c in range(4):
            sq = sqr.next()
            S.op("act", lambda e, sq=sq, c=c: e.activation(out=sq[:, :], in_=pss[c][:, :], func=AF.Square),
                 reads=[pss[c]], writes=[sq])
            S.op("pe", lambda e, sq=sq, c=c: e.matmul(st[:, :], lhsT=ones[:, :], rhs=sq[:, :], start=(c == 0), stop=(c == 3)),
                 reads=[ones, sq], writes=[st])
        rs = rstd_from(st, st[:, :], 1.0 / 512)
        for c in range(4):
            S.op("dve", lambda e, c=c: e.scalar_tensor_tensor(out=cqn[:, c, :], in0=pss[c][:, :],
                                                              scalar=C("b_cq_norm%d" % j_, c), in1=rs[:, :],
                                                              op0=ALU.mult, op1=ALU.mult),
                 reads=[pss[c], cols, rs], writes=[cqn])
        for g4 in range(4):
            slot, wv = load_w(("uqN", j_, g4), b_w_uqN[j_][:, g4 * 512:(g4 + 1) * 512], 4, 512)
            for j in range(4):
                hh = g4 * 4 + j
                ps = mmr.next()
                proj_w(slot, wv, j * 128, 4, cqn, lambda kc: cqn[:, kc, :], ps)
                sb = stg.next()
                group_norm_to(ps, C("b_q_nope_norm%d" % j_), sb, sb[:, :], 128)
                S.op("sp", lambda e, sb=sb, hh=hh: e.dma_start(out=QA[hh, :, t * TW:(t + 1) * TW], in_=sb[:, :]),
                     reads=[sb], writes=[QA_r[hh][t]], dma=sb)
        for g2 in range(2):
            slot, wv = load_w(("uqP", j_, g2), b_w_uqP[j_][:, g2 * 512:(g2 + 1) * 512], 4, 512)
            for j in range(4):
                ch = g2 * 4 + j
                ps = mmr.next()
                proj_w(slot, wv, j * 128, 4, cqn, lambda kc: cqn[:, kc, :], ps)
                sb = stg.next()
                group_norm_to(ps, C("b_q_pe_norm%d" % j_), sb, sb[:, :], 64)
                sb2 = stg.next()
                apply_rotary(sb, 128, sb2, sb2[:, :])
                S.op("sp", lambda e, sb2=sb2, ch=ch: e.dma_start(out=QP[ch, :, t * TW:(t + 1) * TW], in_=sb2[:, :]),
                     reads=[sb2], writes=[QP_r[ch][t]], dma=sb2)

    def hsrc_of(l):
        return (xT, None) if l == 0 else (hT, hT_r)

    for l in range(L):
        src_ap, src_r = hsrc_of(l)
        last = (l == L - 1)
        dst_ap, dst_r = (outT, None) if last else (hT, hT_r)
        if l < 2:
            for t in range(NT):
                a_phase1(l, t, src_ap, src_r[t] if src_r else None)
            if l == 0:
                cur_phase[0] = 1
                if wt_plan is not None:
                    convert_keys([k for k, v in wt_plan.items() if v[0] > 0])
            for h in range(8):
                attention_head("A", l, h)
            for i in range(44):
                S.op("dve", lambda e, i=i: e.memset(utail_t[:, i, :], 0.0), writes=[utail[i]])
            for t in range(NT):
                dense_tile(l, t, 8, a_w_o[l], src_ap, src_r[t] if src_r else None, dst_ap, dst_r)
        else:
            j_ = l - 2
            if l == 2:
                rotary_tables()
                for t in range(NT):
                    shared_kv_tile(t, src_ap, src_r[t])
                S.op("dve", lambda e: e.memset(attkp[64:128, :], 0.0), writes=[attkp])
                S.op("sp", lambda e: e.dma_start(out=attkp[0:64, :], in_=KPE), reads=KPE_r, writes=[attkp], dma=attkp)
                for r_ in qpring.bufs:
                    S.op("dve", lambda e, r_=r_: e.memset(r_[64:128, :], 0.0), writes=[r_])
            for t in range(NT):
                b_phase1(j_, l, t, src_ap, src_r[t])
            for h in range(16):
                attention_head("B", l, h)
            for i in range(44):
                S.op("dve", lambda e, i=i: e.memset(utail_t[:, i, :], 0.0), writes=[utail[i]])
            for t in range(NT):
                dense_tile(l, t, 16, b_w_o[j_], src_ap, src_r[t], dst_ap, dst_r)

    stats = S.emit()
    es.close()
    return nc, stats, cm, rm, wt_first


_PROG = {}


def _get_prog(SL, L):
    key = (SL, L)
    if key not in _PROG:
        plan = build_program(SL, L)[4]
        _PROG[key] = build_program(SL, L, wt_plan=plan)[:4]
    return _PROG[key]


def prepare_inputs(inp, SL, L, cm, rm):
    f = lambda a: np.ascontiguousarray(np.asarray(a, dtype=np.float32))
    B = inp["x"].shape[0]
    cols = np.zeros((128, cm.n), np.float32)
    rows = np.zeros((128, rm.n), np.float32)

    def put_vec(name, v, off=0):
        v = np.asarray(v, np.float32)
        k = v.shape[0] // 128
        cols[:, cm.m[name] + off:cm.m[name] + off + k] = v.reshape(k, 128).T

    for l in range(4):
        put_vec("attn_norm%d" % l, inp["attn_norm"][l])
        put_vec("ffn_norm%d" % l, inp["ffn_norm"][l])
        put_vec("ple_norm%d" % l, inp["ple_norm"][l])
        for j in range(3):
            put_vec("conv_w%d_%d" % (l, j), np.asarray(inp["ffn_conv_w"])[l, j])
        put_vec("conv_b%d" % l, np.asarray(inp["ffn_conv_b"])[l])
    put_vec("kv_norm", inp["kv_norm"])
    for l in range(2):
        put_vec("a_q_norm%d" % l, np.tile(np.asarray(inp["a_q_norm"])[l], 2))
        put_vec("a_k_norm%d" % l, np.tile(np.asarray(inp["a_k_norm"])[l], 2))
        put_vec("b_cq_norm%d" % l, np.asarray(inp["b_cq_norm"])[l])
        put_vec("b_q_nope_norm%d" % l, np.asarray(inp["b_q_nope_norm"])[l])
        put_vec("b_q_pe_norm%d" % l, np.tile(np.asarray(inp["b_q_pe_norm"])[l], 2))
    put_vec("ckv_norm", inp["ckv_norm"])
    put_vec("k_nope_norm", inp["k_nope_norm"])
    put_vec("k_pe_norm", np.tile(np.asarray(inp["k_pe_norm"]), 2))
    tab = np.asarray(inp["rel_bias_table"], np.float32)
    cols[:, cm.m["b31"]:cm.m["b31"] + 8] = tab[31][None, :]
    cols[:, cm.m["table"]:cm.m["table"] + 256] = tab.reshape(1, 256)
    half = 32
    invf = np.exp(-math.log(10000.0) * np.arange(half, dtype=np.float32) * np.float32(2.0 / 64)).astype(np.float32)
    cols[:, cm.m["invfreq"]] = np.tile(invf, 4)
    for l in range(2):
        for nm, key in (("q1", "a_lam_q1"), ("k1", "a_lam_k1"), ("q2", "a_lam_q2"), ("k2", "a_lam_k2")):
            r0 = rm.m["lam_%s%d" % (nm, l)]
            rows[:, r0:r0 + 64] = np.asarray(inp[key], np.float32)[l][None, :]
        r0 = rm.m["sub_gain%d" % l]
        rows[:, r0:r0 + 128] = np.asarray(inp["a_sub_norm"], np.float32)[l][None, :]
    prot = np.zeros((128, 128), np.float32)
    for g in range(2):
        for m in range(64):
            if m < 32:
                prot[g * 64 + m + 32, g * 64 + m] = -1.0
            else:
                prot[g * 64 + m - 32, g * 64 + m] = 1.0
    w_ukv = np.asarray(inp["w_ukv"], np.float32).reshape(256, 16, 256)
    w_uq = np.asarray(inp["b_w_uq"], np.float32).reshape(2, 512, 16, 192)
    shared = dict(
        cols=cols, rows=rows, prot=prot,
        a_w_qkv=f(inp["a_w_qkv"]), a_w_o=f(inp["a_w_o"]), w_dkv=f(inp["w_dkv"]),
        w_ukvK=np.ascontiguousarray(w_ukv[:, :, 0:128].reshape(256, 2048)),
        w_ukvV=np.ascontiguousarray(w_ukv[:, :, 128:256].reshape(256, 2048)),
        b_w_dq=f(inp["b_w_dq"]),
        b_w_uqN=np.ascontiguousarray(w_uq[:, :, :, 0:128].reshape(2, 512, 2048)),
        b_w_uqP=np.ascontiguousarray(w_uq[:, :, :, 128:192].reshape(2, 512, 1024)),
        b_w_o=f(inp["b_w_o"]), ffn_w_in=f(inp["ffn_w_in"]), ffn_w_out=f(inp["ffn_w_out"]),
        ple_w_proj=f(inp["ple_w_proj"]), ple_w_gate=f(inp["ple_w_gate"]),
    )
    x = np.asarray(inp["x"], np.float32)
    p = np.asarray(inp["p"], np.float32)
    pos = np.asarray(inp["positions"]).astype(np.int32)
    maps = []
    for c in range(NCORES):
        b = c % B
        m = dict(shared)
        m["xT"] = np.ascontiguousarray(x[b, :SL].T)
        m["pT"] = np.ascontiguousarray(p[:, b, :SL].transpose(0, 2, 1))
        m["posb"] = np.ascontiguousarray(np.broadcast_to(pos[b, :SL][None, :], (128, SL)))
        m["poscol"] = np.ascontiguousarray(pos[b, :256].reshape(2, 128).T)
        maps.append(m)
    return maps


def run_model(inp, SL, L):
    nc, stats, cm, rm = _get_prog(SL, L)
    maps = prepare_inputs(inp, SL, L, cm, rm)
    res = run_bass_kernel_spmd(nc, maps, core_ids=list(range(NCORES)))
    B = inp["x"].shape[0]
    out = np.stack([np.ascontiguousarray(res.results[b]["outT"].T) for b in range(B)], axis=0)
    return out.astype(np.float32)


def kernel(**inputs):
    return run_model(inputs, 4096, 4)
```

```python
import math
from contextlib import ExitStack

import numpy as np
import concourse.bass as bass
import concourse.mybir as mybir
from concourse.bass_utils import run_bass_kernel_spmd

F32 = mybir.dt.float32
BF16 = mybir.dt.bfloat16
I32 = mybir.dt.int32
AF = mybir.ActivationFunctionType
ALU = mybir.AluOpType

D = 1024
DFF = 2816
NFC = 22
TW = 512
EPS = 1e-6
NCORES = 8


class Buf:
    __slots__ = ("name", "t", "writers", "readers", "dkey")

    def __init__(self, name, t=None):
        self.name = name
        self.t = t
        self.writers = {}
        self.readers = {}
        self.dkey = None

    def __getitem__(self, idx):
        return self.t[idx]


class Op:
    __slots__ = ("q", "fn", "deps", "key", "signaled", "count", "ninc")


class Sched:
    def __init__(self, nc, es):
        self.nc = nc
        self.es = es
        self.ops = []
        self.eng = {"pe": nc.tensor, "act": nc.scalar, "dve": nc.vector,
                    "pool": nc.gpsimd, "sp": nc.sync}
        self.last_dma = {}
        self.ndkeys = 0

    def sbuf(self, name, shape, dt):
        return Buf(name, self.es.enter_context(self.nc.sbuf_tensor("sb_" + name, list(shape), dt)))

    def psum(self, name, shape, dt):
        return Buf(name, self.es.enter_context(self.nc.psum_tensor("ps_" + name, list(shape), dt)))

    def op(self, q, fn, reads=(), writes=(), dma=None, ninc=1):
        o = Op()
        o.q = q
        o.fn = fn
        o.signaled = False
        o.count = 0
        o.ninc = ninc
        isdma = dma is not None
        if isdma:
            if dma.dkey is None:
                dma.dkey = self.ndkeys
                self.ndkeys += 1
            o.key = ("dma", dma.dkey)
        else:
            o.key = q
        key = o.key
        deps = {}
        for b in reads:
            for w in b.writers.values():
                deps[id(w)] = w
        for b in writes:
            for k, r in b.readers.items():
                if k != key or isdma:
                    deps[id(r)] = r
            for k, w in b.writers.items():
                if k != key or isdma:
                    deps[id(w)] = w
        if isdma:
            prev = self.last_dma.get(key)
            if prev is not None:
                deps[id(prev)] = prev
            self.last_dma[key] = o
            o.signaled = True
        o.deps = list(deps.values())
        for d in o.deps:
            d.signaled = True
        for b in reads:
            b.readers[key] = o
        for b in writes:
            b.readers = {}
            b.writers = {key: o}
        self.ops.append(o)
        return o

    def emit(self):
        nc = self.nc
        sems = {}
        counts = {}
        waited = {}
        for o in self.ops:
            if o.signaled:
                if o.key not in sems:
                    nm = "s_" + (o.key if isinstance(o.key, str) else "d%d" % o.key[1])
                    sems[o.key] = self.es.enter_context(nc.semaphore(nm))
                    counts[o.key] = 0
                counts[o.key] += (16 * o.ninc) if not isinstance(o.key, str) else 1
                o.count = counts[o.key]
        nwaits = 0
        for o in self.ops:
            e = self.eng[o.q]
            wq = waited.setdefault(o.q, {})
            for d in o.deps:
                if wq.get(d.key, 0) < d.count:
                    e.wait_ge(sems[d.key], d.count)
                    wq[d.key] = d.count
                    nwaits += 1
            ins = o.fn(e)
            if o.signaled:
                if isinstance(o.key, str):
                    last = ins[-1] if isinstance(ins, (list, tuple)) else ins
                    last.then_inc(sems[o.key], 1)
                else:
                    lst = list(ins) if isinstance(ins, (list, tuple)) else [ins]
                    assert len(lst) == o.ninc, (len(lst), o.ninc)
                    for i in lst:
                        i.then_inc(sems[o.key], 16)
        e = self.eng["sp"]
        wq = waited.setdefault("sp", {})
        for key, s in sems.items():
            if counts[key] > wq.get(key, 0):
                e.wait_ge(s, counts[key])
        return dict(nops=len(self.ops), nwaits=nwaits, nsems=len(sems),
                    maxcount=max(counts.values()) if counts else 0)


class Ring:
    def __init__(self, bufs):
        self.bufs = bufs
        self.i = 0

    def next(self):
        b = self.bufs[self.i % len(self.bufs)]
        self.i += 1
        return b


def t5_thresholds():
    n = np.arange(0, 400)
    f = np.maximum(n, 1).astype(np.float32) / np.float32(16)
    lr = np.log(f).astype(np.float32) / np.float32(math.log(128 / 16))
    large = 16 + (lr * np.float32(16)).astype(np.int32)
    large = np.minimum(large, 31)
    bucket = np.where(n < 16, n, large)
    lo = [int(np.min(n[bucket >= b])) for b in range(32)]
    return lo


class ColMap:
    def __init__(self):
        self.n = 0
        self.m = {}

    def add(self, name, w):
        self.m[name] = self.n
        self.n += w
        return self.m[name]


def make_colmap():
    cm = ColMap()
    for l in range(4):
        cm.add("attn_norm%d" % l, 8)
        cm.add("ffn_norm%d" % l, 8)
        cm.add("ple_norm%d" % l, 8)
        for j in range(3):
            cm.add("conv_w%d_%d" % (l, j), 44)
        cm.add("conv_b%d" % l, 44)
    cm.add("kv_norm", 8)
    for l in range(2):
        cm.add("a_q_norm%d" % l, 1)
        cm.add("a_k_norm%d" % l, 1)
        cm.add("b_cq_norm%d" % l, 4)
        cm.add("b_q_nope_norm%d" % l, 1)
        cm.add("b_q_pe_norm%d" % l, 1)
    cm.add("ckv_norm", 2)
    cm.add("k_nope_norm", 1)
    cm.add("k_pe_norm", 1)
    cm.add("b31", 8)
    cm.add("table", 256)
    cm.add("invfreq", 1)
    return cm


def make_rowmap():
    rm = ColMap()
    for l in range(2):
        for nm in ("q1", "k1", "q2", "k2"):
            rm.add("lam_%s%d" % (nm, l), 64)
        rm.add("sub_gain%d" % l, 128)
    return rm


def build_program(SL, L, wt_plan=None):
    NT = SL // TW
    NB = SL // 128
    nc = bass.Bass("TRN2", target_bir_lowering=False)
    cm = make_colmap()
    rm = make_rowmap()
    lo_thr = t5_thresholds()

    def din(name, shape, dt=F32):
        return nc.dram_tensor(name, list(shape), dt, kind="ExternalInput").ap()

    def dscr(name, shape, dt):
        return nc.dram_tensor(name, list(shape), dt, kind="Internal").ap()

    xT = din("xT", [D, SL])
    pT = din("pT", [4, 256, SL])
    posb = din("posb", [128, SL], I32)
    poscol = din("poscol", [128, 2], I32)
    colsd = din("cols", [128, cm.n])
    rowsd = din("rows", [128, rm.n])
    protd = din("prot", [128, 128])
    a_w_qkv = din("a_w_qkv", [2, D, 3072])
    a_w_o = din("a_w_o", [2, D, D])
    w_dkv = din("w_dkv", [D, 320])
    w_ukvK = din("w_ukvK", [256, 2048])
    w_ukvV = din("w_ukvV", [256, 2048])
    b_w_dq = din("b_w_dq", [2, D, 512])
    b_w_uqN = din("b_w_uqN", [2, 512, 2048])
    b_w_uqP = din("b_w_uqP", [2, 512, 1024])
    b_w_o = din("b_w_o", [2, 2048, D])
    ffn_w_in = din("ffn_w_in", [4, D, 2 * DFF])
    ffn_w_out = din("ffn_w_out", [4, DFF, D])
    ple_w_proj = din("ple_w_proj", [4, 256, D])
    ple_w_gate = din("ple_w_gate", [4, D, D])
    outT = nc.dram_tensor("outT", [D, SL], F32, kind="ExternalOutput").ap()

    hT = dscr("hT", [D, SL], F32)
    QA = dscr("QA", [16, 128, SL], BF16)
    KA = dscr("KA", [16, 128, SL], BF16)
    QP = dscr("QP", [8, 128, SL], BF16)
    KPE = dscr("KPE", [64, SL], BF16)
    VV = dscr("VV", [16, 128, NB, 129], BF16)
    OT = dscr("OT", [16, 128, SL], BF16)
    NWT = max(1, sum(1 for v in wt_plan.values() if v[0] > 0)) if wt_plan is not None else 1
    WT = dscr("WT", [NWT, 128, 4096], BF16)
    COS = dscr("COS", [128, SL], F32)
    SIN = dscr("SIN", [128, SL], F32)

    es = ExitStack()
    S = Sched(nc, es)

    def regions(name, n1):
        return [[Buf("%s_%d_%d" % (name, i, t)) for t in range(NT)] for i in range(n1)]

    hT_r = [[Buf("hT_%d_%d" % (t, k)) for k in range(8)] for t in range(NT)]
    QA_r = regions("QA", 16)
    KA_r = regions("KA", 16)
    QP_r = regions("QP", 8)
    KPE_r = [Buf("KPE_%d" % t) for t in range(NT)]
    VV_r = regions("VV", 16)
    OT_r = regions("OT", 16)
    CS_r = [Buf("CS_%d" % t) for t in range(NT)]

    cols = S.sbuf("cols", [128, cm.n], F32)
    rows = S.sbuf("rows", [128, rm.n], F32)
    ones = S.sbuf("ones", [128, 128], BF16)
    bones = S.sbuf("bones", [128, 128], BF16)
    ident = S.sbuf("ident", [128, 128], F32)
    prot = S.sbuf("prot", [128, 128], BF16)
    tri = S.sbuf("tri", [128, 128], F32)
    epsc = S.sbuf("epsc", [128, 1], F32)
    Rt = S.sbuf("Rt", [128, 2, 8, 128], F32)
    lamt = S.sbuf("lamt", [128, 8], F32)
    subg = S.sbuf("subg", [128, 2, 128], F32)
    wslots = [S.sbuf("w%d" % i, [128, 4096], BF16) for i in range(4)]
    wring = Ring(wslots)
    h_sb = S.sbuf("h_sb", [128, 8, TW], F32)
    h_c = [Buf("h_c%d" % i, h_sb.t) for i in range(8)]
    hn = S.sbuf("hn", [128, 8, TW], BF16)
    sqr = Ring([S.sbuf("sq%d" % i, [128, TW], BF16) for i in range(3)])
    lnr = Ring([S.sbuf("ln%d" % i, [128, TW], F32) for i in range(2)])
    rsr = Ring([S.sbuf("rs%d" % i, [128, TW], F32) for i in range(2)])
    stg = Ring([S.sbuf("stg%d" % i, [128, TW], BF16) for i in range(4)])
    vst = Ring([S.sbuf("vst%d" % i, [128, 4, 4, 129], BF16) for i in range(2)])
    qring = Ring([S.sbuf("attq%d" % i, [128, TW], BF16) for i in range(2)])
    qpring = Ring([S.sbuf("attqp%d" % i, [128, TW], BF16) for i in range(2)])
    attk = S.sbuf("attk", [128, SL], BF16)
    attv = S.sbuf("attv", [128, NB, 129], BF16)
    attkp = S.sbuf("attkp", [128, SL], BF16)
    ppool = Ring([S.sbuf("pt%d" % i, [128, TW], BF16) for i in range(6)])
    otst = Ring([S.sbuf("otst%d" % i, [128, TW], BF16) for i in range(2)])
    o_sb = Ring([S.sbuf("o_sb%d" % i, [128, 128], F32) for i in range(4)])
    on_sb = Ring([S.sbuf("on_sb%d" % i, [128, 128], F32) for i in range(4)])
    smr = Ring([S.sbuf("sm%d" % i, [128, 8], F32) for i in range(4)])
    junk = S.sbuf("junk", [128, 128], F32)
    ot_sb = S.sbuf("ot_sb", [128, 8, TW], BF16)
    z_sb = S.sbuf("z_sb", [128, NFC, TW], BF16)
    uext = Ring([S.sbuf("uext%d" % i, [128, TW + 2], F32) for i in range(4)])
    cpool = Ring([S.sbuf("c%d" % i, [128, TW], F32) for i in range(4)])
    sgp = Ring([S.sbuf("sg%d" % i, [128, TW], F32) for i in range(2)])
    utail_t = es.enter_context(nc.sbuf_tensor("sb_utail", [128, 44, 2], F32))
    utail = [Buf("utail%d" % i, utail_t) for i in range(44)]
    pt_sb = S.sbuf("pt_sb", [128, 2, TW], BF16)
    cs_sb = S.sbuf("cs_sb", [128, 2, TW], F32)
    cqn = S.sbuf("cqn", [128, 4, TW], BF16)
    ckvn = S.sbuf("ckvn", [128, 2, TW], BF16)
    itmp = S.sbuf("itmp", [128, TW], I32)
    itmp2 = S.sbuf("itmp2", [128, TW], I32)
    pb = [S.psum("pb%d" % i, [128, TW], F32) for i in range(8)]

    def C(name, off=0, w=1):
        c0 = cm.m[name] + off
        return cols[:, c0:c0 + w]

    def dve_tt(out_b, out_ap, a_b, a_ap, b_b, b_ap, op, q="dve"):
        S.op(q, lambda e: e.tensor_tensor(out=out_ap, in0=a_ap, in1=b_ap, op=op),
             reads=[a_b, b_b], writes=[out_b])

    wt_first = {}
    wt_tid = {}
    wt_bufs = {}
    cur_phase = [0]

    def _cast_load(slot, srcs, kc, ns, gate=()):
        n = sum(ns)
        view = slot[:, 0:kc * n].rearrange("p (k n) -> p k n", k=kc)
        svs = [a.rearrange("(k p) n -> p k n", p=128) for a in srcs]
        offs = [sum(ns[:i]) for i in range(len(ns))]
        S.op("pool", lambda e: [e.dma_start(out=view[:, :, offs[i]:offs[i] + ns[i]], in_=svs[i]) for i in range(len(ns))],
             reads=list(gate), writes=[slot], dma=slot, ninc=len(ns))
        return view

    def src_for_key(key):
        k0 = key[0]
        if k0 == "wo":
            _, l, half, mg = key
            w = a_w_o[l] if l < 2 else b_w_o[l - 2]
            return [w[half * 1024:(half + 1) * 1024, mg * 512:(mg + 1) * 512]], 8, [512]
        if k0 == "win":
            _, l, ig = key
            w = ffn_w_in[l]
            return [w[:, ig * 128:(ig + 2) * 128], w[:, DFF + ig * 128:DFF + (ig + 2) * 128]], 8, [256, 256]
        if k0 == "wout":
            _, l, m = key
            return [ffn_w_out[l][:, m * 128:(m + 1) * 128]], NFC, [128]
        if k0 == "pproj":
            return [ple_w_proj[key[1]]], 2, [1024]
        if k0 == "pgate":
            _, l, mg = key
            return [ple_w_gate[l][:, mg * 512:(mg + 1) * 512]], 8, [512]
        if k0 == "qkv":
            _, l, grp = key
            return [a_w_qkv[l][:, grp * 512:(grp + 1) * 512]], 8, [512]
        if k0 == "qkvv":
            _, l, half = key
            return [a_w_qkv[l][:, 2048 + half * 512:2048 + (half + 1) * 512]], 8, [512]
        if k0 == "dkv":
            return [w_dkv], 8, [320]
        if k0 == "ukvK":
            return [w_ukvK[:, key[1] * 512:(key[1] + 1) * 512]], 2, [512]
        if k0 == "ukvV":
            return [w_ukvV[:, key[1] * 512:(key[1] + 1) * 512]], 2, [512]
        if k0 == "dq":
            return [b_w_dq[key[1]]], 8, [512]
        if k0 == "uqN":
            return [b_w_uqN[key[1]][:, key[2] * 512:(key[2] + 1) * 512]], 4, [512]
        if k0 == "uqP":
            return [b_w_uqP[key[1]][:, key[2] * 512:(key[2] + 1) * 512]], 4, [512]
        raise KeyError(key)

    if wt_plan is not None:
        for key_, (ph_, _) in wt_plan.items():
            if ph_ > 0:
                wt_tid[key_] = len(wt_tid)
                wt_bufs[key_] = Buf("wt_%d" % wt_tid[key_])

    def convert_keys(keys, gate=()):
        for ki, key in enumerate(keys):
            srcs, kc, ns = src_for_key(key)
            tid = wt_tid[key]
            slot = wring.next()
            _cast_load(slot, srcs, kc, ns, gate=gate if ki == 0 else ())
            n = kc * sum(ns)
            S.op("sp", lambda e, slot=slot, tid=tid, n=n: e.dma_start(out=WT[tid, :, 0:n], in_=slot[:, 0:n]),
                 reads=[slot], writes=[wt_bufs[key]], dma=slot)

    srcs_of = {}

    def _load(key, srcs, kc, ns):
        n = sum(ns)
        if key not in wt_first:
            wt_first[key] = (cur_phase[0], (None, kc, ns))
        srcs_of.setdefault(key, srcs)
        slot = wring.next()
        if wt_plan is None or key not in wt_tid:
            view = _cast_load(slot, srcs, kc, ns)
        else:
            tid = wt_tid[key]
            view = slot[:, 0:kc * n].rearrange("p (k n) -> p k n", k=kc)
            S.op("pool", lambda e: e.dma_start(out=slot[:, 0:kc * n], in_=WT[tid, :, 0:kc * n]),
                 reads=[wt_bufs[key]], writes=[slot], dma=slot)
        return slot, view

    def load_w(key, src_ap, kc, n):
        return _load(key, [src_ap], kc, [n])

    def load_w2(key, src_a, src_b, kc, na, nb_):
        return _load(key, [src_a, src_b], kc, [na, nb_])

    mmr = Ring(pb[0:5])
    str_ = Ring(pb[5:7])
    STAT = pb[7]

    def rstd_from(stat_b, stat_ap, inv_n, width=TW, parts=128):
        ln = lnr.next()
        rs = rsr.next()
        S.op("act", lambda e: e.activation(out=ln[0:parts, 0:width], in_=stat_ap, func=AF.Ln,
                                           bias=epsc[0:parts, :], scale=inv_n),
             reads=[stat_b, epsc], writes=[ln])
        S.op("act", lambda e: e.activation(out=rs[0:parts, 0:width], in_=ln[0:parts, 0:width],
                                           func=AF.Exp, scale=-0.5),
             reads=[ln], writes=[rs])
        return rs

    def rmsnorm(gain_name):
        st = STAT
        for kc in range(8):
            sq = sqr.next()
            S.op("act", lambda e, kc=kc, sq=sq: e.activation(out=sq[:, :], in_=h_sb[:, kc, :], func=AF.Square),
                 reads=[h_c[kc]], writes=[sq])
            S.op("pe", lambda e, kc=kc, sq=sq: e.matmul(st[:, :], lhsT=ones[:, :], rhs=sq[:, :],
                                                        start=(kc == 0), stop=(kc == 7)),
                 reads=[ones, sq], writes=[st])
        rs = rstd_from(st, st[:, :], 1.0 / D)
        for kc in range(8):
            S.op("dve", lambda e, kc=kc: e.scalar_tensor_tensor(
                out=hn[:, kc, :], in0=h_sb[:, kc, :], scalar=C(gain_name, kc), in1=rs[:, :],
                op0=ALU.mult, op1=ALU.mult), reads=[h_c[kc], cols, rs], writes=[hn])

    def load_h(src_ap, src_buf, t):
        for kc in range(8):
            S.op("sp", lambda e, kc=kc: e.dma_start(out=h_sb[:, kc, :],
                                                    in_=src_ap[kc * 128:(kc + 1) * 128, t * TW:(t + 1) * TW]),
                 reads=[src_buf[kc]] if src_buf is not None else [], writes=[h_c[kc]], dma=h_c[kc])

    def proj_fm(wview, j0, kcn, rhs_b, rhs_of_kc, ps):
        for kc in range(kcn):
            S.op("pe", lambda e, kc=kc: e.matmul(ps[:, :], lhsT=wview[:, kc, j0:j0 + 128], rhs=rhs_of_kc(kc),
                                                 start=(kc == 0), stop=(kc == kcn - 1)),
                 reads=[rhs_b], writes=[ps])

    def group_norm_to(ps, gain_ap, out_b, out_ap, group, parts=128):
        sq = sqr.next()
        S.op("act", lambda e: e.activation(out=sq[0:parts, :], in_=ps[0:parts, :], func=AF.Square),
             reads=[ps], writes=[sq])
        st = str_.next()
        lhs = bones if group == 64 else ones
        S.op("pe", lambda e: e.matmul(st[0:parts, :], lhsT=lhs[0:parts, 0:parts], rhs=sq[0:parts, :], start=True, stop=True),
             reads=[lhs, sq], writes=[st])
        rs = rstd_from(st, st[0:parts, :], 1.0 / group, parts=parts)
        S.op("dve", lambda e: e.scalar_tensor_tensor(out=out_ap, in0=ps[0:parts, :], scalar=gain_ap, in1=rs[0:parts, :],
                                                     op0=ALU.mult, op1=ALU.mult),
             reads=[ps, cols, rs], writes=[out_b])

    S.op("sp", lambda e: e.dma_start(out=cols[:, :], in_=colsd), writes=[cols], dma=cols)
    S.op("sp", lambda e: e.dma_start(out=rows[:, :], in_=rowsd), writes=[rows], dma=rows)
    S.op("pool", lambda e: e.dma_start(out=prot[:, :], in_=protd), writes=[prot], dma=prot)
    S.op("dve", lambda e: e.memset(ones[:, :], 1.0), writes=[ones])
    S.op("dve", lambda e: e.memset(bones[:, :], 0.0), writes=[bones])
    S.op("dve", lambda e: e.memset(bones[0:64, 0:64], 1.0), writes=[bones])
    S.op("dve", lambda e: e.memset(bones[64:128, 64:128], 1.0), writes=[bones])
    S.op("dve", lambda e: e.memset(epsc[:, :], EPS), writes=[epsc])
    S.op("pool", lambda e: e.memset(ident[:, :], 0.0), writes=[ident])
    S.op("pool", lambda e: e.affine_select(out=ident[:, :], in_=ident[:, :], pattern=[[-1, 128]],
                                           compare_op=ALU.not_equal, fill=1.0, base=0, channel_multiplier=1),
         reads=[ident], writes=[ident])
    S.op("pool", lambda e: e.memset(tri[:, :], 1.0), writes=[tri])
    S.op("pool", lambda e: e.affine_select(out=tri[:, :], in_=tri[:, :], pattern=[[1, 128]],
                                           compare_op=ALU.is_ge, fill=0.0, base=0, channel_multiplier=-1),
         reads=[tri], writes=[tri])
    for r_ in vst.bufs:
        S.op("dve", lambda e, r_=r_: e.memset(r_[:, :, :, :], 1.0), writes=[r_])

    nA = min(L, 2)
    for r_ in qring.bufs:
        S.op("dve", lambda e, r_=r_: e.memset(r_[64:128, :], 0.0), writes=[r_])
    for r_ in qpring.bufs:
        S.op("dve", lambda e, r_=r_: e.memset(r_[0:64, :], 0.0), writes=[r_])

    if nA > 0:
        class V3:
            def __init__(self, b):
                self.b = b

            def __getitem__(self, idx):
                return self.b[:, 0:256].rearrange("p (d q) -> p d q", d=2)[idx]

        posi = itmp
        pci = S.sbuf("pci", [128, 2], I32)
        posf = sgp.bufs[0]
        pcf = S.sbuf("pcf", [128, 2], F32)
        dt_b = cpool.bufs[0]
        dt_ = V3(dt_b)
        ge0_b = cpool.bufs[1]
        ge0 = V3(ge0_b)
        ge_bufs = [cpool.bufs[2], cpool.bufs[3]]
        ge = Ring([V3(b) for b in ge_bufs])
        dtab = S.sbuf("dtab", [128, 256], F32)
        nb31 = S.sbuf("nb31", [128, 8], F32)
        S.op("sp", lambda e: e.dma_start(out=posi[:, 0:256], in_=posb[:, 0:256]), writes=[posi], dma=posi)
        S.op("sp", lambda e: e.dma_start(out=pci[:, :], in_=poscol), writes=[pci], dma=pci)
        S.op("dve", lambda e: e.tensor_copy(out=posf[:, 0:256], in_=posi[:, 0:256]), reads=[posi], writes=[posf])
        S.op("dve", lambda e: e.tensor_copy(out=pcf[:, :], in_=pci[:, :]), reads=[pci], writes=[pcf])
        S.op("dve", lambda e: e.tensor_scalar(out=dt_[:, :, :], in0=posf[:, 0:256].rearrange("p (d q) -> p d q", d=2),
                                              scalar1=pcf[:, 0:1], scalar2=None, op0=ALU.subtract),
             reads=[posf, pcf], writes=[dt_b])
        tb = cm.m["table"]
        S.op("dve", lambda e: e.tensor_copy(out=dtab[:, 0:8], in_=cols[:, tb:tb + 8]), reads=[cols], writes=[dtab])
        S.op("dve", lambda e: e.tensor_tensor(out=dtab[:, 8:256], in0=cols[:, tb + 8:tb + 256], in1=cols[:, tb:tb + 248],
                                              op=ALU.subtract), reads=[cols], writes=[dtab])
        S.op("dve", lambda e: e.tensor_scalar(out=nb31[:, :], in0=C("b31", 0, 8), scalar1=-1.0, scalar2=None, op0=ALU.mult),
             reads=[cols], writes=[nb31])
        S.op("dve", lambda e: e.tensor_scalar(out=ge0[:, :, :], in0=dt_[:, :, :], scalar1=0.0, scalar2=None, op0=ALU.is_ge),
             reads=[dt_b], writes=[ge0_b])
        for h in range(8):
            S.op("dve", lambda e, h=h: e.tensor_scalar(out=Rt[:, :, h, :], in0=ge0[:, :, :], scalar1=dtab[:, h:h + 1],
                                                       scalar2=None, op0=ALU.mult), reads=[ge0_b, dtab], writes=[Rt])
        for b in range(1, 32):
            g = ge.next()
            S.op("dve", lambda e, g=g, b=b: e.tensor_scalar(out=g[:, :, :], in0=dt_[:, :, :], scalar1=float(lo_thr[b]),
                                                            scalar2=None, op0=ALU.is_ge), reads=[dt_b], writes=[g.b])
            for h in range(8):
                S.op("dve", lambda e, g=g, b=b, h=h: e.scalar_tensor_tensor(
                    out=Rt[:, :, h, :], in0=g[:, :, :], scalar=dtab[:, b * 8 + h:b * 8 + h + 1], in1=Rt[:, :, h, :],
                    op0=ALU.mult, op1=ALU.add), reads=[g.b, dtab, Rt], writes=[Rt])
        for h in range(8):
            S.op("act", lambda e, h=h: e.activation(out=Rt[:, :, h, :], in_=Rt[:, :, h, :], func=AF.Exp,
                                                    bias=nb31[:, h:h + 1], scale=1.0), reads=[Rt, nb31], writes=[Rt])
            S.op("dve", lambda e, h=h: e.tensor_tensor(out=Rt[:, :, h, :], in0=Rt[:, :, h, :], in1=ge0[:, :, :], op=ALU.mult),
                 reads=[Rt, ge0_b], writes=[Rt])
        for l in range(nA):
            lam_init = 0.8 - 0.6 * math.exp(-0.3 * l)
            sm = smr.next()
            for i, (a, b) in enumerate((("q1", "k1"), ("q2", "k2"))):
                ra = rm.m["lam_%s%d" % (a, l)]
                rb = rm.m["lam_%s%d" % (b, l)]
                S.op("dve", lambda e, ra=ra, rb=rb: e.tensor_tensor(out=junk[:, 0:64], in0=rows[:, ra:ra + 64],
                                                                    in1=rows[:, rb:rb + 64], op=ALU.mult),
                     reads=[rows], writes=[junk])
                S.op("act", lambda e, i=i, sm=sm: e.activation(out=junk[:, 64:128], in_=junk[:, 0:64], func=AF.Identity,
                                                               accum_out=sm[:, i:i + 1]), reads=[junk], writes=[junk, sm])
            S.op("act", lambda e, sm=sm: e.activation(out=sm[:, 2:4], in_=sm[:, 0:2], func=AF.Exp), reads=[sm], writes=[sm])
            S.op("dve", lambda e, sm=sm, l=l: e.tensor_tensor(out=lamt[:, 2 * l:2 * l + 1], in0=sm[:, 3:4], in1=sm[:, 2:3],
                                                              op=ALU.subtract), reads=[sm], writes=[lamt])
            S.op("dve", lambda e, l=l, lam_init=lam_init: e.tensor_scalar(
                out=lamt[:, 2 * l:2 * l + 1], in0=lamt[:, 2 * l:2 * l + 1], scalar1=-lam_init, scalar2=None, op0=ALU.add),
                reads=[lamt], writes=[lamt])
            sg0 = rm.m["sub_gain%d" % l]
            S.op("dve", lambda e, l=l, sg0=sg0, lam_init=lam_init: e.tensor_scalar(
                out=subg[:, l, :], in0=rows[:, sg0:sg0 + 128], scalar1=1.0 - lam_init, scalar2=None, op0=ALU.mult),
                reads=[rows], writes=[subg])

    ST = [[pb[0], pb[1]], [pb[2], pb[3]]]
    ACC = [pb[4], pb[5], pb[6]]
    TP = pb[7]

    def attention_head(kind, l, h):
        nsub = 2 if kind == "A" else 1
        sc = 0.125 if kind == "A" else (192.0 ** -0.5)
        S.op("sp", lambda e: e.dma_start(out=attk[:, :], in_=KA[h]), reads=[KA_r[h][t] for t in range(NT)],
             writes=[attk], dma=attk)
        S.op("sp", lambda e: e.dma_start(out=attv[:, :, :], in_=VV[h]), reads=[VV_r[h][t] for t in range(NT)],
             writes=[attv], dma=attv)
        steps = [(t, j) for t in range(NT) for j in range(4 * t + 4)]
        qtiles = {}
        state = {"touched": set()}

        def load_q(t):
            attq = qring.next()
            attqp = qpring.next()
            if kind == "A":
                S.op("sp", lambda e, attq=attq, t=t: e.dma_start(out=attq[0:64, :], in_=QA[h, 0:64, t * TW:(t + 1) * TW]),
                     reads=[QA_r[h][t]], writes=[attq], dma=attq)
                S.op("sp", lambda e, attqp=attqp, t=t: e.dma_start(out=attqp[64:128, :], in_=QA[h, 64:128, t * TW:(t + 1) * TW]),
                     reads=[QA_r[h][t]], writes=[attqp], dma=attqp)
            else:
                S.op("sp", lambda e, attq=attq, t=t: e.dma_start(out=attq[:, :], in_=QA[h, :, t * TW:(t + 1) * TW]),
                     reads=[QA_r[h][t]], writes=[attq], dma=attq)
                r0 = 64 * (h % 2)
                S.op("sp", lambda e, attqp=attqp, t=t, r0=r0: e.dma_start(
                    out=attqp[0:64, :], in_=QP[h // 2, r0:r0 + 64, t * TW:(t + 1) * TW]),
                    reads=[QP_r[h // 2][t]], writes=[attqp], dma=attqp)
            qtiles[t] = (attq, attqp)

        def scores(i):
            t, j = steps[i]
            if j == 0 and t + 1 < NT:
                load_q(t + 1)
            attq, attqp = qtiles[t]
            b0 = max(0, j - 4 * t)
            n = TW - 128 * b0
            q0 = 128 * b0
            par = i % 2
            for c in range(nsub):
                st = ST[c][par]
                if kind == "A":
                    qq = attq if c == 0 else attqp
                    S.op("pe", lambda e, st=st, j=j, n=n, q0=q0, qq=qq: e.matmul(
                        st[:, 0:n], lhsT=attk[:, j * 128:(j + 1) * 128], rhs=qq[:, q0:q0 + n], start=True, stop=True),
                        reads=[attk, qq], writes=[st])
                else:
                    S.op("pe", lambda e, st=st, j=j, n=n, q0=q0, attq=attq: e.matmul(
                        st[:, 0:n], lhsT=attk[:, j * 128:(j + 1) * 128], rhs=attq[:, q0:q0 + n],
                        start=True, stop=False), reads=[attk, attq], writes=[st])
                    S.op("pe", lambda e, st=st, j=j, n=n, q0=q0, attqp=attqp: e.matmul(
                        st[:, 0:n], lhsT=attkp[:, j * 128:(j + 1) * 128], rhs=attqp[:, q0:q0 + n],
                        start=False, stop=True), reads=[attkp, attqp], writes=[st])

        def probs_pv(i):
            t, j = steps[i]
            touched = state["touched"]
            if j == 0:
                touched.clear()
            b0 = max(0, j - 4 * t)
            n = TW - 128 * b0
            par = i % 2
            pts = []
            for c in range(nsub):
                st = ST[c][par]
                pt = ppool.next()
                pts.append(pt)
                if kind == "A":
                    S.op("act", lambda e, pt=pt, st=st, n=n: e.activation(
                        out=pt[:, 0:n], in_=st[:, 0:n], func=AF.Exp, bias=C("b31", h), scale=sc),
                        reads=[st, cols], writes=[pt])
                else:
                    S.op("act", lambda e, pt=pt, st=st, n=n: e.activation(
                        out=pt[:, 0:n], in_=st[:, 0:n], func=AF.Exp, scale=sc), reads=[st], writes=[pt])
                for blk in range(b0, 4):
                    dd = 4 * t + blk - j
                    cs = slice((blk - b0) * 128, (blk - b0 + 1) * 128)
                    if kind == "A" and dd in (0, 1):
                        S.op("dve", lambda e, pt=pt, cs=cs, dd=dd: e.tensor_tensor(
                            out=pt[:, cs], in0=pt[:, cs], in1=Rt[:, dd, h, :], op=ALU.mult),
                            reads=[pt, Rt], writes=[pt])
                    elif kind == "B" and dd == 0:
                        S.op("dve", lambda e, pt=pt, cs=cs: e.tensor_tensor(
                            out=pt[:, cs], in0=pt[:, cs], in1=tri[:, :], op=ALU.mult),
                            reads=[pt, tri], writes=[pt])
            for c in range(nsub):
                pt = pts[c]
                for blk in range(b0, 4):
                    idx = c * 4 + blk
                    bank = ACC[idx // 3]
                    off = (idx % 3) * 129
                    first = (idx // 3) not in touched
                    touched.add(idx // 3)
                    cs = slice((blk - b0) * 128, (blk - b0 + 1) * 128)
                    S.op("pe", lambda e, pt=pt, cs=cs, bank=bank, off=off, first=first, j=j, t=t, blk=blk: e.matmul(
                        bank[:, off:off + 129], lhsT=pt[:, cs], rhs=attv[:, j, :], start=first,
                        stop=(j == 4 * t + blk), skip_group_check=True), reads=[pt, attv], writes=[bank])

        def finalize(t):
            ost = otst.next()
            sms = [smr.next() for _ in range(4)]
            osbs = [o_sb.next() for _ in range(4)]
            acc0 = [(ACC[blk // 3], (blk % 3) * 129) for blk in range(4)]
            for blk in range(4):
                sm, (a0b, a0o) = sms[blk], acc0[blk]
                S.op("dve", lambda e, sm=sm, a0b=a0b, a0o=a0o: e.reciprocal(out=sm[:, 0:1], in_=a0b[:, a0o + 128:a0o + 129]),
                     reads=[a0b], writes=[sm])
            for blk in range(4):
                sm, osb, (a0b, a0o) = sms[blk], osbs[blk], acc0[blk]
                S.op("dve", lambda e, sm=sm, a0b=a0b, a0o=a0o, osb=osb: e.tensor_scalar(
                    out=osb[:, :], in0=a0b[:, a0o:a0o + 128], scalar1=sm[:, 0:1], scalar2=None, op0=ALU.mult),
                    reads=[a0b, sm], writes=[osb])
            srcs = osbs
            if kind == "A":
                onbs = [on_sb.next() for _ in range(4)]
                acc1 = [(ACC[(4 + blk) // 3], ((4 + blk) % 3) * 129) for blk in range(4)]
                for blk in range(4):
                    sm, (a1b, a1o) = sms[blk], acc1[blk]
                    S.op("dve", lambda e, sm=sm, a1b=a1b, a1o=a1o: e.reciprocal(out=sm[:, 1:2], in_=a1b[:, a1o + 128:a1o + 129]),
                         reads=[a1b], writes=[sm])
                for blk in range(4):
                    sm = sms[blk]
                    S.op("dve", lambda e, sm=sm: e.tensor_tensor(out=sm[:, 1:2], in0=sm[:, 1:2], in1=lamt[:, 2 * l:2 * l + 1],
                                                                 op=ALU.mult), reads=[sm, lamt], writes=[sm])
                for blk in range(4):
                    sm, osb, (a1b, a1o) = sms[blk], osbs[blk], acc1[blk]
                    S.op("dve", lambda e, sm=sm, a1b=a1b, a1o=a1o, osb=osb: e.scalar_tensor_tensor(
                        out=osb[:, :], in0=a1b[:, a1o:a1o + 128], scalar=sm[:, 1:2], in1=osb[:, :],
                        op0=ALU.mult, op1=ALU.add), reads=[a1b, sm, osb], writes=[osb])
                for blk in range(4):
                    sm, osb = sms[blk], osbs[blk]
                    S.op("act", lambda e, sm=sm, osb=osb: e.activation(out=junk[:, :], in_=osb[:, :], func=AF.Square,
                                                                       accum_out=sm[:, 2:3]), reads=[osb], writes=[junk, sm])
                for blk in range(4):
                    sm = sms[blk]
                    S.op("act", lambda e, sm=sm: e.activation(out=sm[:, 3:4], in_=sm[:, 2:3], func=AF.Ln, bias=epsc[:, :],
                                                              scale=1.0 / 128), reads=[sm, epsc], writes=[sm])
                for blk in range(4):
                    sm = sms[blk]
                    S.op("act", lambda e, sm=sm: e.activation(out=sm[:, 4:5], in_=sm[:, 3:4], func=AF.Exp, scale=-0.5),
                         reads=[sm], writes=[sm])
                for blk in range(4):
                    sm, osb, onb = sms[blk], osbs[blk], onbs[blk]
                    S.op("dve", lambda e, sm=sm, osb=osb, onb=onb: e.scalar_tensor_tensor(
                        out=onb[:, :], in0=osb[:, :], scalar=sm[:, 4:5], in1=subg[:, l, :], op0=ALU.mult, op1=ALU.mult),
                        reads=[osb, sm, subg], writes=[onb])
                srcs = onbs
            for blk in range(4):
                src = srcs[blk]
                S.op("pe", lambda e, src=src, blk=blk: e.transpose(TP[:, blk * 128:(blk + 1) * 128], src[:, :], ident[:, :]),
                     reads=[src, ident], writes=[TP])
            S.op("act", lambda e, ost=ost: e.activation(out=ost[:, :], in_=TP[:, :], func=AF.Identity), reads=[TP], writes=[ost])
            S.op("sp", lambda e, ost=ost, t=t: e.dma_start(out=OT[h, :, t * TW:(t + 1) * TW], in_=ost[:, :]),
                 reads=[ost], writes=[OT_r[h][t]], dma=ost)

        load_q(0)
        scores(0)
        for i in range(len(steps)):
            if i + 1 < len(steps):
                scores(i + 1)
            probs_pv(i)
            t, j = steps[i]
            if j == 4 * t + 3:
                finalize(t)

    ot_pref = set()
    def dense_tile(l, t, H, w_o_ap, hsrc_ap, hsrc_b, hdst_ap, hdst_b):
        def load_ot(tt, half):
            S.op("sp", lambda e: e.dma_start(
                out=ot_sb[:, :, :], in_=OT[8 * half:8 * half + 8, :, tt * TW:(tt + 1) * TW].rearrange("h p s -> p h s")),
                reads=[OT_r[8 * half + i][tt] for i in range(8)], writes=[ot_sb], dma=ot_sb)

        hdst_cb = hdst_b[t] if hdst_b is not None else [Buf("odst_%d_%d_%d" % (l, t, k)) for k in range(8)]
        if (l, t) not in ot_pref:
            load_ot(t, 0)
        load_h(hsrc_ap, hsrc_b, t)
        for half in range(H // 8):
            if half > 0:
                load_ot(t, half)
            for mg in range(2):
                slot, wv = load_w(("wo", l, half, mg), w_o_ap[half * 1024:(half + 1) * 1024, mg * 512:(mg + 1) * 512], 8, 512)
                for mj in range(4):
                    m = mg * 4 + mj
                    ps = mmr.next()
                    for hh in range(8):
                        S.op("pe", lambda e, hh=hh, mj=mj, ps=ps, wv=wv: e.matmul(
                            ps[:, :], lhsT=wv[:, hh, mj * 128:(mj + 1) * 128], rhs=ot_sb[:, hh, :],
                            start=(hh == 0), stop=(hh == 7)), reads=[slot, ot_sb], writes=[ps])
                    S.op("dve", lambda e, m=m, ps=ps: e.tensor_tensor(out=h_sb[:, m, :], in0=ps[:, :], in1=h_sb[:, m, :],
                                                                      op=ALU.add), reads=[ps, h_c[m]], writes=[h_c[m]])
        rmsnorm("ffn_norm%d" % l)
        w_in = ffn_w_in[l]
        for ig in range(0, NFC, 2):
            slot, wv = load_w2(("win", l, ig), w_in[:, ig * 128:(ig + 2) * 128], w_in[:, DFF + ig * 128:DFF + (ig + 2) * 128], 8, 256, 256)
            for ii in range(2):
                i = ig + ii
                pss, uxs, cbs, ccs = [], [], [], []
                for part in range(2):
                    cc = part * NFC + i
                    ps = mmr.next()
                    proj_w(slot, wv, part * 256 + ii * 128, 8, hn, lambda kc: hn[:, kc, :], ps)
                    pss.append(ps)
                    ccs.append(cc)
                for part in range(2):
                    ux = uext.next()
                    uxs.append(ux)
                    S.op("act", lambda e, ux=ux, ps=pss[part]: e.activation(out=ux[:, 2:TW + 2], in_=ps[:, :], func=AF.Identity),
                         reads=[pss[part]], writes=[ux])
                for part in range(2):
                    ux, cc = uxs[part], ccs[part]
                    S.op("dve", lambda e, ux=ux, cc=cc: e.tensor_copy(out=ux[:, 0:2], in_=utail_t[:, cc, :]),
                         reads=[utail[cc]], writes=[ux])
                for part in range(2):
                    ux, cc = uxs[part], ccs[part]
                    S.op("dve", lambda e, ux=ux, cc=cc: e.tensor_copy(out=utail_t[:, cc, :], in_=ux[:, TW:TW + 2]),
                         reads=[ux], writes=[utail[cc]])
                for part in range(2):
                    cc = ccs[part]
                    cb = cpool.next()
                    cbs.append(cb)
                    S.op("act", lambda e, cb=cb, cc=cc, ps=pss[part]: e.activation(
                        out=cb[:, :], in_=ps[:, :], func=AF.Identity, scale=C("conv_w%d_2" % l, cc), bias=C("conv_b%d" % l, cc)),
                        reads=[pss[part], cols], writes=[cb])
                for jj in (1, 0):
                    for part in range(2):
                        ux, cc, cb = uxs[part], ccs[part], cbs[part]
                        S.op("dve", lambda e, ux=ux, cb=cb, cc=cc, jj=jj: e.scalar_tensor_tensor(
                            out=cb[:, :], in0=ux[:, jj:TW + jj], scalar=C("conv_w%d_%d" % (l, jj), cc), in1=cb[:, :],
                            op0=ALU.mult, op1=ALU.add), reads=[ux, cols, cb], writes=[cb])
                sg = sgp.next()
                S.op("act", lambda e, sg=sg, cg=cbs[1]: e.activation(out=sg[:, :], in_=cg[:, :], func=AF.Silu),
                     reads=[cbs[1]], writes=[sg])
                S.op("dve", lambda e, sg=sg, ca=cbs[0], i=i: e.tensor_tensor(out=z_sb[:, i, :], in0=sg[:, :], in1=ca[:, :],
                                                                             op=ALU.mult), reads=[sg, cbs[0]], writes=[z_sb])
        w_out = ffn_w_out[l]
        for m in range(8):
            slot, wv = load_w(("wout", l, m), w_out[:, m * 128:(m + 1) * 128], NFC, 128)
            ps = mmr.next()
            for i in range(NFC):
                S.op("pe", lambda e, i=i, ps=ps, wv=wv: e.matmul(ps[:, :], lhsT=wv[:, i, :], rhs=z_sb[:, i, :],
                                                                 start=(i == 0), stop=(i == NFC - 1)),
                     reads=[slot, z_sb], writes=[ps])
            S.op("dve", lambda e, m=m, ps=ps: e.tensor_tensor(out=h_sb[:, m, :], in0=ps[:, :], in1=h_sb[:, m, :], op=ALU.add),
                 reads=[ps, h_c[m]], writes=[h_c[m]])
        rmsnorm("ple_norm%d" % l)
        S.op("pool", lambda e: e.dma_start(out=pt_sb[:, :, :],
                                           in_=pT[l, :, t * TW:(t + 1) * TW].rearrange("(k p) s -> p k s", p=128)),
             writes=[pt_sb], dma=pt_sb)
        pslot, pwv = load_w(("pproj", l), ple_w_proj[l], 2, 1024)
        for mg in range(2):
            slot, wv = load_w(("pgate", l, mg), ple_w_gate[l][:, mg * 512:(mg + 1) * 512], 8, 512)
            for mj in range(4):
                m = mg * 4 + mj
                psg = mmr.next()
                for kc in range(8):
                    S.op("pe", lambda e, kc=kc, mj=mj, psg=psg, wv=wv: e.matmul(
                        psg[:, :], lhsT=wv[:, kc, mj * 128:(mj + 1) * 128], rhs=hn[:, kc, :],
                        start=(kc == 0), stop=(kc == 7)), reads=[slot, hn], writes=[psg])
                sg = sgp.next()
                S.op("act", lambda e, sg=sg, psg=psg: e.activation(out=sg[:, :], in_=psg[:, :], func=AF.Sigmoid),
                     reads=[psg], writes=[sg])
                psp = mmr.next()
                for kc in range(2):
                    S.op("pe", lambda e, kc=kc, m=m, psp=psp, pwv=pwv: e.matmul(
                        psp[:, :], lhsT=pwv[:, kc, m * 128:(m + 1) * 128], rhs=pt_sb[:, kc, :],
                        start=(kc == 0), stop=(kc == 1)), reads=[pslot, pt_sb], writes=[psp])
                cb = cpool.next()
                S.op("dve", lambda e, cb=cb, psp=psp, sg=sg: e.tensor_tensor(out=cb[:, :], in0=psp[:, :], in1=sg[:, :],
                                                                             op=ALU.mult), reads=[psp, sg], writes=[cb])
                S.op("dve", lambda e, cb=cb, m=m: e.tensor_tensor(out=h_sb[:, m, :], in0=cb[:, :], in1=h_sb[:, m, :],
                                                                  op=ALU.add), reads=[cb, h_c[m]], writes=[h_c[m]])
                if m == 6 and t + 1 < NT:
                    load_ot(t + 1, 0)
                    ot_pref.add((l, t + 1))
                S.op("sp", lambda e, m=m: e.dma_start(out=hdst_ap[m * 128:(m + 1) * 128, t * TW:(t + 1) * TW], in_=h_sb[:, m, :]),
                     reads=[h_c[m]], writes=[hdst_cb[m]], dma=h_c[m])

    def proj_w(slot, wview, j0, kcn, rhs_b, rhs_of_kc, ps, parts=128, mcols=128):
        for kc in range(kcn):
            S.op("pe", lambda e, kc=kc: e.matmul(ps[0:mcols, :], lhsT=wview[:, kc, j0:j0 + mcols], rhs=rhs_of_kc(kc),
                                                 start=(kc == 0), stop=(kc == kcn - 1)),
                 reads=[slot, rhs_b], writes=[ps])

    _dense_slot = {}

    def a_phase1(l, t, hsrc_ap, hsrc_b):
        load_h(hsrc_ap, hsrc_b, t)
        rmsnorm("attn_norm%d" % l)
        wq = a_w_qkv[l]
        for grp in range(4):
            slot, wv = load_w(("qkv", l, grp), wq[:, grp * 512:(grp + 1) * 512], 8, 512)
            for j in range(4):
                oc = grp * 4 + j
                ps = mmr.next()
                proj_w(slot, wv, j * 128, 8, hn, lambda kc: hn[:, kc, :], ps)
                sb = stg.next()
                gname = ("a_q_norm%d" if oc < 8 else "a_k_norm%d") % l
                group_norm_to(ps, C(gname), sb, sb[:, :], 64)
                dst, dreg = (QA, QA_r) if oc < 8 else (KA, KA_r)
                hh = oc % 8
                S.op("sp", lambda e, sb=sb, dst=dst, hh=hh: e.dma_start(out=dst[hh, :, t * TW:(t + 1) * TW], in_=sb[:, :]),
                     reads=[sb], writes=[dreg[hh][t]], dma=sb)
        for half in range(2):
            slot, wv = load_w(("qkvv", l, half), wq[:, 2048 + half * 512:2048 + (half + 1) * 512], 8, 512)
            vs = vst.next()
            for blk in range(4):
                ps = mmr.next()
                for kc in range(8):
                    S.op("pe", lambda e, kc=kc, blk=blk, ps=ps, wv=wv: e.matmul(
                        ps[:, :], lhsT=hn[:, kc, blk * 128:(blk + 1) * 128], rhs=wv[:, kc, :],
                        start=(kc == 0), stop=(kc == 7)), reads=[slot, hn], writes=[ps])
                S.op("dve", lambda e, blk=blk, ps=ps, vs=vs: e.tensor_copy(
                    out=vs[:, :, blk, 0:128], in_=ps[:, :].rearrange("p (h d) -> p h d", h=4)), reads=[ps], writes=[vs])
            S.op("sp", lambda e, half=half, vs=vs: e.dma_start(
                out=VV[4 * half:4 * half + 4, :, 4 * t:4 * t + 4, :].rearrange("h p b e -> p h b e"), in_=vs[:, :, :, :]),
                reads=[vs], writes=[VV_r[4 * half + i][t] for i in range(4)], dma=vs)

    def rotary_tables():
        TWO_PI = 2.0 * math.pi
        C1 = 6.28125
        C2 = TWO_PI - C1
        pi_t = itmp
        ang = cpool.bufs[0]
        kf = cpool.bufs[1]
        ki = itmp2
        mk = cpool.bufs[2]
        for t in range(NT):
            S.op("sp", lambda e, t=t: e.dma_start(out=pi_t[:, :], in_=posb[:, t * TW:(t + 1) * TW]), writes=[pi_t], dma=pi_t)
            S.op("dve", lambda e: e.tensor_copy(out=ang[:, :], in_=pi_t[:, :]), reads=[pi_t], writes=[ang])
            S.op("dve", lambda e: e.tensor_scalar(out=ang[:, :], in0=ang[:, :], scalar1=C("invfreq"), scalar2=None,
                                                  op0=ALU.mult), reads=[ang, cols], writes=[ang])
            S.op("dve", lambda e: e.tensor_scalar(out=ki[:, :], in0=ang[:, :], scalar1=1.0 / TWO_PI, scalar2=None,
                                                  op0=ALU.mult), reads=[ang], writes=[ki])
            S.op("dve", lambda e: e.tensor_copy(out=kf[:, :], in_=ki[:, :]), reads=[ki], writes=[kf])
            S.op("dve", lambda e: e.scalar_tensor_tensor(out=ang[:, :], in0=kf[:, :], scalar=-C1, in1=ang[:, :],
                                                         op0=ALU.mult, op1=ALU.add), reads=[kf, ang], writes=[ang])
            S.op("dve", lambda e: e.scalar_tensor_tensor(out=ang[:, :], in0=kf[:, :], scalar=-C2, in1=ang[:, :],
                                                         op0=ALU.mult, op1=ALU.add), reads=[kf, ang], writes=[ang])
            S.op("dve", lambda e: e.tensor_scalar(out=mk[:, :], in0=ang[:, :], scalar1=math.pi, scalar2=None,
                                                  op0=ALU.is_gt), reads=[ang], writes=[mk])
            S.op("dve", lambda e: e.scalar_tensor_tensor(out=ang[:, :], in0=mk[:, :], scalar=-TWO_PI, in1=ang[:, :],
                                                         op0=ALU.mult, op1=ALU.add), reads=[mk, ang], writes=[ang])
            S.op("dve", lambda e: e.tensor_scalar(out=mk[:, :], in0=ang[:, :], scalar1=-math.pi, scalar2=None,
                                                  op0=ALU.is_lt), reads=[ang], writes=[mk])
            S.op("dve", lambda e: e.scalar_tensor_tensor(out=ang[:, :], in0=mk[:, :], scalar=TWO_PI, in1=ang[:, :],
                                                         op0=ALU.mult, op1=ALU.add), reads=[mk, ang], writes=[ang])
            S.op("dve", lambda e: e.tensor_scalar(out=ang[:, :], in0=ang[:, :], scalar1=-3.1415925, scalar2=3.1415925,
                                                  op0=ALU.max, op1=ALU.min), reads=[ang], writes=[ang])
            S.op("act", lambda e: e.activation(out=cs_sb[:, 1, :], in_=ang[:, :], func=AF.Sin), reads=[ang], writes=[cs_sb])
            S.op("dve", lambda e: e.scalar_tensor_tensor(out=kf[:, :], in0=ang[:, :], scalar=-1.0, in1=ang[:, :],
                                                         op0=ALU.mult, op1=ALU.max), reads=[ang], writes=[kf])
            S.op("dve", lambda e: e.tensor_scalar(out=kf[:, :], in0=kf[:, :], scalar1=-1.0, scalar2=math.pi / 2,
                                                  op0=ALU.mult, op1=ALU.add), reads=[kf], writes=[kf])
            S.op("act", lambda e: e.activation(out=cs_sb[:, 0, :], in_=kf[:, :], func=AF.Sin), reads=[kf], writes=[cs_sb])
            S.op("sp", lambda e, t=t: [e.dma_start(out=COS[:, t * TW:(t + 1) * TW], in_=cs_sb[:, 0, :]),
                                       e.dma_start(out=SIN[:, t * TW:(t + 1) * TW], in_=cs_sb[:, 1, :])],
                 reads=[cs_sb], writes=[CS_r[t]], dma=cs_sb, ninc=2)

    def load_cs(t):
        S.op("sp", lambda e: [e.dma_start(out=cs_sb[:, 0, :], in_=COS[:, t * TW:(t + 1) * TW]),
                              e.dma_start(out=cs_sb[:, 1, :], in_=SIN[:, t * TW:(t + 1) * TW])],
             reads=[CS_r[t]], writes=[cs_sb], dma=cs_sb, ninc=2)

    def apply_rotary(xb, parts, out_b, out_ap):
        rp = mmr.next()
        S.op("pe", lambda e: e.matmul(rp[0:parts, :], lhsT=prot[0:parts, 0:parts], rhs=xb[0:parts, :], start=True, stop=True),
             reads=[prot, xb], writes=[rp])
        cb = cpool.next()
        S.op("dve", lambda e: e.tensor_tensor(out=cb[0:parts, :], in0=rp[0:parts, :], in1=cs_sb[0:parts, 1, :], op=ALU.mult),
             reads=[rp, cs_sb], writes=[cb])
        cb2 = cpool.next()
        S.op("dve", lambda e: e.tensor_tensor(out=cb2[0:parts, :], in0=xb[0:parts, :], in1=cs_sb[0:parts, 0, :], op=ALU.mult),
             reads=[xb, cs_sb], writes=[cb2])
        S.op("dve", lambda e: e.tensor_tensor(out=out_ap, in0=cb[0:parts, :], in1=cb2[0:parts, :], op=ALU.add),
             reads=[cb, cb2], writes=[out_b])

    def shared_kv_tile(t, hsrc_ap, hsrc_b):
        load_h(hsrc_ap, hsrc_b, t)
        load_cs(t)
        rmsnorm("kv_norm")
        slot, wv = load_w(("dkv",), w_dkv, 8, 320)
        pss = []
        for c in range(2):
            ps = mmr.next()
            proj_w(slot, wv, c * 128, 8, hn, lambda kc: hn[:, kc, :], ps)
            pss.append(ps)
        st = STAT
        for c in range(2):
            sq = sqr.next()
            S.op("act", lambda e, sq=sq, c=c: e.activation(out=sq[:, :], in_=pss[c][:, :], func=AF.Square),
                 reads=[pss[c]], writes=[sq])
            S.op("pe", lambda e, sq=sq, c=c: e.matmul(st[:, :], lhsT=ones[:, :], rhs=sq[:, :], start=(c == 0), stop=(c == 1)),
                 reads=[ones, sq], writes=[st])
        rs = rstd_from(st, st[:, :], 1.0 / 256)
        for c in range(2):
            S.op("dve", lambda e, c=c: e.scalar_tensor_tensor(out=ckvn[:, c, :], in0=pss[c][:, :], scalar=C("ckv_norm", c),
                                                              in1=rs[:, :], op0=ALU.mult, op1=ALU.mult),
                 reads=[pss[c], cols, rs], writes=[ckvn])
        ps = mmr.next()
        proj_w(slot, wv, 256, 8, hn, lambda kc: hn[:, kc, :], ps, mcols=64)
        sb = stg.next()
        group_norm_to(ps, cols[0:64, cm.m["k_pe_norm"]:cm.m["k_pe_norm"] + 1], sb, sb[0:64, :], 64, parts=64)
        sb2 = stg.next()
        apply_rotary(sb, 64, sb2, sb2[0:64, :])
        S.op("sp", lambda e: e.dma_start(out=KPE[:, t * TW:(t + 1) * TW], in_=sb2[0:64, :]), reads=[sb2],
             writes=[KPE_r[t]], dma=sb2)
        for g4 in range(4):
            slot, wv = load_w(("ukvK", g4), w_ukvK[:, g4 * 512:(g4 + 1) * 512], 2, 512)
            for j in range(4):
                hh = g4 * 4 + j
                ps = mmr.next()
                proj_w(slot, wv, j * 128, 2, ckvn, lambda kc: ckvn[:, kc, :], ps)
                sb = stg.next()
                group_norm_to(ps, C("k_nope_norm"), sb, sb[:, :], 128)
                S.op("sp", lambda e, sb=sb, hh=hh: e.dma_start(out=KA[hh, :, t * TW:(t + 1) * TW], in_=sb[:, :]),
                     reads=[sb], writes=[KA_r[hh][t]], dma=sb)
        for g4 in range(4):
            slot, wv = load_w(("ukvV", g4), w_ukvV[:, g4 * 512:(g4 + 1) * 512], 2, 512)
            vs = vst.next()
            for blk in range(4):
                ps = mmr.next()
                for kc in range(2):
                    S.op("pe", lambda e, kc=kc, blk=blk, ps=ps, wv=wv: e.matmul(
                        ps[:, :], lhsT=ckvn[:, kc, blk * 128:(blk + 1) * 128], rhs=wv[:, kc, :],
                        start=(kc == 0), stop=(kc == 1)), reads=[slot, ckvn], writes=[ps])
                S.op("dve", lambda e, blk=blk, ps=ps, vs=vs: e.tensor_copy(
                    out=vs[:, :, blk, 0:128], in_=ps[:, :].rearrange("p (h d) -> p h d", h=4)), reads=[ps], writes=[vs])
            S.op("sp", lambda e, g4=g4, vs=vs: e.dma_start(
                out=VV[4 * g4:4 * g4 + 4, :, 4 * t:4 * t + 4, :].rearrange("h p b e -> p h b e"), in_=vs[:, :, :, :]),
                reads=[vs], writes=[VV_r[4 * g4 + i][t] for i in range(4)], dma=vs)

    def b_phase1(j_, l, t, hsrc_ap, hsrc_b):
        load_h(hsrc_ap, hsrc_b, t)
        load_cs(t)
        rmsnorm("attn_norm%d" % l)
        slot, wv = load_w(("dq", j_), b_w_dq[j_], 8, 512)
        pss = []
        for c in range(4):
            ps = mmr.next()
            proj_w(slot, wv, c * 128, 8, hn, lambda kc: hn[:, kc, :], ps)
            pss.append(ps)
        st = STAT
        for c in range(4):
            sq = sqr.next()
            S.op("act", lambda e, sq=sq, c=c: e.activation(out=sq[:, :], in_=pss[c][:, :], func=AF.Square),
                 reads=[pss[c]], writes=[sq])
            S.op("pe", lambda e, sq=sq, c=c: e.matmul(st[:, :], lhsT=ones[:, :], rhs=sq[:, :], start=(c == 0), stop=(c == 3)),
                 reads=[ones, sq], writes=[st])
        rs = rstd_from(st, st[:, :], 1.0 / 512)
        for c in range(4):
            S.op("dve", lambda e, c=c: e.scalar_tensor_tensor(out=cqn[:, c, :], in0=pss[c][:, :],
                                                              scalar=C("b_cq_norm%d" % j_, c), in1=rs[:, :],
                                                              op0=ALU.mult, op1=ALU.mult),
                 reads=[pss[c], cols, rs], writes=[cqn])
        for g4 in range(4):
            slot, wv = load_w(("uqN", j_, g4), b_w_uqN[j_][:, g4 * 512:(g4 + 1) * 512], 4, 512)
            for j in range(4):
                hh = g4 * 4 + j
                ps = mmr.next()
                proj_w(slot, wv, j * 128, 4, cqn, lambda kc: cqn[:, kc, :], ps)
                sb = stg.next()
                group_norm_to(ps, C("b_q_nope_norm%d" % j_), sb, sb[:, :], 128)
                S.op("sp", lambda e, sb=sb, hh=hh: e.dma_start(out=QA[hh, :, t * TW:(t + 1) * TW], in_=sb[:, :]),
                     reads=[sb], writes=[QA_r[hh][t]], dma=sb)
        for g2 in range(2):
            slot, wv = load_w(("uqP", j_, g2), b_w_uqP[j_][:, g2 * 512:(g2 + 1) * 512], 4, 512)
            for j in range(4):
                ch = g2 * 4 + j
                ps = mmr.next()
                proj_w(slot, wv, j * 128, 4, cqn, lambda kc: cqn[:, kc, :], ps)
                sb = stg.next()
                group_norm_to(ps, C("b_q_pe_norm%d" % j_), sb, sb[:, :], 64)
                sb2 = stg.next()
                apply_rotary(sb, 128, sb2, sb2[:, :])
                S.op("sp", lambda e, sb2=sb2, ch=ch: e.dma_start(out=QP[ch, :, t * TW:(t + 1) * TW], in_=sb2[:, :]),
                     reads=[sb2], writes=[QP_r[ch][t]], dma=sb2)

    def hsrc_of(l):
        return (xT, None) if l == 0 else (hT, hT_r)

    for l in range(L):
        src_ap, src_r = hsrc_of(l)
        last = (l == L - 1)
        dst_ap, dst_r = (outT, None) if last else (hT, hT_r)
        if l < 2:
            for t in range(NT):
                a_phase1(l, t, src_ap, src_r[t] if src_r else None)
            ckeys = []
            if l == 0:
                cur_phase[0] = 1
                if wt_plan is not None:
                    ckeys = [k for k, v in wt_plan.items() if v[0] > 0]
            nck = (len(ckeys) + 7) // 8
            for h in range(8):
                attention_head("A", l, h)
                if ckeys:
                    convert_keys(ckeys[h * nck:(h + 1) * nck], gate=[attk])
            for i in range(44):
                S.op("dve", lambda e, i=i: e.memset(utail_t[:, i, :], 0.0), writes=[utail[i]])
            for t in range(NT):
                dense_tile(l, t, 8, a_w_o[l], src_ap, src_r[t] if src_r else None, dst_ap, dst_r)
        else:
            j_ = l - 2
            if l == 2:
                rotary_tables()
                for t in range(NT):
                    shared_kv_tile(t, src_ap, src_r[t])
                S.op("dve", lambda e: e.memset(attkp[64:128, :], 0.0), writes=[attkp])
                S.op("sp", lambda e: e.dma_start(out=attkp[0:64, :], in_=KPE), reads=KPE_r, writes=[attkp], dma=attkp)
                for r_ in qpring.bufs:
                    S.op("dve", lambda e, r_=r_: e.memset(r_[64:128, :], 0.0), writes=[r_])
            for t in range(NT):
                b_phase1(j_, l, t, src_ap, src_r[t])
            for h in range(16):
                attention_head("B", l, h)
            for i in range(44):
                S.op("dve", lambda e, i=i: e.memset(utail_t[:, i, :], 0.0), writes=[utail[i]])
            for t in range(NT):
                dense_tile(l, t, 16, b_w_o[j_], src_ap, src_r[t], dst_ap, dst_r)

    stats = S.emit()
    es.close()
    return nc, stats, cm, rm, wt_first


_PROG = {}


def _get_prog(SL, L):
    key = (SL, L)
    if key not in _PROG:
        plan = build_program(SL, L)[4]
        _PROG[key] = build_program(SL, L, wt_plan=plan)[:4]
    return _PROG[key]


def prepare_inputs(inp, SL, L, cm, rm):
    f = lambda a: np.ascontiguousarray(np.asarray(a, dtype=np.float32))
    B = inp["x"].shape[0]
    cols = np.zeros((128, cm.n), np.float32)
    rows = np.zeros((128, rm.n), np.float32)

    def put_vec(name, v, off=0):
        v = np.asarray(v, np.float32)
        k = v.shape[0] // 128
        cols[:, cm.m[name] + off:cm.m[name] + off + k] = v.reshape(k, 128).T

    for l in range(4):
        put_vec("attn_norm%d" % l, inp["attn_norm"][l])
        put_vec("ffn_norm%d" % l, inp["ffn_norm"][l])
        put_vec("ple_norm%d" % l, inp["ple_norm"][l])
        for j in range(3):
            put_vec("conv_w%d_%d" % (l, j), np.asarray(inp["ffn_conv_w"])[l, j])
        put_vec("conv_b%d" % l, np.asarray(inp["ffn_conv_b"])[l])
    put_vec("kv_norm", inp["kv_norm"])
    for l in range(2):
        put_vec("a_q_norm%d" % l, np.tile(np.asarray(inp["a_q_norm"])[l], 2))
        put_vec("a_k_norm%d" % l, np.tile(np.asarray(inp["a_k_norm"])[l], 2))
        put_vec("b_cq_norm%d" % l, np.asarray(inp["b_cq_norm"])[l])
        put_vec("b_q_nope_norm%d" % l, np.asarray(inp["b_q_nope_norm"])[l])
        put_vec("b_q_pe_norm%d" % l, np.tile(np.asarray(inp["b_q_pe_norm"])[l], 2))
    put_vec("ckv_norm", inp["ckv_norm"])
    put_vec("k_nope_norm", inp["k_nope_norm"])
    put_vec("k_pe_norm", np.tile(np.asarray(inp["k_pe_norm"]), 2))
    tab = np.asarray(inp["rel_bias_table"], np.float32)
    cols[:, cm.m["b31"]:cm.m["b31"] + 8] = tab[31][None, :]
    cols[:, cm.m["table"]:cm.m["table"] + 256] = tab.reshape(1, 256)
    half = 32
    invf = np.exp(-math.log(10000.0) * np.arange(half, dtype=np.float32) * np.float32(2.0 / 64)).astype(np.float32)
    cols[:, cm.m["invfreq"]] = np.tile(invf, 4)
    for l in range(2):
        for nm, key in (("q1", "a_lam_q1"), ("k1", "a_lam_k1"), ("q2", "a_lam_q2"), ("k2", "a_lam_k2")):
            r0 = rm.m["lam_%s%d" % (nm, l)]
            rows[:, r0:r0 + 64] = np.asarray(inp[key], np.float32)[l][None, :]
        r0 = rm.m["sub_gain%d" % l]
        rows[:, r0:r0 + 128] = np.asarray(inp["a_sub_norm"], np.float32)[l][None, :]
    prot = np.zeros((128, 128), np.float32)
    for g in range(2):
        for m in range(64):
            if m < 32:
                prot[g * 64 + m + 32, g * 64 + m] = -1.0
            else:
                prot[g * 64 + m - 32, g * 64 + m] = 1.0
    w_ukv = np.asarray(inp["w_ukv"], np.float32).reshape(256, 16, 256)
    w_uq = np.asarray(inp["b_w_uq"], np.float32).reshape(2, 512, 16, 192)
    shared = dict(
        cols=cols, rows=rows, prot=prot,
        a_w_qkv=f(inp["a_w_qkv"]), a_w_o=f(inp["a_w_o"]), w_dkv=f(inp["w_dkv"]),
        w_ukvK=np.ascontiguousarray(w_ukv[:, :, 0:128].reshape(256, 2048)),
        w_ukvV=np.ascontiguousarray(w_ukv[:, :, 128:256].reshape(256, 2048)),
        b_w_dq=f(inp["b_w_dq"]),
        b_w_uqN=np.ascontiguousarray(w_uq[:, :, :, 0:128].reshape(2, 512, 2048)),
        b_w_uqP=np.ascontiguousarray(w_uq[:, :, :, 128:192].reshape(2, 512, 1024)),
        b_w_o=f(inp["b_w_o"]), ffn_w_in=f(inp["ffn_w_in"]), ffn_w_out=f(inp["ffn_w_out"]),
        ple_w_proj=f(inp["ple_w_proj"]), ple_w_gate=f(inp["ple_w_gate"]),
    )
    x = np.asarray(inp["x"], np.float32)
    p = np.asarray(inp["p"], np.float32)
    pos = np.asarray(inp["positions"]).astype(np.int32)
    maps = []
    for c in range(NCORES):
        b = c % B
        m = dict(shared)
        m["xT"] = np.ascontiguousarray(x[b, :SL].T)
        m["pT"] = np.ascontiguousarray(p[:, b, :SL].transpose(0, 2, 1))
        m["posb"] = np.ascontiguousarray(np.broadcast_to(pos[b, :SL][None, :], (128, SL)))
        m["poscol"] = np.ascontiguousarray(pos[b, :256].reshape(2, 128).T)
        maps.append(m)
    return maps


def run_model(inp, SL, L):
    nc, stats, cm, rm = _get_prog(SL, L)
    maps = prepare_inputs(inp, SL, L, cm, rm)
    res = run_bass_kernel_spmd(nc, maps, core_ids=list(range(NCORES)))
    B = inp["x"].shape[0]
    out = np.stack([np.ascontiguousarray(res.results[b]["outT"].T) for b in range(B)], axis=0)
    return out.astype(np.float32)


def kernel(**inputs):
    return run_model(inputs, 4096, 4)
```

```python
import math
from contextlib import ExitStack

import numpy as np
import concourse.bass as bass
import concourse.mybir as mybir
from concourse.bass_utils import run_bass_kernel_spmd

F32 = mybir.dt.float32
BF16 = mybir.dt.bfloat16
I32 = mybir.dt.int32
AF = mybir.ActivationFunctionType
ALU = mybir.AluOpType

D = 1024
DFF = 2816
NFC = 22
TW = 512
EPS = 1e-6
NCORES = 8


class Buf:
    __slots__ = ("name", "t", "writers", "readers", "dkey")

    def __init__(self, name, t=None):
        self.name = name
        self.t = t
        self.writers = {}
        self.readers = {}
        self.dkey = None

    def __getitem__(self, idx):
        return self.t[idx]


class Op:
    __slots__ = ("q", "fn", "deps", "key", "signaled", "count", "ninc")


class Sched:
    def __init__(self, nc, es):
        self.nc = nc
        self.es = es
        self.ops = []
        self.eng = {"pe": nc.tensor, "act": nc.scalar, "dve": nc.vector,
                    "pool": nc.gpsimd, "sp": nc.sync}
        self.last_dma = {}
        self.ndkeys = 0

    def sbuf(self, name, shape, dt):
        return Buf(name, self.es.enter_context(self.nc.sbuf_tensor("sb_" + name, list(shape), dt)))

    def psum(self, name, shape, dt):
        return Buf(name, self.es.enter_context(self.nc.psum_tensor("ps_" + name, list(shape), dt)))

    def op(self, q, fn, reads=(), writes=(), dma=None, ninc=1):
        o = Op()
        o.q = q
        o.fn = fn
        o.signaled = False
        o.count = 0
        o.ninc = ninc
        isdma = dma is not None
        if isdma:
            if dma.dkey is None:
                dma.dkey = self.ndkeys
                self.ndkeys += 1
            o.key = ("dma", dma.dkey)
        else:
            o.key = q
        key = o.key
        deps = {}
        for b in reads:
            for w in b.writers.values():
                deps[id(w)] = w
        for b in writes:
            for k, r in b.readers.items():
                if k != key or isdma:
                    deps[id(r)] = r
            for k, w in b.writers.items():
                if k != key or isdma:
                    deps[id(w)] = w
        if isdma:
            prev = self.last_dma.get(key)
            if prev is not None:
                deps[id(prev)] = prev
            self.last_dma[key] = o
            o.signaled = True
        o.deps = list(deps.values())
        for d in o.deps:
            d.signaled = True
        for b in reads:
            b.readers[key] = o
        for b in writes:
            b.readers = {}
            b.writers = {key: o}
        self.ops.append(o)
        return o

    def emit(self):
        nc = self.nc
        sems = {}
        counts = {}
        waited = {}
        for o in self.ops:
            if o.signaled:
                if o.key not in sems:
                    nm = "s_" + (o.key if isinstance(o.key, str) else "d%d" % o.key[1])
                    sems[o.key] = self.es.enter_context(nc.semaphore(nm))
                    counts[o.key] = 0
                counts[o.key] += (16 * o.ninc) if not isinstance(o.key, str) else 1
                o.count = counts[o.key]
        nwaits = 0
        for o in self.ops:
            e = self.eng[o.q]
            wq = waited.setdefault(o.q, {})
            for d in o.deps:
                if wq.get(d.key, 0) < d.count:
                    e.wait_ge(sems[d.key], d.count)
                    wq[d.key] = d.count
                    nwaits += 1
            ins = o.fn(e)
            if o.signaled:
                if isinstance(o.key, str):
                    last = ins[-1] if isinstance(ins, (list, tuple)) else ins
                    last.then_inc(sems[o.key], 1)
                else:
                    lst = list(ins) if isinstance(ins, (list, tuple)) else [ins]
                    assert len(lst) == o.ninc, (len(lst), o.ninc)
                    for i in lst:
                        i.then_inc(sems[o.key], 16)
        e = self.eng["sp"]
        wq = waited.setdefault("sp", {})
        for key, s in sems.items():
            if counts[key] > wq.get(key, 0):
                e.wait_ge(s, counts[key])
        return dict(nops=len(self.ops), nwaits=nwaits, nsems=len(sems),
                    maxcount=max(counts.values()) if counts else 0)


class Ring:
    def __init__(self, bufs):
        self.bufs = bufs
        self.i = 0

    def next(self):
        b = self.bufs[self.i % len(self.bufs)]
        self.i += 1
        return b


def t5_thresholds():
    n = np.arange(0, 400)
    f = np.maximum(n, 1).astype(np.float32) / np.float32(16)
    lr = np.log(f).astype(np.float32) / np.float32(math.log(128 / 16))
    large = 16 + (lr * np.float32(16)).astype(np.int32)
    large = np.minimum(large, 31)
    bucket = np.where(n < 16, n, large)
    lo = [int(np.min(n[bucket >= b])) for b in range(32)]
    return lo


class ColMap:
    def __init__(self):
        self.n = 0
        self.m = {}

    def add(self, name, w):
        self.m[name] = self.n
        self.n += w
        return self.m[name]


def make_colmap():
    cm = ColMap()
    for l in range(4):
        cm.add("attn_norm%d" % l, 8)
        cm.add("ffn_norm%d" % l, 8)
        cm.add("ple_norm%d" % l, 8)
        for j in range(3):
            cm.add("conv_w%d_%d" % (l, j), 44)
        cm.add("conv_b%d" % l, 44)
    cm.add("kv_norm", 8)
    for l in range(2):
        cm.add("a_q_norm%d" % l, 1)
        cm.add("a_k_norm%d" % l, 1)
        cm.add("b_cq_norm%d" % l, 4)
        cm.add("b_q_nope_norm%d" % l, 1)
        cm.add("b_q_pe_norm%d" % l, 1)
    cm.add("ckv_norm", 2)
    cm.add("k_nope_norm", 1)
    cm.add("k_pe_norm", 1)
    cm.add("b31", 8)
    cm.add("table", 256)
    cm.add("invfreq", 1)
    return cm


def make_rowmap():
    rm = ColMap()
    for l in range(2):
        for nm in ("q1", "k1", "q2", "k2"):
            rm.add("lam_%s%d" % (nm, l), 64)
        rm.add("sub_gain%d" % l, 128)
    return rm


def build_program(SL, L, wt_plan=None):
    NT = SL // TW
    NB = SL // 128
    nc = bass.Bass("TRN2", target_bir_lowering=False)
    cm = make_colmap()
    rm = make_rowmap()
    lo_thr = t5_thresholds()

    def din(name, shape, dt=F32):
        return nc.dram_tensor(name, list(shape), dt, kind="ExternalInput").ap()

    def dscr(name, shape, dt):
        return nc.dram_tensor(name, list(shape), dt, kind="Internal").ap()

    xT = din("xT", [D, SL])
    pT = din("pT", [4, 256, SL])
    posb = din("posb", [128, SL], I32)
    poscol = din("poscol", [128, 2], I32)
    colsd = din("cols", [128, cm.n])
    rowsd = din("rows", [128, rm.n])
    protd = din("prot", [128, 128])
    a_w_qkv = din("a_w_qkv", [2, D, 3072])
    a_w_o = din("a_w_o", [2, D, D])
    w_dkv = din("w_dkv", [D, 320])
    w_ukvK = din("w_ukvK", [256, 2048])
    w_ukvV = din("w_ukvV", [256, 2048])
    b_w_dq = din("b_w_dq", [2, D, 512])
    b_w_uqN = din("b_w_uqN", [2, 512, 2048])
    b_w_uqP = din("b_w_uqP", [2, 512, 1024])
    b_w_o = din("b_w_o", [2, 2048, D])
    ffn_w_in = din("ffn_w_in", [4, D, 2 * DFF])
    ffn_w_out = din("ffn_w_out", [4, DFF, D])
    ple_w_proj = din("ple_w_proj", [4, 256, D])
    ple_w_gate = din("ple_w_gate", [4, D, D])
    outT = nc.dram_tensor("outT", [D, SL], F32, kind="ExternalOutput").ap()

    hT = dscr("hT", [D, SL], F32)
    QA = dscr("QA", [16, 128, SL], BF16)
    KA = dscr("KA", [16, 128, SL], BF16)
    QP = dscr("QP", [8, 128, SL], BF16)
    KPE = dscr("KPE", [64, SL], BF16)
    VV = dscr("VV", [16, 128, NB, 129], BF16)
    OT = dscr("OT", [16, 128, SL], BF16)
    NWT = max(1, sum(1 for v in wt_plan.values() if v[0] > 0)) if wt_plan is not None else 1
    WT = dscr("WT", [NWT, 128, 4096], BF16)
    COS = dscr("COS", [128, SL], F32)
    SIN = dscr("SIN", [128, SL], F32)

    es = ExitStack()
    S = Sched(nc, es)

    def regions(name, n1):
        return [[Buf("%s_%d_%d" % (name, i, t)) for t in range(NT)] for i in range(n1)]

    hT_r = [[Buf("hT_%d_%d" % (t, k)) for k in range(8)] for t in range(NT)]
    QA_r = regions("QA", 16)
    KA_r = regions("KA", 16)
    QP_r = regions("QP", 8)
    KPE_r = [Buf("KPE_%d" % t) for t in range(NT)]
    VV_r = regions("VV", 16)
    OT_r = regions("OT", 16)
    CS_r = [Buf("CS_%d" % t) for t in range(NT)]

    cols = S.sbuf("cols", [128, cm.n], F32)
    rows = S.sbuf("rows", [128, rm.n], F32)
    ones = S.sbuf("ones", [128, 128], BF16)
    bones = S.sbuf("bones", [128, 128], BF16)
    ident = S.sbuf("ident", [128, 128], F32)
    prot = S.sbuf("prot", [128, 128], BF16)
    tri = S.sbuf("tri", [128, 128], F32)
    epsc = S.sbuf("epsc", [128, 1], F32)
    Rt = S.sbuf("Rt", [128, 2, 8, 128], F32)
    lamt = S.sbuf("lamt", [128, 8], F32)
    subg = S.sbuf("subg", [128, 2, 128], F32)
    wslots = [S.sbuf("w%d" % i, [128, 4096], BF16) for i in range(4)]
    wring = Ring(wslots)
    h_sb = S.sbuf("h_sb", [128, 8, TW], F32)
    h_c = [Buf("h_c%d" % i, h_sb.t) for i in range(8)]
    hn = S.sbuf("hn", [128, 8, TW], BF16)
    sqr = Ring([S.sbuf("sq%d" % i, [128, TW], BF16) for i in range(3)])
    lnr = Ring([S.sbuf("ln%d" % i, [128, TW], F32) for i in range(2)])
    rsr = Ring([S.sbuf("rs%d" % i, [128, TW], F32) for i in range(2)])
    stg = Ring([S.sbuf("stg%d" % i, [128, TW], BF16) for i in range(4)])
    vst = Ring([S.sbuf("vst%d" % i, [128, 4, 4, 129], BF16) for i in range(2)])
    qring = Ring([S.sbuf("attq%d" % i, [128, TW], BF16) for i in range(2)])
    qpring = Ring([S.sbuf("attqp%d" % i, [128, TW], BF16) for i in range(2)])
    attk = S.sbuf("attk", [128, SL], BF16)
    attv = S.sbuf("attv", [128, NB, 129], BF16)
    attkp = S.sbuf("attkp", [128, SL], BF16)
    ppool = Ring([S.sbuf("pt%d" % i, [128, TW], BF16) for i in range(6)])
    otst = Ring([S.sbuf("otst%d" % i, [128, TW], BF16) for i in range(2)])
    o_sb = Ring([S.sbuf("o_sb%d" % i, [128, 128], F32) for i in range(4)])
    on_sb = Ring([S.sbuf("on_sb%d" % i, [128, 128], F32) for i in range(4)])
    smr = Ring([S.sbuf("sm%d" % i, [128, 8], F32) for i in range(4)])
    junk = S.sbuf("junk", [128, 128], F32)
    ot_sb = S.sbuf("ot_sb", [128, 8, TW], BF16)
    z_sb = S.sbuf("z_sb", [128, NFC, TW], BF16)
    uext = Ring([S.sbuf("uext%d" % i, [128, TW + 2], F32) for i in range(4)])
    cpool = Ring([S.sbuf("c%d" % i, [128, TW], F32) for i in range(4)])
    sgp = Ring([S.sbuf("sg%d" % i, [128, TW], F32) for i in range(2)])
    utail_t = es.enter_context(nc.sbuf_tensor("sb_utail", [128, 44, 2], F32))
    utail = [Buf("utail%d" % i, utail_t) for i in range(44)]
    pt_sb = S.sbuf("pt_sb", [128, 2, TW], BF16)
    cs_sb = S.sbuf("cs_sb", [128, 2, TW], F32)
    cqn = S.sbuf("cqn", [128, 4, TW], BF16)
    ckvn = S.sbuf("ckvn", [128, 2, TW], BF16)
    itmp = S.sbuf("itmp", [128, TW], I32)
    itmp2 = S.sbuf("itmp2", [128, TW], I32)
    pb = [S.psum("pb%d" % i, [128, TW], F32) for i in range(8)]

    def C(name, off=0, w=1):
        c0 = cm.m[name] + off
        return cols[:, c0:c0 + w]

    def dve_tt(out_b, out_ap, a_b, a_ap, b_b, b_ap, op, q="dve"):
        S.op(q, lambda e: e.tensor_tensor(out=out_ap, in0=a_ap, in1=b_ap, op=op),
             reads=[a_b, b_b], writes=[out_b])

    wt_first = {}
    wt_tid = {}
    wt_bufs = {}
    cur_phase = [0]

    def _cast_load(slot, srcs, kc, ns, gate=()):
        n = sum(ns)
        view = slot[:, 0:kc * n].rearrange("p (k n) -> p k n", k=kc)
        svs = [a.rearrange("(k p) n -> p k n", p=128) for a in srcs]
        offs = [sum(ns[:i]) for i in range(len(ns))]
        S.op("pool", lambda e: [e.dma_start(out=view[:, :, offs[i]:offs[i] + ns[i]], in_=svs[i]) for i in range(len(ns))],
             reads=list(gate), writes=[slot], dma=slot, ninc=len(ns))
        return view

    def src_for_key(key):
        k0 = key[0]
        if k0 == "wo":
            _, l, half, mg = key
            w = a_w_o[l] if l < 2 else b_w_o[l - 2]
            return [w[half * 1024:(half + 1) * 1024, mg * 512:(mg + 1) * 512]], 8, [512]
        if k0 == "win":
            _, l, ig = key
            w = ffn_w_in[l]
            return [w[:, ig * 128:(ig + 2) * 128], w[:, DFF + ig * 128:DFF + (ig + 2) * 128]], 8, [256, 256]
        if k0 == "wout":
            _, l, m = key
            return [ffn_w_out[l][:, m * 128:(m + 1) * 128]], NFC, [128]
        if k0 == "pproj":
            return [ple_w_proj[key[1]]], 2, [1024]
        if k0 == "pgate":
            _, l, mg = key
            return [ple_w_gate[l][:, mg * 512:(mg + 1) * 512]], 8, [512]
        if k0 == "qkv":
            _, l, grp = key
            return [a_w_qkv[l][:, grp * 512:(grp + 1) * 512]], 8, [512]
        if k0 == "qkvv":
            _, l, half = key
            return [a_w_qkv[l][:, 2048 + half * 512:2048 + (half + 1) * 512]], 8, [512]
        if k0 == "dkv":
            return [w_dkv], 8, [320]
        if k0 == "ukvK":
            return [w_ukvK[:, key[1] * 512:(key[1] + 1) * 512]], 2, [512]
        if k0 == "ukvV":
            return [w_ukvV[:, key[1] * 512:(key[1] + 1) * 512]], 2, [512]
        if k0 == "dq":
            return [b_w_dq[key[1]]], 8, [512]
        if k0 == "uqN":
            return [b_w_uqN[key[1]][:, key[2] * 512:(key[2] + 1) * 512]], 4, [512]
        if k0 == "uqP":
            return [b_w_uqP[key[1]][:, key[2] * 512:(key[2] + 1) * 512]], 4, [512]
        raise KeyError(key)

    if wt_plan is not None:
        for key_, (ph_, _) in wt_plan.items():
            if ph_ > 0:
                wt_tid[key_] = len(wt_tid)
                wt_bufs[key_] = Buf("wt_%d" % wt_tid[key_])

    def convert_keys(keys, gate=()):
        for ki, key in enumerate(keys):
            srcs, kc, ns = src_for_key(key)
            tid = wt_tid[key]
            slot = wring.next()
            _cast_load(slot, srcs, kc, ns, gate=gate if ki == 0 else ())
            n = kc * sum(ns)
            S.op("sp", lambda e, slot=slot, tid=tid, n=n: e.dma_start(out=WT[tid, :, 0:n], in_=slot[:, 0:n]),
                 reads=[slot], writes=[wt_bufs[key]], dma=slot)

    srcs_of = {}

    def _load(key, srcs, kc, ns):
        n = sum(ns)
        if key not in wt_first:
            wt_first[key] = (cur_phase[0], (None, kc, ns))
        srcs_of.setdefault(key, srcs)
        slot = wring.next()
        if wt_plan is None or key not in wt_tid:
            view = _cast_load(slot, srcs, kc, ns)
        else:
            tid = wt_tid[key]
            view = slot[:, 0:kc * n].rearrange("p (k n) -> p k n", k=kc)
            S.op("pool", lambda e: e.dma_start(out=slot[:, 0:kc * n], in_=WT[tid, :, 0:kc * n]),
                 reads=[wt_bufs[key]], writes=[slot], dma=slot)
        return slot, view

    def load_w(key, src_ap, kc, n):
        return _load(key, [src_ap], kc, [n])

    def load_w2(key, src_a, src_b, kc, na, nb_):
        return _load(key, [src_a, src_b], kc, [na, nb_])

    mmr = Ring(pb[0:5])
    str_ = Ring(pb[5:7])
    STAT = pb[7]

    def rstd_from(stat_b, stat_ap, inv_n, width=TW, parts=128):
        ln = lnr.next()
        rs = rsr.next()
        S.op("act", lambda e: e.activation(out=ln[0:parts, 0:width], in_=stat_ap, func=AF.Ln,
                                           bias=epsc[0:parts, :], scale=inv_n),
             reads=[stat_b, epsc], writes=[ln])
        S.op("act", lambda e: e.activation(out=rs[0:parts, 0:width], in_=ln[0:parts, 0:width],
                                           func=AF.Exp, scale=-0.5),
             reads=[ln], writes=[rs])
        return rs

    def rmsnorm(gain_name):
        st = STAT
        for kc in range(8):
            sq = sqr.next()
            S.op("act", lambda e, kc=kc, sq=sq: e.activation(out=sq[:, :], in_=h_sb[:, kc, :], func=AF.Square),
                 reads=[h_c[kc]], writes=[sq])
            S.op("pe", lambda e, kc=kc, sq=sq: e.matmul(st[:, :], lhsT=ones[:, :], rhs=sq[:, :],
                                                        start=(kc == 0), stop=(kc == 7)),
                 reads=[ones, sq], writes=[st])
        rs = rstd_from(st, st[:, :], 1.0 / D)
        for kc in range(8):
            S.op("dve", lambda e, kc=kc: e.scalar_tensor_tensor(
                out=hn[:, kc, :], in0=h_sb[:, kc, :], scalar=C(gain_name, kc), in1=rs[:, :],
                op0=ALU.mult, op1=ALU.mult), reads=[h_c[kc], cols, rs], writes=[hn])

    def load_h(src_ap, src_buf, t):
        for kc in range(8):
            S.op("sp", lambda e, kc=kc: e.dma_start(out=h_sb[:, kc, :],
                                                    in_=src_ap[kc * 128:(kc + 1) * 128, t * TW:(t + 1) * TW]),
                 reads=[src_buf[kc]] if src_buf is not None else [], writes=[h_c[kc]], dma=h_c[kc])

    def proj_fm(wview, j0, kcn, rhs_b, rhs_of_kc, ps):
        for kc in range(kcn):
            S.op("pe", lambda e, kc=kc: e.matmul(ps[:, :], lhsT=wview[:, kc, j0:j0 + 128], rhs=rhs_of_kc(kc),
                                                 start=(kc == 0), stop=(kc == kcn - 1)),
                 reads=[rhs_b], writes=[ps])

    def group_norm_to(ps, gain_ap, out_b, out_ap, group, parts=128):
        sq = sqr.next()
        S.op("act", lambda e: e.activation(out=sq[0:parts, :], in_=ps[0:parts, :], func=AF.Square),
             reads=[ps], writes=[sq])
        st = str_.next()
        lhs = bones if group == 64 else ones
        S.op("pe", lambda e: e.matmul(st[0:parts, :], lhsT=lhs[0:parts, 0:parts], rhs=sq[0:parts, :], start=True, stop=True),
             reads=[lhs, sq], writes=[st])
        rs = rstd_from(st, st[0:parts, :], 1.0 / group, parts=parts)
        S.op("dve", lambda e: e.scalar_tensor_tensor(out=out_ap, in0=ps[0:parts, :], scalar=gain_ap, in1=rs[0:parts, :],
                                                     op0=ALU.mult, op1=ALU.mult),
             reads=[ps, cols, rs], writes=[out_b])

    def group_norm_multi(entries, group):
        lhs = bones if group == 64 else ones
        sqs, sts, lns, rss = [], [], [], []
        for ps, _, _, _ in entries:
            sq = sqr.next()
            sqs.append(sq)
            S.op("act", lambda e, sq=sq, ps=ps: e.activation(out=sq[:, :], in_=ps[:, :], func=AF.Square),
                 reads=[ps], writes=[sq])
        for sq in sqs:
            st = str_.next()
            sts.append(st)
            S.op("pe", lambda e, sq=sq, st=st: e.matmul(st[:, :], lhsT=lhs[:, :], rhs=sq[:, :], start=True, stop=True),
                 reads=[lhs, sq], writes=[st])
        for st in sts:
            ln = lnr.next()
            lns.append(ln)
            S.op("act", lambda e, ln=ln, st=st: e.activation(out=ln[:, :], in_=st[:, :], func=AF.Ln, bias=epsc[:, :],
                                                             scale=1.0 / group), reads=[st, epsc], writes=[ln])
        for ln in lns:
            rs = rsr.next()
            rss.append(rs)
            S.op("act", lambda e, ln=ln, rs=rs: e.activation(out=rs[:, :], in_=ln[:, :], func=AF.Exp, scale=-0.5),
                 reads=[ln], writes=[rs])
        for (ps, gain_ap, out_b, out_ap), rs in zip(entries, rss):
            S.op("dve", lambda e, ps=ps, gain_ap=gain_ap, out_ap=out_ap, rs=rs: e.scalar_tensor_tensor(
                out=out_ap, in0=ps[:, :], scalar=gain_ap, in1=rs[:, :], op0=ALU.mult, op1=ALU.mult),
                reads=[ps, cols, rs], writes=[out_b])

    S.op("sp", lambda e: e.dma_start(out=cols[:, :], in_=colsd), writes=[cols], dma=cols)
    S.op("sp", lambda e: e.dma_start(out=rows[:, :], in_=rowsd), writes=[rows], dma=rows)
    S.op("pool", lambda e: e.dma_start(out=prot[:, :], in_=protd), writes=[prot], dma=prot)
    S.op("dve", lambda e: e.memset(ones[:, :], 1.0), writes=[ones])
    S.op("dve", lambda e: e.memset(bones[:, :], 0.0), writes=[bones])
    S.op("dve", lambda e: e.memset(bones[0:64, 0:64], 1.0), writes=[bones])
    S.op("dve", lambda e: e.memset(bones[64:128, 64:128], 1.0), writes=[bones])
    S.op("dve", lambda e: e.memset(epsc[:, :], EPS), writes=[epsc])
    S.op("pool", lambda e: e.memset(ident[:, :], 0.0), writes=[ident])
    S.op("pool", lambda e: e.affine_select(out=ident[:, :], in_=ident[:, :], pattern=[[-1, 128]],
                                           compare_op=ALU.not_equal, fill=1.0, base=0, channel_multiplier=1),
         reads=[ident], writes=[ident])
    S.op("pool", lambda e: e.memset(tri[:, :], 1.0), writes=[tri])
    S.op("pool", lambda e: e.affine_select(out=tri[:, :], in_=tri[:, :], pattern=[[1, 128]],
                                           compare_op=ALU.is_ge, fill=0.0, base=0, channel_multiplier=-1),
         reads=[tri], writes=[tri])
    for r_ in vst.bufs:
        S.op("dve", lambda e, r_=r_: e.memset(r_[:, :, :, :], 1.0), writes=[r_])

    nA = min(L, 2)
    for r_ in qring.bufs:
        S.op("dve", lambda e, r_=r_: e.memset(r_[64:128, :], 0.0), writes=[r_])
    for r_ in qpring.bufs:
        S.op("dve", lambda e, r_=r_: e.memset(r_[0:64, :], 0.0), writes=[r_])

    if nA > 0:
        class V3:
            def __init__(self, b):
                self.b = b

            def __getitem__(self, idx):
                return self.b[:, 0:256].rearrange("p (d q) -> p d q", d=2)[idx]

        posi = itmp
        pci = S.sbuf("pci", [128, 2], I32)
        posf = sgp.bufs[0]
        pcf = S.sbuf("pcf", [128, 2], F32)
        dt_b = cpool.bufs[0]
        dt_ = V3(dt_b)
        ge0_b = cpool.bufs[1]
        ge0 = V3(ge0_b)
        ge_bufs = [cpool.bufs[2], cpool.bufs[3]]
        ge = Ring([V3(b) for b in ge_bufs])
        dtab = S.sbuf("dtab", [128, 256], F32)
        nb31 = S.sbuf("nb31", [128, 8], F32)
        S.op("sp", lambda e: e.dma_start(out=posi[:, 0:256], in_=posb[:, 0:256]), writes=[posi], dma=posi)
        S.op("sp", lambda e: e.dma_start(out=pci[:, :], in_=poscol), writes=[pci], dma=pci)
        S.op("dve", lambda e: e.tensor_copy(out=posf[:, 0:256], in_=posi[:, 0:256]), reads=[posi], writes=[posf])
        S.op("dve", lambda e: e.tensor_copy(out=pcf[:, :], in_=pci[:, :]), reads=[pci], writes=[pcf])
        S.op("dve", lambda e: e.tensor_scalar(out=dt_[:, :, :], in0=posf[:, 0:256].rearrange("p (d q) -> p d q", d=2),
                                              scalar1=pcf[:, 0:1], scalar2=None, op0=ALU.subtract),
             reads=[posf, pcf], writes=[dt_b])
        tb = cm.m["table"]
        S.op("dve", lambda e: e.tensor_copy(out=dtab[:, 0:8], in_=cols[:, tb:tb + 8]), reads=[cols], writes=[dtab])
        S.op("dve", lambda e: e.tensor_tensor(out=dtab[:, 8:256], in0=cols[:, tb + 8:tb + 256], in1=cols[:, tb:tb + 248],
                                              op=ALU.subtract), reads=[cols], writes=[dtab])
        S.op("dve", lambda e: e.tensor_scalar(out=nb31[:, :], in0=C("b31", 0, 8), scalar1=-1.0, scalar2=None, op0=ALU.mult),
             reads=[cols], writes=[nb31])
        S.op("dve", lambda e: e.tensor_scalar(out=ge0[:, :, :], in0=dt_[:, :, :], scalar1=0.0, scalar2=None, op0=ALU.is_ge),
             reads=[dt_b], writes=[ge0_b])
        for h in range(8):
            S.op("dve", lambda e, h=h: e.tensor_scalar(out=Rt[:, :, h, :], in0=ge0[:, :, :], scalar1=dtab[:, h:h + 1],
                                                       scalar2=None, op0=ALU.mult), reads=[ge0_b, dtab], writes=[Rt])
        for b in range(1, 32):
            g = ge.next()
            S.op("dve", lambda e, g=g, b=b: e.tensor_scalar(out=g[:, :, :], in0=dt_[:, :, :], scalar1=float(lo_thr[b]),
                                                            scalar2=None, op0=ALU.is_ge), reads=[dt_b], writes=[g.b])
            for h in range(8):
                S.op("dve", lambda e, g=g, b=b, h=h: e.scalar_tensor_tensor(
                    out=Rt[:, :, h, :], in0=g[:, :, :], scalar=dtab[:, b * 8 + h:b * 8 + h + 1], in1=Rt[:, :, h, :],
                    op0=ALU.mult, op1=ALU.add), reads=[g.b, dtab, Rt], writes=[Rt])
        for h in range(8):
            S.op("act", lambda e, h=h: e.activation(out=Rt[:, :, h, :], in_=Rt[:, :, h, :], func=AF.Exp,
                                                    bias=nb31[:, h:h + 1], scale=1.0), reads=[Rt, nb31], writes=[Rt])
            S.op("dve", lambda e, h=h: e.tensor_tensor(out=Rt[:, :, h, :], in0=Rt[:, :, h, :], in1=ge0[:, :, :], op=ALU.mult),
                 reads=[Rt, ge0_b], writes=[Rt])
        for l in range(nA):
            lam_init = 0.8 - 0.6 * math.exp(-0.3 * l)
            sm = smr.next()
            for i, (a, b) in enumerate((("q1", "k1"), ("q2", "k2"))):
                ra = rm.m["lam_%s%d" % (a, l)]
                rb = rm.m["lam_%s%d" % (b, l)]
                S.op("dve", lambda e, ra=ra, rb=rb: e.tensor_tensor(out=junk[:, 0:64], in0=rows[:, ra:ra + 64],
                                                                    in1=rows[:, rb:rb + 64], op=ALU.mult),
                     reads=[rows], writes=[junk])
                S.op("act", lambda e, i=i, sm=sm: e.activation(out=junk[:, 64:128], in_=junk[:, 0:64], func=AF.Identity,
                                                               accum_out=sm[:, i:i + 1]), reads=[junk], writes=[junk, sm])
            S.op("act", lambda e, sm=sm: e.activation(out=sm[:, 2:4], in_=sm[:, 0:2], func=AF.Exp), reads=[sm], writes=[sm])
            S.op("dve", lambda e, sm=sm, l=l: e.tensor_tensor(out=lamt[:, 2 * l:2 * l + 1], in0=sm[:, 3:4], in1=sm[:, 2:3],
                                                              op=ALU.subtract), reads=[sm], writes=[lamt])
            S.op("dve", lambda e, l=l, lam_init=lam_init: e.tensor_scalar(
                out=lamt[:, 2 * l:2 * l + 1], in0=lamt[:, 2 * l:2 * l + 1], scalar1=-lam_init, scalar2=None, op0=ALU.add),
                reads=[lamt], writes=[lamt])
            sg0 = rm.m["sub_gain%d" % l]
            S.op("dve", lambda e, l=l, sg0=sg0, lam_init=lam_init: e.tensor_scalar(
                out=subg[:, l, :], in0=rows[:, sg0:sg0 + 128], scalar1=1.0 - lam_init, scalar2=None, op0=ALU.mult),
                reads=[rows], writes=[subg])

    ST = [[pb[0], pb[1]], [pb[2], pb[3]]]
    ACC = [pb[4], pb[5], pb[6]]
    TP = pb[7]

    def attention_head(kind, l, h):
        nsub = 2 if kind == "A" else 1
        sc = 0.125 if kind == "A" else (192.0 ** -0.5)
        S.op("sp", lambda e: e.dma_start(out=attk[:, :], in_=KA[h]), reads=[KA_r[h][t] for t in range(NT)],
             writes=[attk], dma=attk)
        S.op("sp", lambda e: e.dma_start(out=attv[:, :, :], in_=VV[h]), reads=[VV_r[h][t] for t in range(NT)],
             writes=[attv], dma=attv)
        steps = [(t, j) for t in range(NT) for j in range(4 * t + 4)]
        LA = 1 if kind == "A" else 2
        DEFER = 0 if kind == "A" else 3

        def st_of(c, i):
            return ST[c][i % 2] if kind == "A" else pb[i % 3]

        def acc_of(t, idx):
            if kind == "A":
                return ACC[idx // 3], (idx % 3) * 129
            base = 3 + 2 * (t % 2)
            return pb[base + idx // 3], (idx % 3) * 129

        qtiles = {}
        state = {"touched": set()}

        def load_q(t):
            attq = qring.next()
            attqp = qpring.next()
            if kind == "A":
                S.op("sp", lambda e, attq=attq, t=t: e.dma_start(out=attq[0:64, :], in_=QA[h, 0:64, t * TW:(t + 1) * TW]),
                     reads=[QA_r[h][t]], writes=[attq], dma=attq)
                S.op("sp", lambda e, attqp=attqp, t=t: e.dma_start(out=attqp[64:128, :], in_=QA[h, 64:128, t * TW:(t + 1) * TW]),
                     reads=[QA_r[h][t]], writes=[attqp], dma=attqp)
            else:
                S.op("sp", lambda e, attq=attq, t=t: e.dma_start(out=attq[:, :], in_=QA[h, :, t * TW:(t + 1) * TW]),
                     reads=[QA_r[h][t]], writes=[attq], dma=attq)
                r0 = 64 * (h % 2)
                S.op("sp", lambda e, attqp=attqp, t=t, r0=r0: e.dma_start(
                    out=attqp[0:64, :], in_=QP[h // 2, r0:r0 + 64, t * TW:(t + 1) * TW]),
                    reads=[QP_r[h // 2][t]], writes=[attqp], dma=attqp)
            qtiles[t] = (attq, attqp)

        def scores(i):
            t, j = steps[i]
            if j == 0 and t + 1 < NT:
                load_q(t + 1)
            attq, attqp = qtiles[t]
            b0 = max(0, j - 4 * t)
            n = TW - 128 * b0
            q0 = 128 * b0
            for c in range(nsub):
                st = st_of(c, i)
                if kind == "A":
                    qq = attq if c == 0 else attqp
                    S.op("pe", lambda e, st=st, j=j, n=n, q0=q0, qq=qq: e.matmul(
                        st[:, 0:n], lhsT=attk[:, j * 128:(j + 1) * 128], rhs=qq[:, q0:q0 + n], start=True, stop=True),
                        reads=[attk, qq], writes=[st])
                else:
                    S.op("pe", lambda e, st=st, j=j, n=n, q0=q0, attq=attq: e.matmul(
                        st[:, 0:n], lhsT=attk[:, j * 128:(j + 1) * 128], rhs=attq[:, q0:q0 + n],
                        start=True, stop=False), reads=[attk, attq], writes=[st])
                    S.op("pe", lambda e, st=st, j=j, n=n, q0=q0, attqp=attqp: e.matmul(
                        st[:, 0:n], lhsT=attkp[:, j * 128:(j + 1) * 128], rhs=attqp[:, q0:q0 + n],
                        start=False, stop=True), reads=[attkp, attqp], writes=[st])

        def probs_pv(i):
            t, j = steps[i]
            touched = state["touched"]
            if j == 0:
                touched.clear()
            b0 = max(0, j - 4 * t)
            n = TW - 128 * b0
            pts = []
            for c in range(nsub):
                st = st_of(c, i)
                pt = ppool.next()
                pts.append(pt)
                if kind == "A":
                    S.op("act", lambda e, pt=pt, st=st, n=n: e.activation(
                        out=pt[:, 0:n], in_=st[:, 0:n], func=AF.Exp, bias=C("b31", h), scale=sc),
                        reads=[st, cols], writes=[pt])
                else:
                    S.op("act", lambda e, pt=pt, st=st, n=n: e.activation(
                        out=pt[:, 0:n], in_=st[:, 0:n], func=AF.Exp, scale=sc), reads=[st], writes=[pt])
                for blk in range(b0, 4):
                    dd = 4 * t + blk - j
                    cs = slice((blk - b0) * 128, (blk - b0 + 1) * 128)
                    if kind == "A" and dd in (0, 1):
                        S.op("dve", lambda e, pt=pt, cs=cs, dd=dd: e.tensor_tensor(
                            out=pt[:, cs], in0=pt[:, cs], in1=Rt[:, dd, h, :], op=ALU.mult),
                            reads=[pt, Rt], writes=[pt])
                    elif kind == "B" and dd == 0:
                        S.op("dve", lambda e, pt=pt, cs=cs: e.tensor_tensor(
                            out=pt[:, cs], in0=pt[:, cs], in1=tri[:, :], op=ALU.mult),
                            reads=[pt, tri], writes=[pt])
            for c in range(nsub):
                pt = pts[c]
                for blk in range(b0, 4):
                    idx = c * 4 + blk
                    bank, off = acc_of(t, idx)
                    first = bank.name not in touched
                    touched.add(bank.name)
                    cs = slice((blk - b0) * 128, (blk - b0 + 1) * 128)
                    S.op("pe", lambda e, pt=pt, cs=cs, bank=bank, off=off, first=first, j=j, t=t, blk=blk: e.matmul(
                        bank[:, off:off + 129], lhsT=pt[:, cs], rhs=attv[:, j, :], start=first,
                        stop=(j == 4 * t + blk), skip_group_check=True), reads=[pt, attv], writes=[bank])

        def finalize(t):
            ost = otst.next()
            sms = [smr.next() for _ in range(4)]
            osbs = [o_sb.next() for _ in range(4)]
            acc0 = [acc_of(t, blk) for blk in range(4)]
            for blk in range(4):
                sm, (a0b, a0o) = sms[blk], acc0[blk]
                S.op("dve", lambda e, sm=sm, a0b=a0b, a0o=a0o: e.reciprocal(out=sm[:, 0:1], in_=a0b[:, a0o + 128:a0o + 129]),
                     reads=[a0b], writes=[sm])
            for blk in range(4):
                sm, osb, (a0b, a0o) = sms[blk], osbs[blk], acc0[blk]
                S.op("dve", lambda e, sm=sm, a0b=a0b, a0o=a0o, osb=osb: e.tensor_scalar(
                    out=osb[:, :], in0=a0b[:, a0o:a0o + 128], scalar1=sm[:, 0:1], scalar2=None, op0=ALU.mult),
                    reads=[a0b, sm], writes=[osb])
            srcs = osbs
            if kind == "A":
                onbs = [on_sb.next() for _ in range(4)]
                acc1 = [acc_of(t, 4 + blk) for blk in range(4)]
                for blk in range(4):
                    sm, (a1b, a1o) = sms[blk], acc1[blk]
                    S.op("dve", lambda e, sm=sm, a1b=a1b, a1o=a1o: e.reciprocal(out=sm[:, 1:2], in_=a1b[:, a1o + 128:a1o + 129]),
                         reads=[a1b], writes=[sm])
                for blk in range(4):
                    sm = sms[blk]
                    S.op("dve", lambda e, sm=sm: e.tensor_tensor(out=sm[:, 1:2], in0=sm[:, 1:2], in1=lamt[:, 2 * l:2 * l + 1],
                                                                 op=ALU.mult), reads=[sm, lamt], writes=[sm])
                for blk in range(4):
                    sm, osb, (a1b, a1o) = sms[blk], osbs[blk], acc1[blk]
                    S.op("dve", lambda e, sm=sm, a1b=a1b, a1o=a1o, osb=osb: e.scalar_tensor_tensor(
                        out=osb[:, :], in0=a1b[:, a1o:a1o + 128], scalar=sm[:, 1:2], in1=osb[:, :],
                        op0=ALU.mult, op1=ALU.add), reads=[a1b, sm, osb], writes=[osb])
                for blk in range(4):
                    sm, osb = sms[blk], osbs[blk]
                    S.op("act", lambda e, sm=sm, osb=osb: e.activation(out=junk[:, :], in_=osb[:, :], func=AF.Square,
                                                                       accum_out=sm[:, 2:3]), reads=[osb], writes=[junk, sm])
                for blk in range(4):
                    sm = sms[blk]
                    S.op("act", lambda e, sm=sm: e.activation(out=sm[:, 3:4], in_=sm[:, 2:3], func=AF.Ln, bias=epsc[:, :],
                                                              scale=1.0 / 128), reads=[sm, epsc], writes=[sm])
                for blk in range(4):
                    sm = sms[blk]
                    S.op("act", lambda e, sm=sm: e.activation(out=sm[:, 4:5], in_=sm[:, 3:4], func=AF.Exp, scale=-0.5),
                         reads=[sm], writes=[sm])
                for blk in range(4):
                    sm, osb, onb = sms[blk], osbs[blk], onbs[blk]
                    S.op("dve", lambda e, sm=sm, osb=osb, onb=onb: e.scalar_tensor_tensor(
                        out=onb[:, :], in0=osb[:, :], scalar=sm[:, 4:5], in1=subg[:, l, :], op0=ALU.mult, op1=ALU.mult),
                        reads=[osb, sm, subg], writes=[onb])
                srcs = onbs
            return ost, srcs

        def finalize2(t, ost, srcs):
            for blk in range(4):
                src = srcs[blk]
                S.op("pe", lambda e, src=src, blk=blk: e.transpose(TP[:, blk * 128:(blk + 1) * 128], src[:, :], ident[:, :]),
                     reads=[src, ident], writes=[TP])
            S.op("act", lambda e, ost=ost: e.activation(out=ost[:, :], in_=TP[:, :], func=AF.Identity), reads=[TP], writes=[ost])
            S.op("sp", lambda e, ost=ost, t=t: e.dma_start(out=OT[h, :, t * TW:(t + 1) * TW], in_=ost[:, :]),
                 reads=[ost], writes=[OT_r[h][t]], dma=ost)

        load_q(0)
        for k in range(min(LA, len(steps))):
            scores(k)
        pending = []
        for i in range(len(steps)):
            if i + LA < len(steps):
                scores(i + LA)
            probs_pv(i)
            t, j = steps[i]
            if j == 4 * t + 3:
                ost, srcs = finalize(t)
                pending.append((i + DEFER, t, ost, srcs))
            while pending and pending[0][0] <= i:
                _, tt, ost, srcs = pending.pop(0)
                finalize2(tt, ost, srcs)
        for _, tt, ost, srcs in pending:
            finalize2(tt, ost, srcs)

    ot_pref = set()
    def dense_tile(l, t, H, w_o_ap, hsrc_ap, hsrc_b, hdst_ap, hdst_b):
        def load_ot(tt, half):
            S.op("sp", lambda e: e.dma_start(
                out=ot_sb[:, :, :], in_=OT[8 * half:8 * half + 8, :, tt * TW:(tt + 1) * TW].rearrange("h p s -> p h s")),
                reads=[OT_r[8 * half + i][tt] for i in range(8)], writes=[ot_sb], dma=ot_sb)

        hdst_cb = hdst_b[t] if hdst_b is not None else [Buf("odst_%d_%d_%d" % (l, t, k)) for k in range(8)]
        if (l, t) not in ot_pref:
            load_ot(t, 0)
        load_h(hsrc_ap, hsrc_b, t)
        for half in range(H // 8):
            if half > 0:
                load_ot(t, half)
            for mg in range(2):
                slot, wv = load_w(("wo", l, half, mg), w_o_ap[half * 1024:(half + 1) * 1024, mg * 512:(mg + 1) * 512], 8, 512)
                for mj in range(4):
                    m = mg * 4 + mj
                    ps = mmr.next()
                    for hh in range(8):
                        S.op("pe", lambda e, hh=hh, mj=mj, ps=ps, wv=wv: e.matmul(
                            ps[:, :], lhsT=wv[:, hh, mj * 128:(mj + 1) * 128], rhs=ot_sb[:, hh, :],
                            start=(hh == 0), stop=(hh == 7)), reads=[slot, ot_sb], writes=[ps])
                    S.op("dve", lambda e, m=m, ps=ps: e.tensor_tensor(out=h_sb[:, m, :], in0=ps[:, :], in1=h_sb[:, m, :],
                                                                      op=ALU.add), reads=[ps, h_c[m]], writes=[h_c[m]])
        rmsnorm("ffn_norm%d" % l)
        w_in = ffn_w_in[l]
        for ig in range(0, NFC, 2):
            slot, wv = load_w2(("win", l, ig), w_in[:, ig * 128:(ig + 2) * 128], w_in[:, DFF + ig * 128:DFF + (ig + 2) * 128], 8, 256, 256)
            for ii in range(2):
                i = ig + ii
                pss, uxs, cbs, ccs = [], [], [], []
                for part in range(2):
                    cc = part * NFC + i
                    ps = mmr.next()
                    proj_w(slot, wv, part * 256 + ii * 128, 8, hn, lambda kc: hn[:, kc, :], ps)
                    pss.append(ps)
                    ccs.append(cc)
                for part in range(2):
                    ux = uext.next()
                    uxs.append(ux)
                    S.op("act", lambda e, ux=ux, ps=pss[part]: e.activation(out=ux[:, 2:TW + 2], in_=ps[:, :], func=AF.Identity),
                         reads=[pss[part]], writes=[ux])
                for part in range(2):
                    ux, cc = uxs[part], ccs[part]
                    S.op("dve", lambda e, ux=ux, cc=cc: e.tensor_copy(out=ux[:, 0:2], in_=utail_t[:, cc, :]),
                         reads=[utail[cc]], writes=[ux])
                for part in range(2):
                    ux, cc = uxs[part], ccs[part]
                    S.op("dve", lambda e, ux=ux, cc=cc: e.tensor_copy(out=utail_t[:, cc, :], in_=ux[:, TW:TW + 2]),
                         reads=[ux], writes=[utail[cc]])
                for part in range(2):
                    cc = ccs[part]
                    cb = cpool.next()
                    cbs.append(cb)
                    S.op("act", lambda e, cb=cb, cc=cc, ps=pss[part]: e.activation(
                        out=cb[:, :], in_=ps[:, :], func=AF.Identity, scale=C("conv_w%d_2" % l, cc), bias=C("conv_b%d" % l, cc)),
                        reads=[pss[part], cols], writes=[cb])
                for jj in (1, 0):
                    for part in range(2):
                        ux, cc, cb = uxs[part], ccs[part], cbs[part]
                        S.op("dve", lambda e, ux=ux, cb=cb, cc=cc, jj=jj: e.scalar_tensor_tensor(
                            out=cb[:, :], in0=ux[:, jj:TW + jj], scalar=C("conv_w%d_%d" % (l, jj), cc), in1=cb[:, :],
                            op0=ALU.mult, op1=ALU.add), reads=[ux, cols, cb], writes=[cb])
                sg = sgp.next()
                S.op("act", lambda e, sg=sg, cg=cbs[1]: e.activation(out=sg[:, :], in_=cg[:, :], func=AF.Silu),
                     reads=[cbs[1]], writes=[sg])
                S.op("dve", lambda e, sg=sg, ca=cbs[0], i=i: e.tensor_tensor(out=z_sb[:, i, :], in0=sg[:, :], in1=ca[:, :],
                                                                             op=ALU.mult), reads=[sg, cbs[0]], writes=[z_sb])
        w_out = ffn_w_out[l]
        for m in range(8):
            slot, wv = load_w(("wout", l, m), w_out[:, m * 128:(m + 1) * 128], NFC, 128)
            ps = mmr.next()
            for i in range(NFC):
                S.op("pe", lambda e, i=i, ps=ps, wv=wv: e.matmul(ps[:, :], lhsT=wv[:, i, :], rhs=z_sb[:, i, :],
                                                                 start=(i == 0), stop=(i == NFC - 1)),
                     reads=[slot, z_sb], writes=[ps])
            S.op("dve", lambda e, m=m, ps=ps: e.tensor_tensor(out=h_sb[:, m, :], in0=ps[:, :], in1=h_sb[:, m, :], op=ALU.add),
                 reads=[ps, h_c[m]], writes=[h_c[m]])
        rmsnorm("ple_norm%d" % l)
        S.op("pool", lambda e: e.dma_start(out=pt_sb[:, :, :],
                                           in_=pT[l, :, t * TW:(t + 1) * TW].rearrange("(k p) s -> p k s", p=128)),
             writes=[pt_sb], dma=pt_sb)
        pslot, pwv = load_w(("pproj", l), ple_w_proj[l], 2, 1024)
        for mg in range(2):
            slot, wv = load_w(("pgate", l, mg), ple_w_gate[l][:, mg * 512:(mg + 1) * 512], 8, 512)
            for mj in range(4):
                m = mg * 4 + mj
                psg = mmr.next()
                for kc in range(8):
                    S.op("pe", lambda e, kc=kc, mj=mj, psg=psg, wv=wv: e.matmul(
                        psg[:, :], lhsT=wv[:, kc, mj * 128:(mj + 1) * 128], rhs=hn[:, kc, :],
                        start=(kc == 0), stop=(kc == 7)), reads=[slot, hn], writes=[psg])
                sg = sgp.next()
                S.op("act", lambda e, sg=sg, psg=psg: e.activation(out=sg[:, :], in_=psg[:, :], func=AF.Sigmoid),
                     reads=[psg], writes=[sg])
                psp = mmr.next()
                for kc in range(2):
                    S.op("pe", lambda e, kc=kc, m=m, psp=psp, pwv=pwv: e.matmul(
                        psp[:, :], lhsT=pwv[:, kc, m * 128:(m + 1) * 128], rhs=pt_sb[:, kc, :],
                        start=(kc == 0), stop=(kc == 1)), reads=[pslot, pt_sb], writes=[psp])
                cb = cpool.next()
                S.op("dve", lambda e, cb=cb, psp=psp, sg=sg: e.tensor_tensor(out=cb[:, :], in0=psp[:, :], in1=sg[:, :],
                                                                             op=ALU.mult), reads=[psp, sg], writes=[cb])
                S.op("dve", lambda e, cb=cb, m=m: e.tensor_tensor(out=h_sb[:, m, :], in0=cb[:, :], in1=h_sb[:, m, :],
                                                                  op=ALU.add), reads=[cb, h_c[m]], writes=[h_c[m]])
                if m == 6 and t + 1 < NT:
                    load_ot(t + 1, 0)
                    ot_pref.add((l, t + 1))
                S.op("sp", lambda e, m=m: e.dma_start(out=hdst_ap[m * 128:(m + 1) * 128, t * TW:(t + 1) * TW], in_=h_sb[:, m, :]),
                     reads=[h_c[m]], writes=[hdst_cb[m]], dma=h_c[m])

    def proj_w(slot, wview, j0, kcn, rhs_b, rhs_of_kc, ps, parts=128, mcols=128):
        for kc in range(kcn):
            S.op("pe", lambda e, kc=kc: e.matmul(ps[0:mcols, :], lhsT=wview[:, kc, j0:j0 + mcols], rhs=rhs_of_kc(kc),
                                                 start=(kc == 0), stop=(kc == kcn - 1)),
                 reads=[slot, rhs_b], writes=[ps])

    _dense_slot = {}

    def a_phase1(l, t, hsrc_ap, hsrc_b):
        load_h(hsrc_ap, hsrc_b, t)
        rmsnorm("attn_norm%d" % l)
        wq = a_w_qkv[l]
        for grp in range(4):
            slot, wv = load_w(("qkv", l, grp), wq[:, grp * 512:(grp + 1) * 512], 8, 512)
            for jp in range(2):
                entries, outs = [], []
                for j in (2 * jp, 2 * jp + 1):
                    oc = grp * 4 + j
                    ps = mmr.next()
                    proj_w(slot, wv, j * 128, 8, hn, lambda kc: hn[:, kc, :], ps)
                    sb = stg.next()
                    gname = ("a_q_norm%d" if oc < 8 else "a_k_norm%d") % l
                    entries.append((ps, C(gname), sb, sb[:, :]))
                    outs.append((sb, oc))
                group_norm_multi(entries, 64)
                for sb, oc in outs:
                    dst, dreg = (QA, QA_r) if oc < 8 else (KA, KA_r)
                    hh = oc % 8
                    S.op("sp", lambda e, sb=sb, dst=dst, hh=hh: e.dma_start(out=dst[hh, :, t * TW:(t + 1) * TW], in_=sb[:, :]),
                         reads=[sb], writes=[dreg[hh][t]], dma=sb)
        for half in range(2):
            slot, wv = load_w(("qkvv", l, half), wq[:, 2048 + half * 512:2048 + (half + 1) * 512], 8, 512)
            vs = vst.next()
            for blk in range(4):
                ps = mmr.next()
                for kc in range(8):
                    S.op("pe", lambda e, kc=kc, blk=blk, ps=ps, wv=wv: e.matmul(
                        ps[:, :], lhsT=hn[:, kc, blk * 128:(blk + 1) * 128], rhs=wv[:, kc, :],
                        start=(kc == 0), stop=(kc == 7)), reads=[slot, hn], writes=[ps])
                S.op("dve", lambda e, blk=blk, ps=ps, vs=vs: e.tensor_copy(
                    out=vs[:, :, blk, 0:128], in_=ps[:, :].rearrange("p (h d) -> p h d", h=4)), reads=[ps], writes=[vs])
            S.op("sp", lambda e, half=half, vs=vs: e.dma_start(
                out=VV[4 * half:4 * half + 4, :, 4 * t:4 * t + 4, :].rearrange("h p b e -> p h b e"), in_=vs[:, :, :, :]),
                reads=[vs], writes=[VV_r[4 * half + i][t] for i in range(4)], dma=vs)

    def rotary_tables():
        TWO_PI = 2.0 * math.pi
        C1 = 6.28125
        C2 = TWO_PI - C1
        pi_t = itmp
        ang = cpool.bufs[0]
        kf = cpool.bufs[1]
        ki = itmp2
        mk = cpool.bufs[2]
        for t in range(NT):
            S.op("sp", lambda e, t=t: e.dma_start(out=pi_t[:, :], in_=posb[:, t * TW:(t + 1) * TW]), writes=[pi_t], dma=pi_t)
            S.op("dve", lambda e: e.tensor_copy(out=ang[:, :], in_=pi_t[:, :]), reads=[pi_t], writes=[ang])
            S.op("dve", lambda e: e.tensor_scalar(out=ang[:, :], in0=ang[:, :], scalar1=C("invfreq"), scalar2=None,
                                                  op0=ALU.mult), reads=[ang, cols], writes=[ang])
            S.op("dve", lambda e: e.tensor_scalar(out=ki[:, :], in0=ang[:, :], scalar1=1.0 / TWO_PI, scalar2=None,
                                                  op0=ALU.mult), reads=[ang], writes=[ki])
            S.op("dve", lambda e: e.tensor_copy(out=kf[:, :], in_=ki[:, :]), reads=[ki], writes=[kf])
            S.op("dve", lambda e: e.scalar_tensor_tensor(out=ang[:, :], in0=kf[:, :], scalar=-C1, in1=ang[:, :],
                                                         op0=ALU.mult, op1=ALU.add), reads=[kf, ang], writes=[ang])
            S.op("dve", lambda e: e.scalar_tensor_tensor(out=ang[:, :], in0=kf[:, :], scalar=-C2, in1=ang[:, :],
                                                         op0=ALU.mult, op1=ALU.add), reads=[kf, ang], writes=[ang])
            S.op("dve", lambda e: e.tensor_scalar(out=mk[:, :], in0=ang[:, :], scalar1=math.pi, scalar2=None,
                                                  op0=ALU.is_gt), reads=[ang], writes=[mk])
            S.op("dve", lambda e: e.scalar_tensor_tensor(out=ang[:, :], in0=mk[:, :], scalar=-TWO_PI, in1=ang[:, :],
                                                         op0=ALU.mult, op1=ALU.add), reads=[mk, ang], writes=[ang])
            S.op("dve", lambda e: e.tensor_scalar(out=mk[:, :], in0=ang[:, :], scalar1=-math.pi, scalar2=None,
                                                  op0=ALU.is_lt), reads=[ang], writes=[mk])
            S.op("dve", lambda e: e.scalar_tensor_tensor(out=ang[:, :], in0=mk[:, :], scalar=TWO_PI, in1=ang[:, :],
                                                         op0=ALU.mult, op1=ALU.add), reads=[mk, ang], writes=[ang])
            S.op("dve", lambda e: e.tensor_scalar(out=ang[:, :], in0=ang[:, :], scalar1=-3.1415925, scalar2=3.1415925,
                                                  op0=ALU.max, op1=ALU.min), reads=[ang], writes=[ang])
            S.op("act", lambda e: e.activation(out=cs_sb[:, 1, :], in_=ang[:, :], func=AF.Sin), reads=[ang], writes=[cs_sb])
            S.op("dve", lambda e: e.scalar_tensor_tensor(out=kf[:, :], in0=ang[:, :], scalar=-1.0, in1=ang[:, :],
                                                         op0=ALU.mult, op1=ALU.max), reads=[ang], writes=[kf])
            S.op("dve", lambda e: e.tensor_scalar(out=kf[:, :], in0=kf[:, :], scalar1=-1.0, scalar2=math.pi / 2,
                                                  op0=ALU.mult, op1=ALU.add), reads=[kf], writes=[kf])
            S.op("act", lambda e: e.activation(out=cs_sb[:, 0, :], in_=kf[:, :], func=AF.Sin), reads=[kf], writes=[cs_sb])
            S.op("sp", lambda e, t=t: [e.dma_start(out=COS[:, t * TW:(t + 1) * TW], in_=cs_sb[:, 0, :]),
                                       e.dma_start(out=SIN[:, t * TW:(t + 1) * TW], in_=cs_sb[:, 1, :])],
                 reads=[cs_sb], writes=[CS_r[t]], dma=cs_sb, ninc=2)

    def load_cs(t):
        S.op("sp", lambda e: [e.dma_start(out=cs_sb[:, 0, :], in_=COS[:, t * TW:(t + 1) * TW]),
                              e.dma_start(out=cs_sb[:, 1, :], in_=SIN[:, t * TW:(t + 1) * TW])],
             reads=[CS_r[t]], writes=[cs_sb], dma=cs_sb, ninc=2)

    def apply_rotary(xb, parts, out_b, out_ap):
        rp = mmr.next()
        S.op("pe", lambda e: e.matmul(rp[0:parts, :], lhsT=prot[0:parts, 0:parts], rhs=xb[0:parts, :], start=True, stop=True),
             reads=[prot, xb], writes=[rp])
        cb = cpool.next()
        S.op("dve", lambda e: e.tensor_tensor(out=cb[0:parts, :], in0=rp[0:parts, :], in1=cs_sb[0:parts, 1, :], op=ALU.mult),
             reads=[rp, cs_sb], writes=[cb])
        cb2 = cpool.next()
        S.op("dve", lambda e: e.tensor_tensor(out=cb2[0:parts, :], in0=xb[0:parts, :], in1=cs_sb[0:parts, 0, :], op=ALU.mult),
             reads=[xb, cs_sb], writes=[cb2])
        S.op("dve", lambda e: e.tensor_tensor(out=out_ap, in0=cb[0:parts, :], in1=cb2[0:parts, :], op=ALU.add),
             reads=[cb, cb2], writes=[out_b])

    def shared_kv_tile(t, hsrc_ap, hsrc_b):
        load_h(hsrc_ap, hsrc_b, t)
        load_cs(t)
        rmsnorm("kv_norm")
        slot, wv = load_w(("dkv",), w_dkv, 8, 320)
        pss = []
        for c in range(2):
            ps = mmr.next()
            proj_w(slot, wv, c * 128, 8, hn, lambda kc: hn[:, kc, :], ps)
            pss.append(ps)
        st = STAT
        for c in range(2):
            sq = sqr.next()
            S.op("act", lambda e, sq=sq, c=c: e.activation(out=sq[:, :], in_=pss[c][:, :], func=AF.Square),
                 reads=[pss[c]], writes=[sq])
            S.op("pe", lambda e, sq=sq, c=c: e.matmul(st[:, :], lhsT=ones[:, :], rhs=sq[:, :], start=(c == 0), stop=(c == 1)),
                 reads=[ones, sq], writes=[st])
        rs = rstd_from(st, st[:, :], 1.0 / 256)
        for c in range(2):
            S.op("dve", lambda e, c=c: e.scalar_tensor_tensor(out=ckvn[:, c, :], in0=pss[c][:, :], scalar=C("ckv_norm", c),
                                                              in1=rs[:, :], op0=ALU.mult, op1=ALU.mult),
                 reads=[pss[c], cols, rs], writes=[ckvn])
        ps = mmr.next()
        proj_w(slot, wv, 256, 8, hn, lambda kc: hn[:, kc, :], ps, mcols=64)
        sb = stg.next()
        group_norm_to(ps, cols[0:64, cm.m["k_pe_norm"]:cm.m["k_pe_norm"] + 1], sb, sb[0:64, :], 64, parts=64)
        sb2 = stg.next()
        apply_rotary(sb, 64, sb2, sb2[0:64, :])
        S.op("sp", lambda e: e.dma_start(out=KPE[:, t * TW:(t + 1) * TW], in_=sb2[0:64, :]), reads=[sb2],
             writes=[KPE_r[t]], dma=sb2)
        for g4 in range(4):
            slot, wv = load_w(("ukvK", g4), w_ukvK[:, g4 * 512:(g4 + 1) * 512], 2, 512)
            for jp in range(2):
                entries, outs = [], []
                for j in (2 * jp, 2 * jp + 1):
                    hh = g4 * 4 + j
                    ps = mmr.next()
                    proj_w(slot, wv, j * 128, 2, ckvn, lambda kc: ckvn[:, kc, :], ps)
                    sb = stg.next()
                    entries.append((ps, C("k_nope_norm"), sb, sb[:, :]))
                    outs.append((sb, hh))
                group_norm_multi(entries, 128)
                for sb, hh in outs:
                    S.op("sp", lambda e, sb=sb, hh=hh: e.dma_start(out=KA[hh, :, t * TW:(t + 1) * TW], in_=sb[:, :]),
                         reads=[sb], writes=[KA_r[hh][t]], dma=sb)
        for g4 in range(4):
            slot, wv = load_w(("ukvV", g4), w_ukvV[:, g4 * 512:(g4 + 1) * 512], 2, 512)
            vs = vst.next()
            for blk in range(4):
                ps = mmr.next()
                for kc in range(2):
                    S.op("pe", lambda e, kc=kc, blk=blk, ps=ps, wv=wv: e.matmul(
                        ps[:, :], lhsT=ckvn[:, kc, blk * 128:(blk + 1) * 128], rhs=wv[:, kc, :],
                        start=(kc == 0), stop=(kc == 1)), reads=[slot, ckvn], writes=[ps])
                S.op("dve", lambda e, blk=blk, ps=ps, vs=vs: e.tensor_copy(
                    out=vs[:, :, blk, 0:128], in_=ps[:, :].rearrange("p (h d) -> p h d", h=4)), reads=[ps], writes=[vs])
            S.op("sp", lambda e, g4=g4, vs=vs: e.dma_start(
                out=VV[4 * g4:4 * g4 + 4, :, 4 * t:4 * t + 4, :].rearrange("h p b e -> p h b e"), in_=vs[:, :, :, :]),
                reads=[vs], writes=[VV_r[4 * g4 + i][t] for i in range(4)], dma=vs)

    def b_phase1(j_, l, t, hsrc_ap, hsrc_b):
        load_h(hsrc_ap, hsrc_b, t)
        load_cs(t)
        rmsnorm("attn_norm%d" % l)
        slot, wv = load_w(("dq", j_), b_w_dq[j_], 8, 512)
        pss = []
        for c in range(4):
            ps = mmr.next()
            proj_w(slot, wv, c * 128, 8, hn, lambda kc: hn[:, kc, :], ps)
            pss.append(ps)
        st = STAT
        for c in range(4):
            sq = sqr.next()
            S.op("act", lambda e, sq=sq, c=c: e.activation(out=sq[:, :], in_=pss[c][:, :], func=AF.Square),
                 reads=[pss[c]], writes=[sq])
            S.op("pe", lambda e, sq=sq, c=c: e.matmul(st[:, :], lhsT=ones[:, :], rhs=sq[:, :], start=(c == 0), stop=(c == 3)),
                 reads=[ones, sq], writes=[st])
        rs = rstd_from(st, st[:, :], 1.0 / 512)
        for c in range(4):
            S.op("dve", lambda e, c=c: e.scalar_tensor_tensor(out=cqn[:, c, :], in0=pss[c][:, :],
                                                              scalar=C("b_cq_norm%d" % j_, c), in1=rs[:, :],
                                                              op0=ALU.mult, op1=ALU.mult),
                 reads=[pss[c], cols, rs], writes=[cqn])
        for g4 in range(4):
            slot, wv = load_w(("uqN", j_, g4), b_w_uqN[j_][:, g4 * 512:(g4 + 1) * 512], 4, 512)
            for jp in range(2):
                entries, outs = [], []
                for j in (2 * jp, 2 * jp + 1):
                    hh = g4 * 4 + j
                    ps = mmr.next()
                    proj_w(slot, wv, j * 128, 4, cqn, lambda kc: cqn[:, kc, :], ps)
                    sb = stg.next()
                    entries.append((ps, C("b_q_nope_norm%d" % j_), sb, sb[:, :]))
                    outs.append((sb, hh))
                group_norm_multi(entries, 128)
                for sb, hh in outs:
                    S.op("sp", lambda e, sb=sb, hh=hh: e.dma_start(out=QA[hh, :, t * TW:(t + 1) * TW], in_=sb[:, :]),
                         reads=[sb], writes=[QA_r[hh][t]], dma=sb)
        for g2 in range(2):
            slot, wv = load_w(("uqP", j_, g2), b_w_uqP[j_][:, g2 * 512:(g2 + 1) * 512], 4, 512)
            for j in range(4):
                ch = g2 * 4 + j
                ps = mmr.next()
                proj_w(slot, wv, j * 128, 4, cqn, lambda kc: cqn[:, kc, :], ps)
                sb = stg.next()
                group_norm_to(ps, C("b_q_pe_norm%d" % j_), sb, sb[:, :], 64)
                sb2 = stg.next()
                apply_rotary(sb, 128, sb2, sb2[:, :])
                S.op("sp", lambda e, sb2=sb2, ch=ch: e.dma_start(out=QP[ch, :, t * TW:(t + 1) * TW], in_=sb2[:, :]),
                     reads=[sb2], writes=[QP_r[ch][t]], dma=sb2)

    def hsrc_of(l):
        return (xT, None) if l == 0 else (hT, hT_r)

    for l in range(L):
        src_ap, src_r = hsrc_of(l)
        last = (l == L - 1)
        dst_ap, dst_r = (outT, None) if last else (hT, hT_r)
        if l < 2:
            for t in range(NT):
                a_phase1(l, t, src_ap, src_r[t] if src_r else None)
            ckeys = []
            if l == 0:
                cur_phase[0] = 1
                if wt_plan is not None:
                    ckeys = [k for k, v in wt_plan.items() if v[0] > 0]
            nck = (len(ckeys) + 7) // 8
            for h in range(8):
                attention_head("A", l, h)
                if ckeys:
                    convert_keys(ckeys[h * nck:(h + 1) * nck], gate=[attk])
            for i in range(44):
                S.op("dve", lambda e, i=i: e.memset(utail_t[:, i, :], 0.0), writes=[utail[i]])
            for t in range(NT):
                dense_tile(l, t, 8, a_w_o[l], src_ap, src_r[t] if src_r else None, dst_ap, dst_r)
        else:
            j_ = l - 2
            if l == 2:
                rotary_tables()
                for t in range(NT):
                    shared_kv_tile(t, src_ap, src_r[t])
                S.op("dve", lambda e: e.memset(attkp[64:128, :], 0.0), writes=[attkp])
                S.op("sp", lambda e: e.dma_start(out=attkp[0:64, :], in_=KPE), reads=KPE_r, writes=[attkp], dma=attkp)
                for r_ in qpring.bufs:
                    S.op("dve", lambda e, r_=r_: e.memset(r_[64:128, :], 0.0), writes=[r_])
            for t in range(NT):
                b_phase1(j_, l, t, src_ap, src_r[t])
            for h in range(16):
                attention_head("B", l, h)
            for i in range(44):
                S.op("dve", lambda e, i=i: e.memset(utail_t[:, i, :], 0.0), writes=[utail[i]])
            for t in range(NT):
                dense_tile(l, t, 16, b_w_o[j_], src_ap, src_r[t], dst_ap, dst_r)

    stats = S.emit()
    es.close()
    return nc, stats, cm, rm, wt_first


_PROG = {}


def _get_prog(SL, L):
    key = (SL, L)
    if key not in _PROG:
        plan = build_program(SL, L)[4]
        _PROG[key] = build_program(SL, L, wt_plan=plan)[:4]
    return _PROG[key]


def prepare_inputs(inp, SL, L, cm, rm):
    f = lambda a: np.ascontiguousarray(np.asarray(a, dtype=np.float32))
    B = inp["x"].shape[0]
    cols = np.zeros((128, cm.n), np.float32)
    rows = np.zeros((128, rm.n), np.float32)

    def put_vec(name, v, off=0):
        v = np.asarray(v, np.float32)
        k = v.shape[0] // 128
        cols[:, cm.m[name] + off:cm.m[name] + off + k] = v.reshape(k, 128).T

    for l in range(4):
        put_vec("attn_norm%d" % l, inp["attn_norm"][l])
        put_vec("ffn_norm%d" % l, inp["ffn_norm"][l])
        put_vec("ple_norm%d" % l, inp["ple_norm"][l])
        for j in range(3):
            put_vec("conv_w%d_%d" % (l, j), np.asarray(inp["ffn_conv_w"])[l, j])
        put_vec("conv_b%d" % l, np.asarray(inp["ffn_conv_b"])[l])
    put_vec("kv_norm", inp["kv_norm"])
    for l in range(2):
        put_vec("a_q_norm%d" % l, np.tile(np.asarray(inp["a_q_norm"])[l], 2))
        put_vec("a_k_norm%d" % l, np.tile(np.asarray(inp["a_k_norm"])[l], 2))
        put_vec("b_cq_norm%d" % l, np.asarray(inp["b_cq_norm"])[l])
        put_vec("b_q_nope_norm%d" % l, np.asarray(inp["b_q_nope_norm"])[l])
        put_vec("b_q_pe_norm%d" % l, np.tile(np.asarray(inp["b_q_pe_norm"])[l], 2))
    put_vec("ckv_norm", inp["ckv_norm"])
    put_vec("k_nope_norm", inp["k_nope_norm"])
    put_vec("k_pe_norm", np.tile(np.asarray(inp["k_pe_norm"]), 2))
    tab = np.asarray(inp["rel_bias_table"], np.float32)
    cols[:, cm.m["b31"]:cm.m["b31"] + 8] = tab[31][None, :]
    cols[:, cm.m["table"]:cm.m["table"] + 256] = tab.reshape(1, 256)
    half = 32
    invf = np.exp(-math.log(10000.0) * np.arange(half, dtype=np.float32) * np.float32(2.0 / 64)).astype(np.float32)
    cols[:, cm.m["invfreq"]] = np.tile(invf, 4)
    for l in range(2):
        for nm, key in (("q1", "a_lam_q1"), ("k1", "a_lam_k1"), ("q2", "a_lam_q2"), ("k2", "a_lam_k2")):
            r0 = rm.m["lam_%s%d" % (nm, l)]
            rows[:, r0:r0 + 64] = np.asarray(inp[key], np.float32)[l][None, :]
        r0 = rm.m["sub_gain%d" % l]
        rows[:, r0:r0 + 128] = np.asarray(inp["a_sub_norm"], np.float32)[l][None, :]
    prot = np.zeros((128, 128), np.float32)
    for g in range(2):
        for m in range(64):
            if m < 32:
                prot[g * 64 + m + 32, g * 64 + m] = -1.0
            else:
                prot[g * 64 + m - 32, g * 64 + m] = 1.0
    w_ukv = np.asarray(inp["w_ukv"], np.float32).reshape(256, 16, 256)
    w_uq = np.asarray(inp["b_w_uq"], np.float32).reshape(2, 512, 16, 192)
    shared = dict(
        cols=cols, rows=rows, prot=prot,
        a_w_qkv=f(inp["a_w_qkv"]), a_w_o=f(inp["a_w_o"]), w_dkv=f(inp["w_dkv"]),
        w_ukvK=np.ascontiguousarray(w_ukv[:, :, 0:128].reshape(256, 2048)),
        w_ukvV=np.ascontiguousarray(w_ukv[:, :, 128:256].reshape(256, 2048)),
        b_w_dq=f(inp["b_w_dq"]),
        b_w_uqN=np.ascontiguousarray(w_uq[:, :, :, 0:128].reshape(2, 512, 2048)),
        b_w_uqP=np.ascontiguousarray(w_uq[:, :, :, 128:192].reshape(2, 512, 1024)),
        b_w_o=f(inp["b_w_o"]), ffn_w_in=f(inp["ffn_w_in"]), ffn_w_out=f(inp["ffn_w_out"]),
        ple_w_proj=f(inp["ple_w_proj"]), ple_w_gate=f(inp["ple_w_gate"]),
    )
    x = np.asarray(inp["x"], np.float32)
    p = np.asarray(inp["p"], np.float32)
    pos = np.asarray(inp["positions"]).astype(np.int32)
    maps = []
    for c in range(NCORES):
        b = c % B
        m = dict(shared)
        m["xT"] = np.ascontiguousarray(x[b, :SL].T)
        m["pT"] = np.ascontiguousarray(p[:, b, :SL].transpose(0, 2, 1))
        m["posb"] = np.ascontiguousarray(np.broadcast_to(pos[b, :SL][None, :], (128, SL)))
        m["poscol"] = np.ascontiguousarray(pos[b, :256].reshape(2, 128).T)
        maps.append(m)
    return maps


def run_model(inp, SL, L):
    nc, stats, cm, rm = _get_prog(SL, L)
    maps = prepare_inputs(inp, SL, L, cm, rm)
    res = run_bass_kernel_spmd(nc, maps, core_ids=list(range(NCORES)))
    B = inp["x"].shape[0]
    out = np.stack([np.ascontiguousarray(res.results[b]["outT"].T) for b in range(B)], axis=0)
    return out.astype(np.float32)


def kernel(**inputs):
    return run_model(inputs, 4096, 4)
```

```python
import math
from contextlib import ExitStack

import numpy as np
import concourse.bass as bass
import concourse.mybir as mybir
from concourse.bass_utils import run_bass_kernel_spmd

F32 = mybir.dt.float32
BF16 = mybir.dt.bfloat16
I32 = mybir.dt.int32
AF = mybir.ActivationFunctionType
ALU = mybir.AluOpType

D = 1024
DFF = 2816
NFC = 22
TW = 512
EPS = 1e-6
NCORES = 8


class Buf:
    __slots__ = ("name", "t", "writers", "readers", "dkey")

    def __init__(self, name, t=None):
        self.name = name
        self.t = t
        self.writers = {}
        self.readers = {}
        self.dkey = None

    def __getitem__(self, idx):
        return self.t[idx]


class Op:
    __slots__ = ("q", "fn", "deps", "key", "signaled", "count", "ninc")


class Sched:
    def __init__(self, nc, es):
        self.nc = nc
        self.es = es
        self.ops = []
        self.eng = {"pe": nc.tensor, "act": nc.scalar, "dve": nc.vector,
                    "pool": nc.gpsimd, "sp": nc.sync}
        self.last_dma = {}
        self.ndkeys = 0

    def sbuf(self, name, shape, dt):
        return Buf(name, self.es.enter_context(self.nc.sbuf_tensor("sb_" + name, list(shape), dt)))

    def psum(self, name, shape, dt):
        return Buf(name, self.es.enter_context(self.nc.psum_tensor("ps_" + name, list(shape), dt)))

    def op(self, q, fn, reads=(), writes=(), dma=None, ninc=1):
        o = Op()
        o.q = q
        o.fn = fn
        o.signaled = False
        o.count = 0
        o.ninc = ninc
        isdma = dma is not None
        if isdma:
            if dma.dkey is None:
                dma.dkey = self.ndkeys
                self.ndkeys += 1
            o.key = ("dma", dma.dkey)
        else:
            o.key = q
        key = o.key
        deps = {}
        for b in reads:
            for w in b.writers.values():
                deps[id(w)] = w
        for b in writes:
            for k, r in b.readers.items():
                if k != key or isdma:
                    deps[id(r)] = r
            for k, w in b.writers.items():
                if k != key or isdma:
                    deps[id(w)] = w
        if isdma:
            prev = self.last_dma.get(key)
            if prev is not None:
                deps[id(prev)] = prev
            self.last_dma[key] = o
            o.signaled = True
        o.deps = list(deps.values())
        for d in o.deps:
            d.signaled = True
        for b in reads:
            b.readers[key] = o
        for b in writes:
            b.readers = {}
            b.writers = {key: o}
        self.ops.append(o)
        return o

    def emit(self):
        nc = self.nc
        sems = {}
        counts = {}
        waited = {}
        for o in self.ops:
            if o.signaled:
                if o.key not in sems:
                    nm = "s_" + (o.key if isinstance(o.key, str) else "d%d" % o.key[1])
                    sems[o.key] = self.es.enter_context(nc.semaphore(nm))
                    counts[o.key] = 0
                counts[o.key] += (16 * o.ninc) if not isinstance(o.key, str) else 1
                o.count = counts[o.key]
        nwaits = 0
        for o in self.ops:
            e = self.eng[o.q]
            wq = waited.setdefault(o.q, {})
            for d in o.deps:
                if wq.get(d.key, 0) < d.count:
                    e.wait_ge(sems[d.key], d.count)
                    wq[d.key] = d.count
                    nwaits += 1
            ins = o.fn(e)
            if o.signaled:
                if isinstance(o.key, str):
                    last = ins[-1] if isinstance(ins, (list, tuple)) else ins
                    last.then_inc(sems[o.key], 1)
                else:
                    lst = list(ins) if isinstance(ins, (list, tuple)) else [ins]
                    assert len(lst) == o.ninc, (len(lst), o.ninc)
                    for i in lst:
                        i.then_inc(sems[o.key], 16)
        e = self.eng["sp"]
        wq = waited.setdefault("sp", {})
        for key, s in sems.items():
            if counts[key] > wq.get(key, 0):
                e.wait_ge(s, counts[key])
        return dict(nops=len(self.ops), nwaits=nwaits, nsems=len(sems),
                    maxcount=max(counts.values()) if counts else 0)


class Ring:
    def __init__(self, bufs):
        self.bufs = bufs
        self.i = 0

    def next(self):
        b = self.bufs[self.i % len(self.bufs)]
        self.i += 1
        return b


def t5_thresholds():
    n = np.arange(0, 400)
    f = np.maximum(n, 1).astype(np.float32) / np.float32(16)
    lr = np.log(f).astype(np.float32) / np.float32(math.log(128 / 16))
    large = 16 + (lr * np.float32(16)).astype(np.int32)
    large = np.minimum(large, 31)
    bucket = np.where(n < 16, n, large)
    lo = [int(np.min(n[bucket >= b])) for b in range(32)]
    return lo


class ColMap:
    def __init__(self):
        self.n = 0
        self.m = {}

    def add(self, name, w):
        self.m[name] = self.n
        self.n += w
        return self.m[name]


def make_colmap():
    cm = ColMap()
    for l in range(4):
        cm.add("attn_norm%d" % l, 8)
        cm.add("ffn_norm%d" % l, 8)
        cm.add("ple_norm%d" % l, 8)
        for j in range(3):
            cm.add("conv_w%d_%d" % (l, j), 44)
        cm.add("conv_b%d" % l, 44)
    cm.add("kv_norm", 8)
    for l in range(2):
        cm.add("a_q_norm%d" % l, 1)
        cm.add("a_k_norm%d" % l, 1)
        cm.add("b_cq_norm%d" % l, 4)
        cm.add("b_q_nope_norm%d" % l, 1)
        cm.add("b_q_pe_norm%d" % l, 1)
    cm.add("ckv_norm", 2)
    cm.add("k_nope_norm", 1)
    cm.add("k_pe_norm", 1)
    cm.add("b31", 8)
    cm.add("table", 256)
    cm.add("invfreq", 1)
    return cm


def make_rowmap():
    rm = ColMap()
    for l in range(2):
        for nm in ("q1", "k1", "q2", "k2"):
            rm.add("lam_%s%d" % (nm, l), 64)
        rm.add("sub_gain%d" % l, 128)
    return rm


def build_program(SL, L, wt_plan=None):
    NT = SL // TW
    NB = SL // 128
    nc = bass.Bass("TRN2", target_bir_lowering=False)
    cm = make_colmap()
    rm = make_rowmap()
    lo_thr = t5_thresholds()

    def din(name, shape, dt=F32):
        return nc.dram_tensor(name, list(shape), dt, kind="ExternalInput").ap()

    def dscr(name, shape, dt):
        return nc.dram_tensor(name, list(shape), dt, kind="Internal").ap()

    xT = din("xT", [D, SL])
    pT = din("pT", [4, 256, SL])
    posb = din("posb", [128, SL], I32)
    poscol = din("poscol", [128, 2], I32)
    colsd = din("cols", [128, cm.n])
    rowsd = din("rows", [128, rm.n])
    protd = din("prot", [128, 128])
    a_w_qkv = din("a_w_qkv", [2, D, 3072])
    a_w_o = din("a_w_o", [2, D, D])
    w_dkv = din("w_dkv", [D, 320])
    w_ukvK = din("w_ukvK", [256, 2048])
    w_ukvV = din("w_ukvV", [256, 2048])
    b_w_dq = din("b_w_dq", [2, D, 512])
    b_w_uqN = din("b_w_uqN", [2, 512, 2048])
    b_w_uqP = din("b_w_uqP", [2, 512, 1024])
    b_w_o = din("b_w_o", [2, 2048, D])
    ffn_w_in = din("ffn_w_in", [4, D, 2 * DFF])
    ffn_w_out = din("ffn_w_out", [4, DFF, D])
    ple_w_proj = din("ple_w_proj", [4, 256, D])
    ple_w_gate = din("ple_w_gate", [4, D, D])
    outT = nc.dram_tensor("outT", [D, SL], F32, kind="ExternalOutput").ap()

    hT = dscr("hT", [D, SL], F32)
    QA = dscr("QA", [16, 128, SL], BF16)
    KA = dscr("KA", [16, 128, SL], BF16)
    QP = dscr("QP", [8, 128, SL], BF16)
    KPE = dscr("KPE", [64, SL], BF16)
    VV = dscr("VV", [16, 128, NB, 129], BF16)
    OT = dscr("OT", [16, 128, SL], BF16)
    NWT = max(1, sum(1 for v in wt_plan.values() if v[0] > 0)) if wt_plan is not None else 1
    WT = dscr("WT", [NWT, 128, 4096], BF16)
    COS = dscr("COS", [128, SL], F32)
    SIN = dscr("SIN", [128, SL], F32)

    es = ExitStack()
    S = Sched(nc, es)

    def regions(name, n1):
        return [[Buf("%s_%d_%d" % (name, i, t)) for t in range(NT)] for i in range(n1)]

    hT_r = [[Buf("hT_%d_%d" % (t, k)) for k in range(8)] for t in range(NT)]
    QA_r = regions("QA", 16)
    KA_r = regions("KA", 16)
    QP_r = regions("QP", 8)
    KPE_r = [Buf("KPE_%d" % t) for t in range(NT)]
    VV_r = regions("VV", 16)
    OT_r = regions("OT", 16)
    CS_r = [Buf("CS_%d" % t) for t in range(NT)]

    cols = S.sbuf("cols", [128, cm.n], F32)
    rows = S.sbuf("rows", [128, rm.n], F32)
    ones = S.sbuf("ones", [128, 128], BF16)
    bones = S.sbuf("bones", [128, 128], BF16)
    ident = S.sbuf("ident", [128, 128], F32)
    prot = S.sbuf("prot", [128, 128], BF16)
    tri = S.sbuf("tri", [128, 128], F32)
    epsc = S.sbuf("epsc", [128, 1], F32)
    Rt = S.sbuf("Rt", [128, 2, 8, 128], F32)
    lamt = S.sbuf("lamt", [128, 8], F32)
    subg = S.sbuf("subg", [128, 2, 128], F32)
    wslots = [S.sbuf("w%d" % i, [128, 4096], BF16) for i in range(4)]
    wring = Ring(wslots)
    h_sb = S.sbuf("h_sb", [128, 8, TW], F32)
    h_c = [Buf("h_c%d" % i, h_sb.t) for i in range(8)]
    hn = S.sbuf("hn", [128, 8, TW], BF16)
    sqr = Ring([S.sbuf("sq%d" % i, [128, TW], BF16) for i in range(3)])
    lnr = Ring([S.sbuf("ln%d" % i, [128, TW], F32) for i in range(2)])
    rsr = Ring([S.sbuf("rs%d" % i, [128, TW], F32) for i in range(2)])
    stg = Ring([S.sbuf("stg%d" % i, [128, TW], BF16) for i in range(4)])
    vst = Ring([S.sbuf("vst%d" % i, [128, 4, 4, 129], BF16) for i in range(2)])
    qring = Ring([S.sbuf("attq%d" % i, [128, TW], BF16) for i in range(2)])
    qpring = Ring([S.sbuf("attqp%d" % i, [128, TW], BF16) for i in range(2)])
    attk = S.sbuf("attk", [128, SL], BF16)
    attv = S.sbuf("attv", [128, NB, 129], BF16)
    attkp = S.sbuf("attkp", [128, SL], BF16)
    ppool = Ring([S.sbuf("pt%d" % i, [128, TW], BF16) for i in range(6)])
    otst = Ring([S.sbuf("otst%d" % i, [128, TW], BF16) for i in range(2)])
    o_sb = Ring([S.sbuf("o_sb%d" % i, [128, 128], F32) for i in range(4)])
    on_sb = Ring([S.sbuf("on_sb%d" % i, [128, 128], F32) for i in range(4)])
    smr = Ring([S.sbuf("sm%d" % i, [128, 8], F32) for i in range(4)])
    junk = S.sbuf("junk", [128, 128], F32)
    ot_sb = S.sbuf("ot_sb", [128, 8, TW], BF16)
    z_sb = S.sbuf("z_sb", [128, NFC, TW], BF16)
    uext = Ring([S.sbuf("uext%d" % i, [128, TW + 2], F32) for i in range(4)])
    cpool = Ring([S.sbuf("c%d" % i, [128, TW], F32) for i in range(4)])
    sgp = Ring([S.sbuf("sg%d" % i, [128, TW], F32) for i in range(2)])
    utail_t = es.enter_context(nc.sbuf_tensor("sb_utail", [128, 44, 2], F32))
    utail = [Buf("utail%d" % i, utail_t) for i in range(44)]
    pt_sb = S.sbuf("pt_sb", [128, 2, TW], BF16)
    cs_sb = S.sbuf("cs_sb", [128, 2, TW], F32)
    cqn = S.sbuf("cqn", [128, 4, TW], BF16)
    ckvn = S.sbuf("ckvn", [128, 2, TW], BF16)
    itmp = S.sbuf("itmp", [128, TW], I32)
    itmp2 = S.sbuf("itmp2", [128, TW], I32)
    pb = [S.psum("pb%d" % i, [128, TW], F32) for i in range(8)]

    def C(name, off=0, w=1):
        c0 = cm.m[name] + off
        return cols[:, c0:c0 + w]

    def dve_tt(out_b, out_ap, a_b, a_ap, b_b, b_ap, op, q="dve"):
        S.op(q, lambda e: e.tensor_tensor(out=out_ap, in0=a_ap, in1=b_ap, op=op),
             reads=[a_b, b_b], writes=[out_b])

    wt_first = {}
    wt_tid = {}
    wt_bufs = {}
    cur_phase = [0]

    def _cast_load(slot, srcs, kc, ns, gate=()):
        n = sum(ns)
        view = slot[:, 0:kc * n].rearrange("p (k n) -> p k n", k=kc)
        svs = [a.rearrange("(k p) n -> p k n", p=128) for a in srcs]
        offs = [sum(ns[:i]) for i in range(len(ns))]
        S.op("pool", lambda e: [e.dma_start(out=view[:, :, offs[i]:offs[i] + ns[i]], in_=svs[i]) for i in range(len(ns))],
             reads=list(gate), writes=[slot], dma=slot, ninc=len(ns))
        return view

    def src_for_key(key):
        k0 = key[0]
        if k0 == "wo":
            _, l, half, mg = key
            w = a_w_o[l] if l < 2 else b_w_o[l - 2]
            return [w[half * 1024:(half + 1) * 1024, mg * 512:(mg + 1) * 512]], 8, [512]
        if k0 == "win":
            _, l, ig = key
            w = ffn_w_in[l]
            return [w[:, ig * 128:(ig + 2) * 128], w[:, DFF + ig * 128:DFF + (ig + 2) * 128]], 8, [256, 256]
        if k0 == "wout":
            _, l, m = key
            return [ffn_w_out[l][:, m * 128:(m + 1) * 128]], NFC, [128]
        if k0 == "pproj":
            return [ple_w_proj[key[1]]], 2, [1024]
        if k0 == "pgate":
            _, l, mg = key
            return [ple_w_gate[l][:, mg * 512:(mg + 1) * 512]], 8, [512]
        if k0 == "qkv":
            _, l, grp = key
            return [a_w_qkv[l][:, grp * 512:(grp + 1) * 512]], 8, [512]
        if k0 == "qkvv":
            _, l, half = key
            return [a_w_qkv[l][:, 2048 + half * 512:2048 + (half + 1) * 512]], 8, [512]
        if k0 == "dkv":
            return [w_dkv], 8, [320]
        if k0 == "ukvK":
            return [w_ukvK[:, key[1] * 512:(key[1] + 1) * 512]], 2, [512]
        if k0 == "ukvV":
            return [w_ukvV[:, key[1] * 512:(key[1] + 1) * 512]], 2, [512]
        if k0 == "dq":
            return [b_w_dq[key[1]]], 8, [512]
        if k0 == "uqN":
            return [b_w_uqN[key[1]][:, key[2] * 512:(key[2] + 1) * 512]], 4, [512]
        if k0 == "uqP":
            return [b_w_uqP[key[1]][:, key[2] * 512:(key[2] + 1) * 512]], 4, [512]
        raise KeyError(key)

    if wt_plan is not None:
        for key_, (ph_, _) in wt_plan.items():
            if ph_ > 0:
                wt_tid[key_] = len(wt_tid)
                wt_bufs[key_] = Buf("wt_%d" % wt_tid[key_])

    def convert_keys(keys, gate=()):
        for ki, key in enumerate(keys):
            srcs, kc, ns = src_for_key(key)
            tid = wt_tid[key]
            slot = wring.next()
            _cast_load(slot, srcs, kc, ns, gate=gate if ki == 0 else ())
            n = kc * sum(ns)
            S.op("sp", lambda e, slot=slot, tid=tid, n=n: e.dma_start(out=WT[tid, :, 0:n], in_=slot[:, 0:n]),
                 reads=[slot], writes=[wt_bufs[key]], dma=slot)

    srcs_of = {}

    def _load(key, srcs, kc, ns):
        n = sum(ns)
        if key not in wt_first:
            wt_first[key] = (cur_phase[0], (None, kc, ns))
        srcs_of.setdefault(key, srcs)
        slot = wring.next()
        if wt_plan is None or key not in wt_tid:
            view = _cast_load(slot, srcs, kc, ns)
        else:
            tid = wt_tid[key]
            view = slot[:, 0:kc * n].rearrange("p (k n) -> p k n", k=kc)
            S.op("pool", lambda e: e.dma_start(out=slot[:, 0:kc * n], in_=WT[tid, :, 0:kc * n]),
                 reads=[wt_bufs[key]], writes=[slot], dma=slot)
        return slot, view

    def load_w(key, src_ap, kc, n):
        return _load(key, [src_ap], kc, [n])

    def load_w2(key, src_a, src_b, kc, na, nb_):
        return _load(key, [src_a, src_b], kc, [na, nb_])

    mmr = Ring(pb[0:5])
    str_ = Ring(pb[5:7])
    STAT = pb[7]

    def rstd_from(stat_b, stat_ap, inv_n, width=TW, parts=128):
        ln = lnr.next()
        rs = rsr.next()
        S.op("act", lambda e: e.activation(out=ln[0:parts, 0:width], in_=stat_ap, func=AF.Ln,
                                           bias=epsc[0:parts, :], scale=inv_n),
             reads=[stat_b, epsc], writes=[ln])
        S.op("act", lambda e: e.activation(out=rs[0:parts, 0:width], in_=ln[0:parts, 0:width],
                                           func=AF.Exp, scale=-0.5),
             reads=[ln], writes=[rs])
        return rs

    def rmsnorm(gain_name):
        st = STAT
        for kc in range(8):
            sq = sqr.next()
            S.op("act", lambda e, kc=kc, sq=sq: e.activation(out=sq[:, :], in_=h_sb[:, kc, :], func=AF.Square),
                 reads=[h_c[kc]], writes=[sq])
            S.op("pe", lambda e, kc=kc, sq=sq: e.matmul(st[:, :], lhsT=ones[:, :], rhs=sq[:, :],
                                                        start=(kc == 0), stop=(kc == 7)),
                 reads=[ones, sq], writes=[st])
        rs = rstd_from(st, st[:, :], 1.0 / D)
        for kc in range(8):
            S.op("dve", lambda e, kc=kc: e.scalar_tensor_tensor(
                out=hn[:, kc, :], in0=h_sb[:, kc, :], scalar=C(gain_name, kc), in1=rs[:, :],
                op0=ALU.mult, op1=ALU.mult), reads=[h_c[kc], cols, rs], writes=[hn])

    def load_h(src_ap, src_buf, t):
        for kc in range(8):
            S.op("sp", lambda e, kc=kc: e.dma_start(out=h_sb[:, kc, :],
                                                    in_=src_ap[kc * 128:(kc + 1) * 128, t * TW:(t + 1) * TW]),
                 reads=[src_buf[kc]] if src_buf is not None else [], writes=[h_c[kc]], dma=h_c[kc])

    def proj_fm(wview, j0, kcn, rhs_b, rhs_of_kc, ps):
        for kc in range(kcn):
            S.op("pe", lambda e, kc=kc: e.matmul(ps[:, :], lhsT=wview[:, kc, j0:j0 + 128], rhs=rhs_of_kc(kc),
                                                 start=(kc == 0), stop=(kc == kcn - 1)),
                 reads=[rhs_b], writes=[ps])

    def group_norm_to(ps, gain_ap, out_b, out_ap, group, parts=128):
        sq = sqr.next()
        S.op("act", lambda e: e.activation(out=sq[0:parts, :], in_=ps[0:parts, :], func=AF.Square),
             reads=[ps], writes=[sq])
        st = str_.next()
        lhs = bones if group == 64 else ones
        S.op("pe", lambda e: e.matmul(st[0:parts, :], lhsT=lhs[0:parts, 0:parts], rhs=sq[0:parts, :], start=True, stop=True),
             reads=[lhs, sq], writes=[st])
        rs = rstd_from(st, st[0:parts, :], 1.0 / group, parts=parts)
        S.op("dve", lambda e: e.scalar_tensor_tensor(out=out_ap, in0=ps[0:parts, :], scalar=gain_ap, in1=rs[0:parts, :],
                                                     op0=ALU.mult, op1=ALU.mult),
             reads=[ps, cols, rs], writes=[out_b])

    def group_norm_multi(entries, group):
        lhs = bones if group == 64 else ones
        sqs, sts, lns, rss = [], [], [], []
        for ps, _, _, _ in entries:
            sq = sqr.next()
            sqs.append(sq)
            S.op("act", lambda e, sq=sq, ps=ps: e.activation(out=sq[:, :], in_=ps[:, :], func=AF.Square),
                 reads=[ps], writes=[sq])
        for sq in sqs:
            st = str_.next()
            sts.append(st)
            S.op("pe", lambda e, sq=sq, st=st: e.matmul(st[:, :], lhsT=lhs[:, :], rhs=sq[:, :], start=True, stop=True),
                 reads=[lhs, sq], writes=[st])
        for st in sts:
            ln = lnr.next()
            lns.append(ln)
            S.op("act", lambda e, ln=ln, st=st: e.activation(out=ln[:, :], in_=st[:, :], func=AF.Ln, bias=epsc[:, :],
                                                             scale=1.0 / group), reads=[st, epsc], writes=[ln])
        for ln in lns:
            rs = rsr.next()
            rss.append(rs)
            S.op("act", lambda e, ln=ln, rs=rs: e.activation(out=rs[:, :], in_=ln[:, :], func=AF.Exp, scale=-0.5),
                 reads=[ln], writes=[rs])
        for (ps, gain_ap, out_b, out_ap), rs in zip(entries, rss):
            S.op("dve", lambda e, ps=ps, gain_ap=gain_ap, out_ap=out_ap, rs=rs: e.scalar_tensor_tensor(
                out=out_ap, in0=ps[:, :], scalar=gain_ap, in1=rs[:, :], op0=ALU.mult, op1=ALU.mult),
                reads=[ps, cols, rs], writes=[out_b])

    S.op("sp", lambda e: e.dma_start(out=cols[:, :], in_=colsd), writes=[cols], dma=cols)
    S.op("sp", lambda e: e.dma_start(out=rows[:, :], in_=rowsd), writes=[rows], dma=rows)
    S.op("pool", lambda e: e.dma_start(out=prot[:, :], in_=protd), writes=[prot], dma=prot)
    S.op("dve", lambda e: e.memset(ones[:, :], 1.0), writes=[ones])
    S.op("dve", lambda e: e.memset(bones[:, :], 0.0), writes=[bones])
    S.op("dve", lambda e: e.memset(bones[0:64, 0:64], 1.0), writes=[bones])
    S.op("dve", lambda e: e.memset(bones[64:128, 64:128], 1.0), writes=[bones])
    S.op("dve", lambda e: e.memset(epsc[:, :], EPS), writes=[epsc])
    S.op("pool", lambda e: e.memset(ident[:, :], 0.0), writes=[ident])
    S.op("pool", lambda e: e.affine_select(out=ident[:, :], in_=ident[:, :], pattern=[[-1, 128]],
                                           compare_op=ALU.not_equal, fill=1.0, base=0, channel_multiplier=1),
         reads=[ident], writes=[ident])
    S.op("pool", lambda e: e.memset(tri[:, :], 1.0), writes=[tri])
    S.op("pool", lambda e: e.affine_select(out=tri[:, :], in_=tri[:, :], pattern=[[1, 128]],
                                           compare_op=ALU.is_ge, fill=0.0, base=0, channel_multiplier=-1),
         reads=[tri], writes=[tri])
    for r_ in vst.bufs:
        S.op("dve", lambda e, r_=r_: e.memset(r_[:, :, :, :], 1.0), writes=[r_])

    nA = min(L, 2)
    for r_ in qring.bufs:
        S.op("dve", lambda e, r_=r_: e.memset(r_[64:128, :], 0.0), writes=[r_])
    for r_ in qpring.bufs:
        S.op("dve", lambda e, r_=r_: e.memset(r_[0:64, :], 0.0), writes=[r_])

    if nA > 0:
        class V3:
            def __init__(self, b):
                self.b = b

            def __getitem__(self, idx):
                return self.b[:, 0:256].rearrange("p (d q) -> p d q", d=2)[idx]

        posi = itmp
        pci = S.sbuf("pci", [128, 2], I32)
        posf = sgp.bufs[0]
        pcf = S.sbuf("pcf", [128, 2], F32)
        dt_b = cpool.bufs[0]
        dt_ = V3(dt_b)
        ge0_b = cpool.bufs[1]
        ge0 = V3(ge0_b)
        ge_bufs = [cpool.bufs[2], cpool.bufs[3]]
        ge = Ring([V3(b) for b in ge_bufs])
        dtab = S.sbuf("dtab", [128, 256], F32)
        nb31 = S.sbuf("nb31", [128, 8], F32)
        S.op("sp", lambda e: e.dma_start(out=posi[:, 0:256], in_=posb[:, 0:256]), writes=[posi], dma=posi)
        S.op("sp", lambda e: e.dma_start(out=pci[:, :], in_=poscol), writes=[pci], dma=pci)
        S.op("dve", lambda e: e.tensor_copy(out=posf[:, 0:256], in_=posi[:, 0:256]), reads=[posi], writes=[posf])
        S.op("dve", lambda e: e.tensor_copy(out=pcf[:, :], in_=pci[:, :]), reads=[pci], writes=[pcf])
        S.op("dve", lambda e: e.tensor_scalar(out=dt_[:, :, :], in0=posf[:, 0:256].rearrange("p (d q) -> p d q", d=2),
                                              scalar1=pcf[:, 0:1], scalar2=None, op0=ALU.subtract),
             reads=[posf, pcf], writes=[dt_b])
        tb = cm.m["table"]
        S.op("dve", lambda e: e.tensor_copy(out=dtab[:, 0:8], in_=cols[:, tb:tb + 8]), reads=[cols], writes=[dtab])
        S.op("dve", lambda e: e.tensor_tensor(out=dtab[:, 8:256], in0=cols[:, tb + 8:tb + 256], in1=cols[:, tb:tb + 248],
                                              op=ALU.subtract), reads=[cols], writes=[dtab])
        S.op("dve", lambda e: e.tensor_scalar(out=nb31[:, :], in0=C("b31", 0, 8), scalar1=-1.0, scalar2=None, op0=ALU.mult),
             reads=[cols], writes=[nb31])
        S.op("dve", lambda e: e.tensor_scalar(out=ge0[:, :, :], in0=dt_[:, :, :], scalar1=0.0, scalar2=None, op0=ALU.is_ge),
             reads=[dt_b], writes=[ge0_b])
        for h in range(8):
            S.op("dve", lambda e, h=h: e.tensor_scalar(out=Rt[:, :, h, :], in0=ge0[:, :, :], scalar1=dtab[:, h:h + 1],
                                                       scalar2=None, op0=ALU.mult), reads=[ge0_b, dtab], writes=[Rt])
        for b in range(1, 32):
            g = ge.next()
            S.op("dve", lambda e, g=g, b=b: e.tensor_scalar(out=g[:, :, :], in0=dt_[:, :, :], scalar1=float(lo_thr[b]),
                                                            scalar2=None, op0=ALU.is_ge), reads=[dt_b], writes=[g.b])
            for h in range(8):
                S.op("dve", lambda e, g=g, b=b, h=h: e.scalar_tensor_tensor(
                    out=Rt[:, :, h, :], in0=g[:, :, :], scalar=dtab[:, b * 8 + h:b * 8 + h + 1], in1=Rt[:, :, h, :],
                    op0=ALU.mult, op1=ALU.add), reads=[g.b, dtab, Rt], writes=[Rt])
        for h in range(8):
            S.op("act", lambda e, h=h: e.activation(out=Rt[:, :, h, :], in_=Rt[:, :, h, :], func=AF.Exp,
                                                    bias=nb31[:, h:h + 1], scale=1.0), reads=[Rt, nb31], writes=[Rt])
            S.op("dve", lambda e, h=h: e.tensor_tensor(out=Rt[:, :, h, :], in0=Rt[:, :, h, :], in1=ge0[:, :, :], op=ALU.mult),
                 reads=[Rt, ge0_b], writes=[Rt])
        for l in range(nA):
            lam_init = 0.8 - 0.6 * math.exp(-0.3 * l)
            sm = smr.next()
            for i, (a, b) in enumerate((("q1", "k1"), ("q2", "k2"))):
                ra = rm.m["lam_%s%d" % (a, l)]
                rb = rm.m["lam_%s%d" % (b, l)]
                S.op("dve", lambda e, ra=ra, rb=rb: e.tensor_tensor(out=junk[:, 0:64], in0=rows[:, ra:ra + 64],
                                                                    in1=rows[:, rb:rb + 64], op=ALU.mult),
                     reads=[rows], writes=[junk])
                S.op("act", lambda e, i=i, sm=sm: e.activation(out=junk[:, 64:128], in_=junk[:, 0:64], func=AF.Identity,
                                                               accum_out=sm[:, i:i + 1]), reads=[junk], writes=[junk, sm])
            S.op("act", lambda e, sm=sm: e.activation(out=sm[:, 2:4], in_=sm[:, 0:2], func=AF.Exp), reads=[sm], writes=[sm])
            S.op("dve", lambda e, sm=sm, l=l: e.tensor_tensor(out=lamt[:, 2 * l:2 * l + 1], in0=sm[:, 3:4], in1=sm[:, 2:3],
                                                              op=ALU.subtract), reads=[sm], writes=[lamt])
            S.op("dve", lambda e, l=l, lam_init=lam_init: e.tensor_scalar(
                out=lamt[:, 2 * l:2 * l + 1], in0=lamt[:, 2 * l:2 * l + 1], scalar1=-lam_init, scalar2=None, op0=ALU.add),
                reads=[lamt], writes=[lamt])
            sg0 = rm.m["sub_gain%d" % l]
            S.op("dve", lambda e, l=l, sg0=sg0, lam_init=lam_init: e.tensor_scalar(
                out=subg[:, l, :], in0=rows[:, sg0:sg0 + 128], scalar1=1.0 - lam_init, scalar2=None, op0=ALU.mult),
                reads=[rows], writes=[subg])

    ST = [[pb[0], pb[1]], [pb[2], pb[3]]]
    ACC = [pb[4], pb[5], pb[6]]
    TP = pb[7]

    def attention_head(kind, l, h):
        nsub = 2 if kind == "A" else 1
        sc = 0.125 if kind == "A" else (192.0 ** -0.5)
        S.op("sp", lambda e: e.dma_start(out=attk[:, :], in_=KA[h]), reads=[KA_r[h][t] for t in range(NT)],
             writes=[attk], dma=attk)
        S.op("sp", lambda e: e.dma_start(out=attv[:, :, :], in_=VV[h]), reads=[VV_r[h][t] for t in range(NT)],
             writes=[attv], dma=attv)
        steps = [(t, j) for t in range(NT) for j in range(4 * t + 4)]
        LA = 1 if kind == "A" else 2
        DEFER = 0 if kind == "A" else 3

        def st_of(c, i):
            return ST[c][i % 2] if kind == "A" else pb[i % 3]

        def acc_of(t, idx):
            if kind == "A":
                return ACC[idx // 3], (idx % 3) * 129
            base = 3 + 2 * (t % 2)
            return pb[base + idx // 3], (idx % 3) * 129

        qtiles = {}
        state = {"touched": set()}

        def load_q(t):
            attq = qring.next()
            attqp = qpring.next()
            if kind == "A":
                S.op("sp", lambda e, attq=attq, t=t: e.dma_start(out=attq[0:64, :], in_=QA[h, 0:64, t * TW:(t + 1) * TW]),
                     reads=[QA_r[h][t]], writes=[attq], dma=attq)
                S.op("sp", lambda e, attqp=attqp, t=t: e.dma_start(out=attqp[64:128, :], in_=QA[h, 64:128, t * TW:(t + 1) * TW]),
                     reads=[QA_r[h][t]], writes=[attqp], dma=attqp)
            else:
                S.op("sp", lambda e, attq=attq, t=t: e.dma_start(out=attq[:, :], in_=QA[h, :, t * TW:(t + 1) * TW]),
                     reads=[QA_r[h][t]], writes=[attq], dma=attq)
                r0 = 64 * (h % 2)
                S.op("sp", lambda e, attqp=attqp, t=t, r0=r0: e.dma_start(
                    out=attqp[0:64, :], in_=QP[h // 2, r0:r0 + 64, t * TW:(t + 1) * TW]),
                    reads=[QP_r[h // 2][t]], writes=[attqp], dma=attqp)
            qtiles[t] = (attq, attqp)

        def scores(i):
            t, j = steps[i]
            if j == 0 and t + 1 < NT:
                load_q(t + 1)
            attq, attqp = qtiles[t]
            b0 = max(0, j - 4 * t)
            n = TW - 128 * b0
            q0 = 128 * b0
            for c in range(nsub):
                st = st_of(c, i)
                if kind == "A":
                    qq = attq if c == 0 else attqp
                    S.op("pe", lambda e, st=st, j=j, n=n, q0=q0, qq=qq: e.matmul(
                        st[:, 0:n], lhsT=attk[:, j * 128:(j + 1) * 128], rhs=qq[:, q0:q0 + n], start=True, stop=True),
                        reads=[attk, qq], writes=[st])
                else:
                    S.op("pe", lambda e, st=st, j=j, n=n, q0=q0, attq=attq: e.matmul(
                        st[:, 0:n], lhsT=attk[:, j * 128:(j + 1) * 128], rhs=attq[:, q0:q0 + n],
                        start=True, stop=False), reads=[attk, attq], writes=[st])
                    S.op("pe", lambda e, st=st, j=j, n=n, q0=q0, attqp=attqp: e.matmul(
                        st[:, 0:n], lhsT=attkp[:, j * 128:(j + 1) * 128], rhs=attqp[:, q0:q0 + n],
                        start=False, stop=True), reads=[attkp, attqp], writes=[st])

        def probs_pv(i):
            t, j = steps[i]
            touched = state["touched"]
            if j == 0:
                touched.clear()
            b0 = max(0, j - 4 * t)
            n = TW - 128 * b0
            pts = []
            for c in range(nsub):
                st = st_of(c, i)
                pt = ppool.next()
                pts.append(pt)
                if kind == "A":
                    S.op("act", lambda e, pt=pt, st=st, n=n: e.activation(
                        out=pt[:, 0:n], in_=st[:, 0:n], func=AF.Exp, bias=C("b31", h), scale=sc),
                        reads=[st, cols], writes=[pt])
                else:
                    S.op("act", lambda e, pt=pt, st=st, n=n: e.activation(
                        out=pt[:, 0:n], in_=st[:, 0:n], func=AF.Exp, scale=sc), reads=[st], writes=[pt])
                for blk in range(b0, 4):
                    dd = 4 * t + blk - j
                    cs = slice((blk - b0) * 128, (blk - b0 + 1) * 128)
                    if kind == "A" and dd in (0, 1):
                        S.op("dve", lambda e, pt=pt, cs=cs, dd=dd: e.tensor_tensor(
                            out=pt[:, cs], in0=pt[:, cs], in1=Rt[:, dd, h, :], op=ALU.mult),
                            reads=[pt, Rt], writes=[pt])
                    elif kind == "B" and dd == 0:
                        S.op("dve", lambda e, pt=pt, cs=cs: e.tensor_tensor(
                            out=pt[:, cs], in0=pt[:, cs], in1=tri[:, :], op=ALU.mult),
                            reads=[pt, tri], writes=[pt])
            for c in range(nsub):
                pt = pts[c]
                for blk in range(b0, 4):
                    idx = c * 4 + blk
                    bank, off = acc_of(t, idx)
                    first = bank.name not in touched
                    touched.add(bank.name)
                    cs = slice((blk - b0) * 128, (blk - b0 + 1) * 128)
                    S.op("pe", lambda e, pt=pt, cs=cs, bank=bank, off=off, first=first, j=j, t=t, blk=blk: e.matmul(
                        bank[:, off:off + 129], lhsT=pt[:, cs], rhs=attv[:, j, :], start=first,
                        stop=(j == 4 * t + blk), skip_group_check=True), reads=[pt, attv], writes=[bank])

        def finalize(t):
            ost = otst.next()
            sms = [smr.next() for _ in range(4)]
            osbs = [o_sb.next() for _ in range(4)]
            acc0 = [acc_of(t, blk) for blk in range(4)]
            for blk in range(4):
                sm, (a0b, a0o) = sms[blk], acc0[blk]
                S.op("dve", lambda e, sm=sm, a0b=a0b, a0o=a0o: e.reciprocal(out=sm[:, 0:1], in_=a0b[:, a0o + 128:a0o + 129]),
                     reads=[a0b], writes=[sm])
            for blk in range(4):
                sm, osb, (a0b, a0o) = sms[blk], osbs[blk], acc0[blk]
                S.op("dve", lambda e, sm=sm, a0b=a0b, a0o=a0o, osb=osb: e.tensor_scalar(
                    out=osb[:, :], in0=a0b[:, a0o:a0o + 128], scalar1=sm[:, 0:1], scalar2=None, op0=ALU.mult),
                    reads=[a0b, sm], writes=[osb])
            srcs = osbs
            if kind == "A":
                onbs = [on_sb.next() for _ in range(4)]
                acc1 = [acc_of(t, 4 + blk) for blk in range(4)]
                for blk in range(4):
                    sm, (a1b, a1o) = sms[blk], acc1[blk]
                    S.op("dve", lambda e, sm=sm, a1b=a1b, a1o=a1o: e.reciprocal(out=sm[:, 1:2], in_=a1b[:, a1o + 128:a1o + 129]),
                         reads=[a1b], writes=[sm])
                for blk in range(4):
                    sm = sms[blk]
                    S.op("dve", lambda e, sm=sm: e.tensor_tensor(out=sm[:, 1:2], in0=sm[:, 1:2], in1=lamt[:, 2 * l:2 * l + 1],
                                                                 op=ALU.mult), reads=[sm, lamt], writes=[sm])
                for blk in range(4):
                    sm, osb, (a1b, a1o) = sms[blk], osbs[blk], acc1[blk]
                    S.op("dve", lambda e, sm=sm, a1b=a1b, a1o=a1o, osb=osb: e.scalar_tensor_tensor(
                        out=osb[:, :], in0=a1b[:, a1o:a1o + 128], scalar=sm[:, 1:2], in1=osb[:, :],
                        op0=ALU.mult, op1=ALU.add), reads=[a1b, sm, osb], writes=[osb])
                for blk in range(4):
                    sm, osb = sms[blk], osbs[blk]
                    S.op("act", lambda e, sm=sm, osb=osb: e.activation(out=junk[:, :], in_=osb[:, :], func=AF.Square,
                                                                       accum_out=sm[:, 2:3]), reads=[osb], writes=[junk, sm])
                for blk in range(4):
                    sm = sms[blk]
                    S.op("act", lambda e, sm=sm: e.activation(out=sm[:, 3:4], in_=sm[:, 2:3], func=AF.Ln, bias=epsc[:, :],
                                                              scale=1.0 / 128), reads=[sm, epsc], writes=[sm])
                for blk in range(4):
                    sm = sms[blk]
                    S.op("act", lambda e, sm=sm: e.activation(out=sm[:, 4:5], in_=sm[:, 3:4], func=AF.Exp, scale=-0.5),
                         reads=[sm], writes=[sm])
                for blk in range(4):
                    sm, osb, onb = sms[blk], osbs[blk], onbs[blk]
                    S.op("dve", lambda e, sm=sm, osb=osb, onb=onb: e.scalar_tensor_tensor(
                        out=onb[:, :], in0=osb[:, :], scalar=sm[:, 4:5], in1=subg[:, l, :], op0=ALU.mult, op1=ALU.mult),
                        reads=[osb, sm, subg], writes=[onb])
                srcs = onbs
            return ost, srcs

        def finalize2(t, ost, srcs):
            for blk in range(4):
                src = srcs[blk]
                S.op("pe", lambda e, src=src, blk=blk: e.transpose(TP[:, blk * 128:(blk + 1) * 128], src[:, :], ident[:, :]),
                     reads=[src, ident], writes=[TP])
            S.op("act", lambda e, ost=ost: e.activation(out=ost[:, :], in_=TP[:, :], func=AF.Identity), reads=[TP], writes=[ost])
            S.op("sp", lambda e, ost=ost, t=t: e.dma_start(out=OT[h, :, t * TW:(t + 1) * TW], in_=ost[:, :]),
                 reads=[ost], writes=[OT_r[h][t]], dma=ost)

        load_q(0)
        for k in range(min(LA, len(steps))):
            scores(k)
        pending = []
        for i in range(len(steps)):
            if i + LA < len(steps):
                scores(i + LA)
            probs_pv(i)
            t, j = steps[i]
            if j == 4 * t + 3:
                ost, srcs = finalize(t)
                pending.append((i + DEFER, t, ost, srcs))
            while pending and pending[0][0] <= i:
                _, tt, ost, srcs = pending.pop(0)
                finalize2(tt, ost, srcs)
        for _, tt, ost, srcs in pending:
            finalize2(tt, ost, srcs)

    ot_pref = set()
    def dense_tile(l, t, H, w_o_ap, hsrc_ap, hsrc_b, hdst_ap, hdst_b):
        def load_ot(tt, half):
            S.op("sp", lambda e: e.dma_start(
                out=ot_sb[:, :, :], in_=OT[8 * half:8 * half + 8, :, tt * TW:(tt + 1) * TW].rearrange("h p s -> p h s")),
                reads=[OT_r[8 * half + i][tt] for i in range(8)], writes=[ot_sb], dma=ot_sb)

        hdst_cb = hdst_b[t] if hdst_b is not None else [Buf("odst_%d_%d_%d" % (l, t, k)) for k in range(8)]
        if (l, t) not in ot_pref:
            load_ot(t, 0)
        load_h(hsrc_ap, hsrc_b, t)
        for half in range(H // 8):
            if half > 0:
                load_ot(t, half)
            for mg in range(2):
                slot, wv = load_w(("wo", l, half, mg), w_o_ap[half * 1024:(half + 1) * 1024, mg * 512:(mg + 1) * 512], 8, 512)
                for mj in range(4):
                    m = mg * 4 + mj
                    ps = mmr.next()
                    for hh in range(8):
                        S.op("pe", lambda e, hh=hh, mj=mj, ps=ps, wv=wv: e.matmul(
                            ps[:, :], lhsT=wv[:, hh, mj * 128:(mj + 1) * 128], rhs=ot_sb[:, hh, :],
                            start=(hh == 0), stop=(hh == 7)), reads=[slot, ot_sb], writes=[ps])
                    S.op("dve", lambda e, m=m, ps=ps: e.tensor_tensor(out=h_sb[:, m, :], in0=ps[:, :], in1=h_sb[:, m, :],
                                                                      op=ALU.add), reads=[ps, h_c[m]], writes=[h_c[m]])
        rmsnorm("ffn_norm%d" % l)
        w_in = ffn_w_in[l]
        for ig in range(0, NFC, 2):
            slot, wv = load_w2(("win", l, ig), w_in[:, ig * 128:(ig + 2) * 128], w_in[:, DFF + ig * 128:DFF + (ig + 2) * 128], 8, 256, 256)
            for ii in range(2):
                i = ig + ii
                pss, uxs, cbs, ccs = [], [], [], []
                for part in range(2):
                    cc = part * NFC + i
                    ps = mmr.next()
                    proj_w(slot, wv, part * 256 + ii * 128, 8, hn, lambda kc: hn[:, kc, :], ps)
                    pss.append(ps)
                    ccs.append(cc)
                for part in range(2):
                    ux = uext.next()
                    uxs.append(ux)
                    S.op("act", lambda e, ux=ux, ps=pss[part]: e.activation(out=ux[:, 2:TW + 2], in_=ps[:, :], func=AF.Identity),
                         reads=[pss[part]], writes=[ux])
                for part in range(2):
                    ux, cc = uxs[part], ccs[part]
                    S.op("dve", lambda e, ux=ux, cc=cc: e.tensor_copy(out=ux[:, 0:2], in_=utail_t[:, cc, :]),
                         reads=[utail[cc]], writes=[ux])
                for part in range(2):
                    ux, cc = uxs[part], ccs[part]
                    S.op("dve", lambda e, ux=ux, cc=cc: e.tensor_copy(out=utail_t[:, cc, :], in_=ux[:, TW:TW + 2]),
                         reads=[ux], writes=[utail[cc]])
                for part in range(2):
                    cc = ccs[part]
                    cb = cpool.next()
                    cbs.append(cb)
                    S.op("act", lambda e, cb=cb, cc=cc, ps=pss[part]: e.activation(
                        out=cb[:, :], in_=ps[:, :], func=AF.Identity, scale=C("conv_w%d_2" % l, cc), bias=C("conv_b%d" % l, cc)),
                        reads=[pss[part], cols], writes=[cb])
                for jj in (1, 0):
                    for part in range(2):
                        ux, cc, cb = uxs[part], ccs[part], cbs[part]
                        S.op("dve", lambda e, ux=ux, cb=cb, cc=cc, jj=jj: e.scalar_tensor_tensor(
                            out=cb[:, :], in0=ux[:, jj:TW + jj], scalar=C("conv_w%d_%d" % (l, jj), cc), in1=cb[:, :],
                            op0=ALU.mult, op1=ALU.add), reads=[ux, cols, cb], writes=[cb])
                sg = sgp.next()
                S.op("act", lambda e, sg=sg, cg=cbs[1]: e.activation(out=sg[:, :], in_=cg[:, :], func=AF.Silu),
                     reads=[cbs[1]], writes=[sg])
                S.op("dve", lambda e, sg=sg, ca=cbs[0], i=i: e.tensor_tensor(out=z_sb[:, i, :], in0=sg[:, :], in1=ca[:, :],
                                                                             op=ALU.mult), reads=[sg, cbs[0]], writes=[z_sb])
        w_out = ffn_w_out[l]
        for m in range(8):
            slot, wv = load_w(("wout", l, m), w_out[:, m * 128:(m + 1) * 128], NFC, 128)
            ps = mmr.next()
            for i in range(NFC):
                S.op("pe", lambda e, i=i, ps=ps, wv=wv: e.matmul(ps[:, :], lhsT=wv[:, i, :], rhs=z_sb[:, i, :],
                                                                 start=(i == 0), stop=(i == NFC - 1)),
                     reads=[slot, z_sb], writes=[ps])
            S.op("dve", lambda e, m=m, ps=ps: e.tensor_tensor(out=h_sb[:, m, :], in0=ps[:, :], in1=h_sb[:, m, :], op=ALU.add),
                 reads=[ps, h_c[m]], writes=[h_c[m]])
        rmsnorm("ple_norm%d" % l)
        S.op("pool", lambda e: e.dma_start(out=pt_sb[:, :, :],
                                           in_=pT[l, :, t * TW:(t + 1) * TW].rearrange("(k p) s -> p k s", p=128)),
             writes=[pt_sb], dma=pt_sb)
        pslot, pwv = load_w(("pproj", l), ple_w_proj[l], 2, 1024)
        for mg in range(2):
            slot, wv = load_w(("pgate", l, mg), ple_w_gate[l][:, mg * 512:(mg + 1) * 512], 8, 512)
            for mj in range(4):
                m = mg * 4 + mj
                psg = mmr.next()
                for kc in range(8):
                    S.op("pe", lambda e, kc=kc, mj=mj, psg=psg, wv=wv: e.matmul(
                        psg[:, :], lhsT=wv[:, kc, mj * 128:(mj + 1) * 128], rhs=hn[:, kc, :],
                        start=(kc == 0), stop=(kc == 7)), reads=[slot, hn], writes=[psg])
                sg = sgp.next()
                S.op("act", lambda e, sg=sg, psg=psg: e.activation(out=sg[:, :], in_=psg[:, :], func=AF.Sigmoid),
                     reads=[psg], writes=[sg])
                psp = mmr.next()
                for kc in range(2):
                    S.op("pe", lambda e, kc=kc, m=m, psp=psp, pwv=pwv: e.matmul(
                        psp[:, :], lhsT=pwv[:, kc, m * 128:(m + 1) * 128], rhs=pt_sb[:, kc, :],
                        start=(kc == 0), stop=(kc == 1)), reads=[pslot, pt_sb], writes=[psp])
                cb = cpool.next()
                S.op("dve", lambda e, cb=cb, psp=psp, sg=sg: e.tensor_tensor(out=cb[:, :], in0=psp[:, :], in1=sg[:, :],
                                                                             op=ALU.mult), reads=[psp, sg], writes=[cb])
                S.op("dve", lambda e, cb=cb, m=m: e.tensor_tensor(out=h_sb[:, m, :], in0=cb[:, :], in1=h_sb[:, m, :],
                                                                  op=ALU.add), reads=[cb, h_c[m]], writes=[h_c[m]])
                if m == 6 and t + 1 < NT:
                    load_ot(t + 1, 0)
                    ot_pref.add((l, t + 1))
                S.op("sp", lambda e, m=m: e.dma_start(out=hdst_ap[m * 128:(m + 1) * 128, t * TW:(t + 1) * TW], in_=h_sb[:, m, :]),
                     reads=[h_c[m]], writes=[hdst_cb[m]], dma=h_c[m])

    def proj_w(slot, wview, j0, kcn, rhs_b, rhs_of_kc, ps, parts=128, mcols=128):
        for kc in range(kcn):
            S.op("pe", lambda e, kc=kc: e.matmul(ps[0:mcols, :], lhsT=wview[:, kc, j0:j0 + mcols], rhs=rhs_of_kc(kc),
                                                 start=(kc == 0), stop=(kc == kcn - 1)),
                 reads=[slot, rhs_b], writes=[ps])

    _dense_slot = {}

    def a_phase1(l, t, hsrc_ap, hsrc_b):
        load_h(hsrc_ap, hsrc_b, t)
        rmsnorm("attn_norm%d" % l)
        wq = a_w_qkv[l]
        for grp in range(4):
            slot, wv = load_w(("qkv", l, grp), wq[:, grp * 512:(grp + 1) * 512], 8, 512)
            for jp in range(2):
                entries, outs = [], []
                for j in (2 * jp, 2 * jp + 1):
                    oc = grp * 4 + j
                    ps = mmr.next()
                    proj_w(slot, wv, j * 128, 8, hn, lambda kc: hn[:, kc, :], ps)
                    sb = stg.next()
                    gname = ("a_q_norm%d" if oc < 8 else "a_k_norm%d") % l
                    entries.append((ps, C(gname), sb, sb[:, :]))
                    outs.append((sb, oc))
                group_norm_multi(entries, 64)
                for sb, oc in outs:
                    dst, dreg = (QA, QA_r) if oc < 8 else (KA, KA_r)
                    hh = oc % 8
                    S.op("sp", lambda e, sb=sb, dst=dst, hh=hh: e.dma_start(out=dst[hh, :, t * TW:(t + 1) * TW], in_=sb[:, :]),
                         reads=[sb], writes=[dreg[hh][t]], dma=sb)
        for half in range(2):
            slot, wv = load_w(("qkvv", l, half), wq[:, 2048 + half * 512:2048 + (half + 1) * 512], 8, 512)
            vs = vst.next()
            for blk in range(4):
                ps = mmr.next()
                for kc in range(8):
                    S.op("pe", lambda e, kc=kc, blk=blk, ps=ps, wv=wv: e.matmul(
                        ps[:, :], lhsT=hn[:, kc, blk * 128:(blk + 1) * 128], rhs=wv[:, kc, :],
                        start=(kc == 0), stop=(kc == 7)), reads=[slot, hn], writes=[ps])
                S.op("dve", lambda e, blk=blk, ps=ps, vs=vs: e.tensor_copy(
                    out=vs[:, :, blk, 0:128], in_=ps[:, :].rearrange("p (h d) -> p h d", h=4)), reads=[ps], writes=[vs])
            S.op("sp", lambda e, half=half, vs=vs: e.dma_start(
                out=VV[4 * half:4 * half + 4, :, 4 * t:4 * t + 4, :].rearrange("h p b e -> p h b e"), in_=vs[:, :, :, :]),
                reads=[vs], writes=[VV_r[4 * half + i][t] for i in range(4)], dma=vs)

    def rotary_tables():
        TWO_PI = 2.0 * math.pi
        C1 = 6.28125
        C2 = TWO_PI - C1
        pi_t = itmp
        ang = cpool.bufs[0]
        kf = cpool.bufs[1]
        ki = itmp2
        mk = cpool.bufs[2]
        for t in range(NT):
            S.op("sp", lambda e, t=t: e.dma_start(out=pi_t[:, :], in_=posb[:, t * TW:(t + 1) * TW]), writes=[pi_t], dma=pi_t)
            S.op("dve", lambda e: e.tensor_copy(out=ang[:, :], in_=pi_t[:, :]), reads=[pi_t], writes=[ang])
            S.op("dve", lambda e: e.tensor_scalar(out=ang[:, :], in0=ang[:, :], scalar1=C("invfreq"), scalar2=None,
                                                  op0=ALU.mult), reads=[ang, cols], writes=[ang])
            S.op("dve", lambda e: e.tensor_scalar(out=ki[:, :], in0=ang[:, :], scalar1=1.0 / TWO_PI, scalar2=None,
                                                  op0=ALU.mult), reads=[ang], writes=[ki])
            S.op("dve", lambda e: e.tensor_copy(out=kf[:, :], in_=ki[:, :]), reads=[ki], writes=[kf])
            S.op("dve", lambda e: e.scalar_tensor_tensor(out=ang[:, :], in0=kf[:, :], scalar=-C1, in1=ang[:, :],
                                                         op0=ALU.mult, op1=ALU.add), reads=[kf, ang], writes=[ang])
            S.op("dve", lambda e: e.scalar_tensor_tensor(out=ang[:, :], in0=kf[:, :], scalar=-C2, in1=ang[:, :],
                                                         op0=ALU.mult, op1=ALU.add), reads=[kf, ang], writes=[ang])
            S.op("dve", lambda e: e.tensor_scalar(out=mk[:, :], in0=ang[:, :], scalar1=math.pi, scalar2=None,
                                                  op0=ALU.is_gt), reads=[ang], writes=[mk])
            S.op("dve", lambda e: e.scalar_tensor_tensor(out=ang[:, :], in0=mk[:, :], scalar=-TWO_PI, in1=ang[:, :],
                                                         op0=ALU.mult, op1=ALU.add), reads=[mk, ang], writes=[ang])
            S.op("dve", lambda e: e.tensor_scalar(out=mk[:, :], in0=ang[:, :], scalar1=-math.pi, scalar2=None,
                                                  op0=ALU.is_lt), reads=[ang], writes=[mk])
            S.op("dve", lambda e: e.scalar_tensor_tensor(out=ang[:, :], in0=mk[:, :], scalar=TWO_PI, in1=ang[:, :],
                                                         op0=ALU.mult, op1=ALU.add), reads=[mk, ang], writes=[ang])
            S.op("dve", lambda e: e.tensor_scalar(out=ang[:, :], in0=ang[:, :], scalar1=-3.1415925, scalar2=3.1415925,
                                                  op0=ALU.max, op1=ALU.min), reads=[ang], writes=[ang])
            S.op("act", lambda e: e.activation(out=cs_sb[:, 1, :], in_=ang[:, :], func=AF.Sin), reads=[ang], writes=[cs_sb])
            S.op("dve", lambda e: e.scalar_tensor_tensor(out=kf[:, :], in0=ang[:, :], scalar=-1.0, in1=ang[:, :],
                                                         op0=ALU.mult, op1=ALU.max), reads=[ang], writes=[kf])
            S.op("dve", lambda e: e.tensor_scalar(out=kf[:, :], in0=kf[:, :], scalar1=-1.0, scalar2=math.pi / 2,
                                                  op0=ALU.mult, op1=ALU.add), reads=[kf], writes=[kf])
            S.op("act", lambda e: e.activation(out=cs_sb[:, 0, :], in_=kf[:, :], func=AF.Sin), reads=[kf], writes=[cs_sb])
            S.op("sp", lambda e, t=t: [e.dma_start(out=COS[:, t * TW:(t + 1) * TW], in_=cs_sb[:, 0, :]),
                                       e.dma_start(out=SIN[:, t * TW:(t + 1) * TW], in_=cs_sb[:, 1, :])],
                 reads=[cs_sb], writes=[CS_r[t]], dma=cs_sb, ninc=2)

    def load_cs(t):
        S.op("sp", lambda e: [e.dma_start(out=cs_sb[:, 0, :], in_=COS[:, t * TW:(t + 1) * TW]),
                              e.dma_start(out=cs_sb[:, 1, :], in_=SIN[:, t * TW:(t + 1) * TW])],
             reads=[CS_r[t]], writes=[cs_sb], dma=cs_sb, ninc=2)

    def apply_rotary(xb, parts, out_b, out_ap):
        rp = mmr.next()
        S.op("pe", lambda e: e.matmul(rp[0:parts, :], lhsT=prot[0:parts, 0:parts], rhs=xb[0:parts, :], start=True, stop=True),
             reads=[prot, xb], writes=[rp])
        cb = cpool.next()
        S.op("dve", lambda e: e.tensor_tensor(out=cb[0:parts, :], in0=rp[0:parts, :], in1=cs_sb[0:parts, 1, :], op=ALU.mult),
             reads=[rp, cs_sb], writes=[cb])
        cb2 = cpool.next()
        S.op("dve", lambda e: e.tensor_tensor(out=cb2[0:parts, :], in0=xb[0:parts, :], in1=cs_sb[0:parts, 0, :], op=ALU.mult),
             reads=[xb, cs_sb], writes=[cb2])
        S.op("dve", lambda e: e.tensor_tensor(out=out_ap, in0=cb[0:parts, :], in1=cb2[0:parts, :], op=ALU.add),
             reads=[cb, cb2], writes=[out_b])

    def shared_kv_tile(t, hsrc_ap, hsrc_b):
        load_h(hsrc_ap, hsrc_b, t)
        load_cs(t)
        rmsnorm("kv_norm")
        slot, wv = load_w(("dkv",), w_dkv, 8, 320)
        pss = []
        for c in range(2):
            ps = mmr.next()
            proj_w(slot, wv, c * 128, 8, hn, lambda kc: hn[:, kc, :], ps)
            pss.append(ps)
        st = STAT
        for c in range(2):
            sq = sqr.next()
            S.op("act", lambda e, sq=sq, c=c: e.activation(out=sq[:, :], in_=pss[c][:, :], func=AF.Square),
                 reads=[pss[c]], writes=[sq])
            S.op("pe", lambda e, sq=sq, c=c: e.matmul(st[:, :], lhsT=ones[:, :], rhs=sq[:, :], start=(c == 0), stop=(c == 1)),
                 reads=[ones, sq], writes=[st])
        rs = rstd_from(st, st[:, :], 1.0 / 256)
        for c in range(2):
            S.op("dve", lambda e, c=c: e.scalar_tensor_tensor(out=ckvn[:, c, :], in0=pss[c][:, :], scalar=C("ckv_norm", c),
                                                              in1=rs[:, :], op0=ALU.mult, op1=ALU.mult),
                 reads=[pss[c], cols, rs], writes=[ckvn])
        ps = mmr.next()
        proj_w(slot, wv, 256, 8, hn, lambda kc: hn[:, kc, :], ps, mcols=64)
        sb = stg.next()
        group_norm_to(ps, cols[0:64, cm.m["k_pe_norm"]:cm.m["k_pe_norm"] + 1], sb, sb[0:64, :], 64, parts=64)
        sb2 = stg.next()
        apply_rotary(sb, 64, sb2, sb2[0:64, :])
        S.op("sp", lambda e: e.dma_start(out=KPE[:, t * TW:(t + 1) * TW], in_=sb2[0:64, :]), reads=[sb2],
             writes=[KPE_r[t]], dma=sb2)
        for g4 in range(4):
            slot, wv = load_w(("ukvK", g4), w_ukvK[:, g4 * 512:(g4 + 1) * 512], 2, 512)
            for jp in range(2):
                entries, outs = [], []
                for j in (2 * jp, 2 * jp + 1):
                    hh = g4 * 4 + j
                    ps = mmr.next()
                    proj_w(slot, wv, j * 128, 2, ckvn, lambda kc: ckvn[:, kc, :], ps)
                    sb = stg.next()
                    entries.append((ps, C("k_nope_norm"), sb, sb[:, :]))
                    outs.append((sb, hh))
                group_norm_multi(entries, 128)
                for sb, hh in outs:
                    S.op("sp", lambda e, sb=sb, hh=hh: e.dma_start(out=KA[hh, :, t * TW:(t + 1) * TW], in_=sb[:, :]),
                         reads=[sb], writes=[KA_r[hh][t]], dma=sb)
        for g4 in range(4):
            slot, wv = load_w(("ukvV", g4), w_ukvV[:, g4 * 512:(g4 + 1) * 512], 2, 512)
            vs = vst.next()
            for blk in range(4):
                ps = mmr.next()
                for kc in range(2):
                    S.op("pe", lambda e, kc=kc, blk=blk, ps=ps, wv=wv: e.matmul(
                        ps[:, :], lhsT=ckvn[:, kc, blk * 128:(blk + 1) * 128], rhs=wv[:, kc, :],
                        start=(kc == 0), stop=(kc == 1)), reads=[slot, ckvn], writes=[ps])
                S.op("dve", lambda e, blk=blk, ps=ps, vs=vs: e.tensor_copy(
                    out=vs[:, :, blk, 0:128], in_=ps[:, :].rearrange("p (h d) -> p h d", h=4)), reads=[ps], writes=[vs])
            S.op("sp", lambda e, g4=g4, vs=vs: e.dma_start(
                out=VV[4 * g4:4 * g4 + 4, :, 4 * t:4 * t + 4, :].rearrange("h p b e -> p h b e"), in_=vs[:, :, :, :]),
                reads=[vs], writes=[VV_r[4 * g4 + i][t] for i in range(4)], dma=vs)

    def b_phase1(j_, l, t, hsrc_ap, hsrc_b):
        load_h(hsrc_ap, hsrc_b, t)
        load_cs(t)
        rmsnorm("attn_norm%d" % l)
        slot, wv = load_w(("dq", j_), b_w_dq[j_], 8, 512)
        pss = []
        for c in range(4):
            ps = mmr.next()
            proj_w(slot, wv, c * 128, 8, hn, lambda kc: hn[:, kc, :], ps)
            pss.append(ps)
        st = STAT
        for c in range(4):
            sq = sqr.next()
            S.op("act", lambda e, sq=sq, c=c: e.activation(out=sq[:, :], in_=pss[c][:, :], func=AF.Square),
                 reads=[pss[c]], writes=[sq])
            S.op("pe", lambda e, sq=sq, c=c: e.matmul(st[:, :], lhsT=ones[:, :], rhs=sq[:, :], start=(c == 0), stop=(c == 3)),
                 reads=[ones, sq], writes=[st])
        rs = rstd_from(st, st[:, :], 1.0 / 512)
        for c in range(4):
            S.op("dve", lambda e, c=c: e.scalar_tensor_tensor(out=cqn[:, c, :], in0=pss[c][:, :],
                                                              scalar=C("b_cq_norm%d" % j_, c), in1=rs[:, :],
                                                              op0=ALU.mult, op1=ALU.mult),
                 reads=[pss[c], cols, rs], writes=[cqn])
        for g4 in range(4):
            slot, wv = load_w(("uqN", j_, g4), b_w_uqN[j_][:, g4 * 512:(g4 + 1) * 512], 4, 512)
            for jp in range(2):
                entries, outs = [], []
                for j in (2 * jp, 2 * jp + 1):
                    hh = g4 * 4 + j
                    ps = mmr.next()
                    proj_w(slot, wv, j * 128, 4, cqn, lambda kc: cqn[:, kc, :], ps)
                    sb = stg.next()
                    entries.append((ps, C("b_q_nope_norm%d" % j_), sb, sb[:, :]))
                    outs.append((sb, hh))
                group_norm_multi(entries, 128)
                for sb, hh in outs:
                    S.op("sp", lambda e, sb=sb, hh=hh: e.dma_start(out=QA[hh, :, t * TW:(t + 1) * TW], in_=sb[:, :]),
                         reads=[sb], writes=[QA_r[hh][t]], dma=sb)
        for g2 in range(2):
            slot, wv = load_w(("uqP", j_, g2), b_w_uqP[j_][:, g2 * 512:(g2 + 1) * 512], 4, 512)
            for j in range(4):
                ch = g2 * 4 + j
                ps = mmr.next()
                proj_w(slot, wv, j * 128, 4, cqn, lambda kc: cqn[:, kc, :], ps)
                sb = stg.next()
                group_norm_to(ps, C("b_q_pe_norm%d" % j_), sb, sb[:, :], 64)
                sb2 = stg.next()
                apply_rotary(sb, 128, sb2, sb2[:, :])
                S.op("sp", lambda e, sb2=sb2, ch=ch: e.dma_start(out=QP[ch, :, t * TW:(t + 1) * TW], in_=sb2[:, :]),
                     reads=[sb2], writes=[QP_r[ch][t]], dma=sb2)

    def hsrc_of(l):
        return (xT, None) if l == 0 else (hT, hT_r)

    def conv_during_attention(l, nheads, kind):
        ckeys = []
        if wt_plan is not None:
            ckeys = [k for k, v in wt_plan.items() if v[0] in (2 * l + 1, 2 * l + 2)]
        nck = (len(ckeys) + nheads - 1) // nheads
        for h in range(nheads):
            attention_head(kind, l, h)
            if ckeys:
                convert_keys(ckeys[h * nck:(h + 1) * nck], gate=[attk])

    for l in range(L):
        src_ap, src_r = hsrc_of(l)
        last = (l == L - 1)
        dst_ap, dst_r = (outT, None) if last else (hT, hT_r)
        cur_phase[0] = 2 * l
        if l < 2:
            for t in range(NT):
                a_phase1(l, t, src_ap, src_r[t] if src_r else None)
            cur_phase[0] = 2 * l + 1
            conv_during_attention(l, 8, "A")
            for i in range(44):
                S.op("dve", lambda e, i=i: e.memset(utail_t[:, i, :], 0.0), writes=[utail[i]])
            for t in range(NT):
                dense_tile(l, t, 8, a_w_o[l], src_ap, src_r[t] if src_r else None, dst_ap, dst_r)
        else:
            j_ = l - 2
            if l == 2:
                rotary_tables()
                for t in range(NT):
                    shared_kv_tile(t, src_ap, src_r[t])
                S.op("dve", lambda e: e.memset(attkp[64:128, :], 0.0), writes=[attkp])
                S.op("sp", lambda e: e.dma_start(out=attkp[0:64, :], in_=KPE), reads=KPE_r, writes=[attkp], dma=attkp)
                for r_ in qpring.bufs:
                    S.op("dve", lambda e, r_=r_: e.memset(r_[64:128, :], 0.0), writes=[r_])
            for t in range(NT):
                b_phase1(j_, l, t, src_ap, src_r[t])
            cur_phase[0] = 2 * l + 1
            conv_during_attention(l, 16, "B")
            for i in range(44):
                S.op("dve", lambda e, i=i: e.memset(utail_t[:, i, :], 0.0), writes=[utail[i]])
            for t in range(NT):
                dense_tile(l, t, 16, b_w_o[j_], src_ap, src_r[t], dst_ap, dst_r)

    stats = S.emit()
    es.close()
    return nc, stats, cm, rm, wt_first


_PROG = {}


def _get_prog(SL, L):
    key = (SL, L)
    if key not in _PROG:
        plan = build_program(SL, L)[4]
        _PROG[key] = build_program(SL, L, wt_plan=plan)[:4]
    return _PROG[key]


def prepare_inputs(inp, SL, L, cm, rm):
    f = lambda a: np.ascontiguousarray(np.asarray(a, dtype=np.float32))
    B = inp["x"].shape[0]
    cols = np.zeros((128, cm.n), np.float32)
    rows = np.zeros((128, rm.n), np.float32)

    def put_vec(name, v, off=0):
        v = np.asarray(v, np.float32)
        k = v.shape[0] // 128
        cols[:, cm.m[name] + off:cm.m[name] + off + k] = v.reshape(k, 128).T

    for l in range(4):
        put_vec("attn_norm%d" % l, inp["attn_norm"][l])
        put_vec("ffn_norm%d" % l, inp["ffn_norm"][l])
        put_vec("ple_norm%d" % l, inp["ple_norm"][l])
        for j in range(3):
            put_vec("conv_w%d_%d" % (l, j), np.asarray(inp["ffn_conv_w"])[l, j])
        put_vec("conv_b%d" % l, np.asarray(inp["ffn_conv_b"])[l])
    put_vec("kv_norm", inp["kv_norm"])
    for l in range(2):
        put_vec("a_q_norm%d" % l, np.tile(np.asarray(inp["a_q_norm"])[l], 2))
        put_vec("a_k_norm%d" % l, np.tile(np.asarray(inp["a_k_norm"])[l], 2))
        put_vec("b_cq_norm%d" % l, np.asarray(inp["b_cq_norm"])[l])
        put_vec("b_q_nope_norm%d" % l, np.asarray(inp["b_q_nope_norm"])[l])
        put_vec("b_q_pe_norm%d" % l, np.tile(np.asarray(inp["b_q_pe_norm"])[l], 2))
    put_vec("ckv_norm", inp["ckv_norm"])
    put_vec("k_nope_norm", inp["k_nope_norm"])
    put_vec("k_pe_norm", np.tile(np.asarray(inp["k_pe_norm"]), 2))
    tab = np.asarray(inp["rel_bias_table"], np.float32)
    cols[:, cm.m["b31"]:cm.m["b31"] + 8] = tab[31][None, :]
    cols[:, cm.m["table"]:cm.m["table"] + 256] = tab.reshape(1, 256)
    half = 32
    invf = np.exp(-math.log(10000.0) * np.arange(half, dtype=np.float32) * np.float32(2.0 / 64)).astype(np.float32)
    cols[:, cm.m["invfreq"]] = np.tile(invf, 4)
    for l in range(2):
        for nm, key in (("q1", "a_lam_q1"), ("k1", "a_lam_k1"), ("q2", "a_lam_q2"), ("k2", "a_lam_k2")):
            r0 = rm.m["lam_%s%d" % (nm, l)]
            rows[:, r0:r0 + 64] = np.asarray(inp[key], np.float32)[l][None, :]
        r0 = rm.m["sub_gain%d" % l]
        rows[:, r0:r0 + 128] = np.asarray(inp["a_sub_norm"], np.float32)[l][None, :]
    prot = np.zeros((128, 128), np.float32)
    for g in range(2):
        for m in range(64):
            if m < 32:
                prot[g * 64 + m + 32, g * 64 + m] = -1.0
            else:
                prot[g * 64 + m - 32, g * 64 + m] = 1.0
    w_ukv = np.asarray(inp["w_ukv"], np.float32).reshape(256, 16, 256)
    w_uq = np.asarray(inp["b_w_uq"], np.float32).reshape(2, 512, 16, 192)
    shared = dict(
        cols=cols, rows=rows, prot=prot,
        a_w_qkv=f(inp["a_w_qkv"]), a_w_o=f(inp["a_w_o"]), w_dkv=f(inp["w_dkv"]),
        w_ukvK=np.ascontiguousarray(w_ukv[:, :, 0:128].reshape(256, 2048)),
        w_ukvV=np.ascontiguousarray(w_ukv[:, :, 128:256].reshape(256, 2048)),
        b_w_dq=f(inp["b_w_dq"]),
        b_w_uqN=np.ascontiguousarray(w_uq[:, :, :, 0:128].reshape(2, 512, 2048)),
        b_w_uqP=np.ascontiguousarray(w_uq[:, :, :, 128:192].reshape(2, 512, 1024)),
        b_w_o=f(inp["b_w_o"]), ffn_w_in=f(inp["ffn_w_in"]), ffn_w_out=f(inp["ffn_w_out"]),
        ple_w_proj=f(inp["ple_w_proj"]), ple_w_gate=f(inp["ple_w_gate"]),
    )
    x = np.asarray(inp["x"], np.float32)
    p = np.asarray(inp["p"], np.float32)
    pos = np.asarray(inp["positions"]).astype(np.int32)
    maps = []
    for c in range(NCORES):
        b = c % B
        m = dict(shared)
        m["xT"] = np.ascontiguousarray(x[b, :SL].T)
        m["pT"] = np.ascontiguousarray(p[:, b, :SL].transpose(0, 2, 1))
        m["posb"] = np.ascontiguousarray(np.broadcast_to(pos[b, :SL][None, :], (128, SL)))
        m["poscol"] = np.ascontiguousarray(pos[b, :256].reshape(2, 128).T)
        maps.append(m)
    return maps


def run_model(inp, SL, L):
    nc, stats, cm, rm = _get_prog(SL, L)
    maps = prepare_inputs(inp, SL, L, cm, rm)
    res = run_bass_kernel_spmd(nc, maps, core_ids=list(range(NCORES)))
    B = inp["x"].shape[0]
    out = np.stack([np.ascontiguousarray(res.results[b]["outT"].T) for b in range(B)], axis=0)
    return out.astype(np.float32)


def kernel(**inputs):
    return run_model(inputs, 4096, 4)
```
